# Optimizing a Trainium2 kernel written in Bass

```python
import jax, jax.numpy as jnp
from jax import lax
import numpy as np

D_MODEL = 1024
BATCH = 16
SEQ = 256
DEPTH = 2
DEC_BATCH = 2
DEC_SEQ = 1024
PAST_LEN = 512

GRID_W = 64
N_MIXERS = 2
N_REC = (DEPTH + 1) // 2
N_ATT = DEPTH // 2
HEAD_DIM = 128
N_HEADS = D_MODEL // HEAD_DIM
N_KV_HEADS = 2
GROUP = N_HEADS // N_KV_HEADS
QKV_DIM = (N_HEADS + 2 * N_KV_HEADS) * HEAD_DIM
ROPE_THETA = 10000.0
Q_BLOCK = 128
D_RNN = D_MODEL
LRU_BLOCKS = 16
LRU_BLOCK_W = D_RNN // LRU_BLOCKS
CONV_W = 4
LRU_C = 8.0
D_FF = 2816
N_SUB = 3
EPS = 1e-6

kernel_name = "hybrid_rglru_gqa_prefix_diffusion_step"

F32 = jnp.float32


def rmsnorm(x, g):
    xf = x.astype(F32)
    y = xf * lax.rsqrt(jnp.mean(xf * xf, axis=-1, keepdims=True) + EPS)
    return (y * g.astype(F32)).astype(x.dtype)


def modulation(cvec, w, b):
    m = jax.nn.silu(cvec) @ w + b
    return m.reshape(cvec.shape[0], N_SUB, 3, D_MODEL)


def sub_in(x, g, m, sidx):
    shift, scale = m[:, sidx, 0], m[:, sidx, 1]
    return rmsnorm(x, g[sidx]) * (1 + scale[:, None, :]) + shift[:, None, :]


def sub_gate(m, sidx):
    return m[:, sidx, 2][:, None, :]


def swiglu(h, w_gu, w_down):
    gt, up = jnp.split(h @ w_gu, 2, axis=-1)
    return (jax.nn.silu(gt) * up) @ w_down


def centred_dwconv(x, w, b):
    T = x.shape[1]
    left = (CONV_W - 1) // 2
    right = CONV_W - 1 - left
    xp = jnp.pad(x, ((0, 0), (left, right), (0, 0)))
    out = b
    for k in range(CONV_W):
        out = out + xp[:, k:k + T] * w[k]
    return out


def linear_scan(a, u, h0, reverse):
    def step(h, au):
        a_t, u_t = au
        h = a_t * h + u_t
        return h, h
    h_last, hs = lax.scan(step, h0, (a.swapaxes(0, 1), u.swapaxes(0, 1)), reverse=reverse)
    return hs.swapaxes(0, 1), h_last


def rglru_block(xn, w_in, conv_w, conv_b, gate_w, gate_b, lam, w_out, h0):
    B, T, _ = xn.shape
    xb, yb = jnp.split(xn @ w_in, 2, axis=-1)
    xc = centred_dwconv(xb, conv_w, conv_b)
    xblk = xc.reshape(B, T, LRU_BLOCKS, LRU_BLOCK_W)
    gl = jnp.einsum('btnk,dgnkj->dgbtnj', xblk, gate_w).reshape(2, 2, B, T, D_RNN)
    gates = jax.nn.sigmoid(gl.astype(F32) + gate_b.astype(F32)[:, :, None, None, :])
    r, i = gates[:, 0], gates[:, 1]
    log_a = LRU_C * r * jax.nn.log_sigmoid(lam.astype(F32))[:, None, None, :]
    a = jnp.exp(log_a)
    u = jnp.sqrt(-jnp.expm1(2.0 * log_a)) * (i * xc.astype(F32)[None])
    h0f = h0.astype(F32)
    hf, hf_last = linear_scan(a[0], u[0], h0f[:, 0], False)
    hb, hb_last = linear_scan(a[1], u[1], h0f[:, 1], True)
    y = (hf + hb).astype(xn.dtype) * jax.nn.gelu(yb)
    return y @ w_out, jnp.stack([hf_last, hb_last], axis=1).astype(xn.dtype)


def axial_rope_angles(n_tok):
    rows = n_tok // GRID_W
    r_idx = jnp.broadcast_to(jnp.arange(rows)[:, None], (rows, GRID_W)).reshape(n_tok).astype(F32)
    c_idx = jnp.broadcast_to(jnp.arange(GRID_W)[None, :], (rows, GRID_W)).reshape(n_tok).astype(F32)
    n_freq = HEAD_DIM // 4
    inv = ROPE_THETA ** (-jnp.arange(n_freq, dtype=F32) / n_freq)
    ang = jnp.stack([r_idx[:, None] * inv, c_idx[:, None] * inv], axis=1)
    return jnp.cos(ang), jnp.sin(ang)


def apply_axial_rope(x, cos, sin):
    B, T, H, _ = x.shape
    xr = x.astype(F32).reshape(B, T, H, 2, 2, HEAD_DIM // 4)
    x0, x1 = xr[..., 0, :], xr[..., 1, :]
    c = cos[None, :, None]
    s = sin[None, :, None]
    out = jnp.stack([x0 * c - x1 * s, x0 * s + x1 * c], axis=-2)
    return out.reshape(B, T, H, HEAD_DIM).astype(x.dtype)


def qkv_heads(xn, w_qkv, q_g, k_g):
    B, T, _ = xn.shape
    q, k, v = jnp.split(xn @ w_qkv, [N_HEADS * HEAD_DIM, (N_HEADS + N_KV_HEADS) * HEAD_DIM], axis=-1)
    q = rmsnorm(q.reshape(B, T, N_HEADS, HEAD_DIM), q_g)
    k = rmsnorm(k.reshape(B, T, N_KV_HEADS, HEAD_DIM), k_g)
    v = v.reshape(B, T, N_KV_HEADS, HEAD_DIM)
    return q, k, v


def block_attention(q, k, v):
    B, S = q.shape[0], q.shape[1]
    nb = S // Q_BLOCK
    qb = q.reshape(B, nb, Q_BLOCK, N_KV_HEADS, GROUP, HEAD_DIM).transpose(1, 0, 2, 3, 4, 5)
    scale = HEAD_DIM ** -0.5

    def one_block(qblk):
        s = jnp.einsum('bqkgd,btkd->bkgqt', qblk, k).astype(F32) * scale
        p = jax.nn.softmax(s, axis=-1).astype(v.dtype)
        return jnp.einsum('bkgqt,btkd->bqkgd', p, v)

    o = lax.map(one_block, qb)
    return o.transpose(1, 0, 2, 3, 4, 5).reshape(B, S, N_HEADS * HEAD_DIM)


def setup_inputs(seed: int = 0) -> dict:
    key = jax.random.key(seed)
    ks = jax.random.split(key, 26)

    def nrm(k, shape, s):
        return jax.random.normal(k, shape, F32) * s

    u = jax.random.uniform(ks[15], (N_REC, 2, D_RNN), F32, minval=0.9, maxval=0.999)
    a0 = u ** (1.0 / LRU_C)
    lam = jnp.log(a0) - jnp.log1p(-a0)
    return {
        "x_prompt": nrm(ks[0], (BATCH, SEQ, D_MODEL), 1.0),
        "x_sample": nrm(ks[1], (DEC_BATCH, DEC_SEQ, D_MODEL), 1.0),
        "c": nrm(ks[2], (DEC_BATCH, D_MODEL), 1.0),
        "state_lru": nrm(ks[3], (DEC_BATCH, N_REC, 2, D_RNN), 0.5),
        "cache_k": nrm(ks[4], (DEC_BATCH, N_ATT, PAST_LEN, N_KV_HEADS, HEAD_DIM), 1.0),
        "cache_v": nrm(ks[5], (DEC_BATCH, N_ATT, PAST_LEN, N_KV_HEADS, HEAD_DIM), 1.0),
        "c_ctx": nrm(ks[6], (D_MODEL,), 1.0),
        "mod_w": nrm(ks[7], (DEPTH, D_MODEL, N_SUB * 3 * D_MODEL), 0.5 * D_MODEL ** -0.5),
        "mod_b": nrm(ks[8], (DEPTH, N_SUB * 3 * D_MODEL), 0.02),
        "norm_g": 1.0 + nrm(ks[9], (DEPTH, N_SUB, D_MODEL), 0.02),
        "ffn_w_gu": nrm(ks[10], (DEPTH, 2, D_MODEL, 2 * D_FF), D_MODEL ** -0.5),
        "ffn_w_down": nrm(ks[11], (DEPTH, 2, D_FF, D_MODEL), D_FF ** -0.5),
        "lru_w_in": nrm(ks[12], (N_REC, D_MODEL, 2 * D_RNN), D_MODEL ** -0.5),
        "lru_conv_w": nrm(ks[13], (N_REC, CONV_W, D_RNN), CONV_W ** -0.5),
        "lru_conv_b": nrm(ks[14], (N_REC, D_RNN), 0.01),
        "lru_gate_w": nrm(ks[16], (N_REC, 2, 2, LRU_BLOCKS, LRU_BLOCK_W, LRU_BLOCK_W), LRU_BLOCK_W ** -0.5),
        "lru_gate_b": nrm(ks[17], (N_REC, 2, 2, D_RNN), 0.01),
        "lru_lambda": lam,
        "lru_w_out": nrm(ks[18], (N_REC, D_RNN, D_MODEL), D_RNN ** -0.5),
        "att_w_qkv": nrm(ks[19], (N_ATT, D_MODEL, QKV_DIM), D_MODEL ** -0.5),
        "att_q_g": 1.0 + nrm(ks[20], (N_ATT, HEAD_DIM), 0.02),
        "att_k_g": 1.0 + nrm(ks[21], (N_ATT, HEAD_DIM), 0.02),
        "att_w_o": nrm(ks[22], (N_ATT, N_HEADS * HEAD_DIM, D_MODEL), (N_HEADS * HEAD_DIM) ** -0.5),
        "final_g": 1.0 + nrm(ks[23], (D_MODEL,), 0.02),
    }


def reference(x_prompt, x_sample, c, state_lru, cache_k, cache_v, c_ctx, mod_w, mod_b, norm_g,
              ffn_w_gu, ffn_w_down, lru_w_in, lru_conv_w, lru_conv_b, lru_gate_w, lru_gate_b,
              lru_lambda, lru_w_out, att_w_qkv, att_q_g, att_k_g, att_w_o, final_g):
    xp, xs = x_prompt, x_sample
    cos, sin = axial_rope_angles(xs.shape[1])
    new_states, new_k, new_v = [], [], []
    for layer in range(DEPTH):
        j = layer // N_MIXERS
        g = norm_g[layer]
        mp = modulation(c_ctx[None], mod_w[layer], mod_b[layer])
        ms = modulation(c, mod_w[layer], mod_b[layer])

        xp = xp + 0.5 * sub_gate(mp, 0) * swiglu(sub_in(xp, g, mp, 0), ffn_w_gu[layer, 0], ffn_w_down[layer, 0])
        xs = xs + 0.5 * sub_gate(ms, 0) * swiglu(sub_in(xs, g, ms, 0), ffn_w_gu[layer, 0], ffn_w_down[layer, 0])

        hp = sub_in(xp, g, mp, 1)
        hs = sub_in(xs, g, ms, 1)
        if layer % N_MIXERS == 0:
            lru_p = (lru_w_in[j], lru_conv_w[j], lru_conv_b[j], lru_gate_w[j], lru_gate_b[j],
                     lru_lambda[j], lru_w_out[j])
            h0 = jnp.zeros((xp.shape[0], 2, D_RNN), xp.dtype)
            op, st = rglru_block(hp, *lru_p, h0)
            new_states.append(st)
            os_, _ = rglru_block(hs, *lru_p, state_lru[:, j])
        else:
            qp, kp, vp = qkv_heads(hp, att_w_qkv[j], att_q_g[j], att_k_g[j])
            new_k.append(kp)
            new_v.append(vp)
            op = block_attention(qp, kp, vp) @ att_w_o[j]
            qs, ks_, vs = qkv_heads(hs, att_w_qkv[j], att_q_g[j], att_k_g[j])
            qs = apply_axial_rope(qs, cos, sin)
            ks_ = apply_axial_rope(ks_, cos, sin)
            k_all = jnp.concatenate([cache_k[:, j].astype(ks_.dtype), ks_], axis=1)
            v_all = jnp.concatenate([cache_v[:, j].astype(vs.dtype), vs], axis=1)
            os_ = block_attention(qs, k_all, v_all) @ att_w_o[j]
        xp = xp + sub_gate(mp, 1) * op
        xs = xs + sub_gate(ms, 1) * os_

        xp = xp + 0.5 * sub_gate(mp, 2) * swiglu(sub_in(xp, g, mp, 2), ffn_w_gu[layer, 1], ffn_w_down[layer, 1])
        xs = xs + 0.5 * sub_gate(ms, 2) * swiglu(sub_in(xs, g, ms, 2), ffn_w_gu[layer, 1], ffn_w_down[layer, 1])

    y_prompt = rmsnorm(xp, final_g)
    y_sample = rmsnorm(xs, final_g)
    new_state_lru = jnp.stack(new_states, axis=1)
    new_cache_k = jnp.stack(new_k, axis=1)
    new_cache_v = jnp.stack(new_v, axis=1)
    return (y_prompt, y_sample, new_state_lru, new_cache_k, new_cache_v)
```

```python
import contextlib
import numpy as np
import concourse.bass as bass
import concourse.mybir as mybir
from concourse.bass_utils import run_bass_kernel_spmd

F32 = mybir.dt.float32
BF16 = mybir.dt.bfloat16
ALU = mybir.AluOpType
AF = mybir.ActivationFunctionType
AX = mybir.AxisListType

D = 1024
DC = 8
T = 1024
TT = 2
NSEG = 4
SEG = 256
DFF = 2816
FC = 22
HD = 128
NH = 8
NKV = 2
PAST = 512
KC = 12
EPS = 1e-6
BIG = 16384.0
SCALE = float(HD) ** -0.5
NSLOT = 5
SLOT_ELEMS = 4096

DEBUG_STOP = None


class Sem:
    def __init__(self, h, name):
        self.h = h
        self.name = name
        self.count = 0


class Buf:
    __slots__ = ("w", "r", "name")

    def __init__(self, name="", epoch=()):
        self.w = None
        self.r = list(epoch)
        self.name = name


class Tn:
    def __init__(self, t, bufs):
        self.t = t
        self.b = bufs

    def __getitem__(self, k):
        return self.t[k]


class KB:
    def __init__(self, nc, stack):
        self.nc = nc
        self.stack = stack
        self.topstack = stack
        self.eng = {"pe": nc.tensor, "act": nc.scalar, "dve": nc.vector, "pool": nc.gpsimd, "sp": nc.sync}
        self.waited = {e: {} for e in self.eng}
        self.freed = []
        self._alloc_lists = []
        self.psem = {}
        self.dma_sems = []
        for e in ("pe", "act", "dve", "pool"):
            self.psem[e] = self.newsem("prog_" + e)
        self.nsem = 0

    def newsem(self, name, dma=False):
        h = self.topstack.enter_context(self.nc.semaphore(name))
        s = Sem(h, name)
        if dma:
            self.dma_sems.append(s)
        return s

    def epoch(self, addr, size):
        need = {}
        for (a0, a1, toks) in self.freed:
            if a0 < addr + size and addr < a1:
                for s_, v in toks:
                    if need.get(s_, (s_, 0))[1] < v:
                        need[s_] = (s_, v)
        return list(need.values())

    @contextlib.contextmanager
    def scope(self):
        saved = self.stack
        allocs = []
        self._alloc_lists.append(allocs)
        with contextlib.ExitStack() as st:
            self.stack = st
            try:
                yield
            finally:
                self._alloc_lists.pop()
                for tn in allocs:
                    flat = []
                    for b in tn.b:
                        flat.extend(b if isinstance(b, list) else [b])
                    toks = {}
                    for b in flat:
                        for tok in ([b.w] if b.w is not None else []) + list(b.r):
                            s_, v = tok
                            if toks.get(s_, (s_, 0))[1] < v:
                                toks[s_] = (s_, v)
                    self.freed.append((tn.addr, tn.addr + tn.size, list(toks.values())))
                self.stack = saved

    def _wait(self, e, deps):
        need = {}
        for (s, v) in deps:
            if need.get(s, (None, 0))[1] < v:
                need[s] = (s, v)
        eng = self.eng[e]
        wd = self.waited[e]
        for s, v in need.values():
            if wd.get(s, 0) < v:
                eng.wait_ge(s.h, v)
                wd[s] = v

    def _deps(self, r, w):
        deps = []
        for b in r:
            if b.w is not None:
                deps.append(b.w)
        for b in w:
            if b.w is not None:
                deps.append(b.w)
            deps.extend(b.r)
        return deps

    def _commit(self, tok, r, w):
        for b in r:
            b.r.append(tok)
        for b in w:
            b.w = tok
            b.r = []

    def emit(self, e, fns, r=(), w=()):
        if callable(fns):
            fns = [fns]
        self._wait(e, self._deps(r, w))
        inst = None
        for f in fns:
            inst = f()
        s = self.psem[e]
        s.count += 1
        inst.then_inc(s.h, 1)
        tok = (s, s.count)
        self._commit(tok, r, w)
        return tok

    def dma(self, q, out, in_, sem, r=(), w=(), **kw):
        self._wait(q, self._deps(r, w))
        inst = self.eng[q].dma_start(out=out, in_=in_, **kw)
        sem.count += 16
        inst.then_inc(sem.h, 16)
        tok = (sem, sem.count)
        self._commit(tok, r, w)
        return tok

    def alloc(self, name, shape, dtype, nb=1, psum=False):
        self.nsem += 1
        name = f"t{self.nsem}_{name}"
        if psum:
            t = self.stack.enter_context(self.nc.psum_tensor(name, shape, dtype))
        else:
            t = self.stack.enter_context(self.nc.sbuf_tensor(name, shape, dtype))
        if psum:
            addr, size = 0, 0
            ep = []
        else:
            ml = self.nc.lookup_mloc(t)
            addr, size = int(ml.addr), int(ml.dims[1])
            ep = self.epoch(addr, size)
        if isinstance(nb, int):
            bufs = [Buf(f"{name}{i}", ep) for i in range(nb)]
        else:
            bufs = [[Buf(f"{name}{i}_{j}", ep) for j in range(nb[1])] for i in range(nb[0])]
        tn = Tn(t, bufs)
        tn.addr, tn.size = addr, size
        if self._alloc_lists:
            self._alloc_lists[-1].append(tn)
        return tn


def bc(ap, axis, n):
    l = [list(x) for x in ap.ap]
    l.insert(axis, [0, n])
    return bass.AP(ap.tensor, ap.offset, l)


def build_program():
    nc = bass.Bass("TRN2", target_bir_lowering=False)

    def din(name, shape, dt=F32):
        return nc.dram_tensor(name, list(shape), dt, kind="ExternalInput").ap()

    def dout(name, shape, dt=F32):
        return nc.dram_tensor(name, list(shape), dt, kind="ExternalOutput").ap()

    xin = din("xin", [T, D])
    cvec = din("cvec", [D])
    h0in = din("h0", [2, D])
    linkin = din("link", [128, 1])
    ckin = din("ck", [PAST, NKV * HD])
    cvin = din("cv", [PAST, NKV * HD])
    maskin = din("maskadd", [128, KC * 4])
    cosin = din("rcos", [T, 64])
    sinin = din("rsin", [T, 64])
    identin = din("ident", [128, 128])
    mod_w = din("mod_w", [2, D, 9 * D])
    mod_b = din("mod_b", [2, 9 * D])
    norm_g = din("norm_g", [2, 3, D])
    w_gu = din("ffn_w_gu", [2, 2, D, 2 * DFF])
    w_down = din("ffn_w_down", [2, 2, DFF, D])
    lru_w_in = din("lru_w_in", [1, D, 2 * D])
    lru_conv_w = din("lru_conv_w", [1, 4, D])
    lru_conv_b = din("lru_conv_b", [1, D])
    lru_gate_w = din("lru_gate_w", [1, 2, 2, 16, 64, 64])
    lru_gate_b = din("lru_gate_b", [1, 2, 2, D])
    lru_lambda = din("lru_lambda", [1, 2, D])
    lru_w_out = din("lru_w_out", [1, D, D])
    att_w_qkv = din("att_w_qkv", [1, D, 1536])
    att_q_g = din("att_q_g", [1, HD])
    att_k_g = din("att_k_g", [1, HD])
    att_w_o = din("att_w_o", [1, D, D])
    final_g = din("final_g", [D])
    y_out = dout("y", [T, D])
    st_out = dout("st", [8, D])
    nk_out = dout("nk", [T, NKV * HD])
    nv_out = dout("nv", [T, NKV * HD])

    stack = contextlib.ExitStack()
    with stack:
        stack.enter_context(nc.allow_non_contiguous_dma(reason="small strided constant loads"))
        kb = KB(nc, stack)
        V = nc.vector
        A = nc.scalar
        PE = nc.tensor

        xT = kb.alloc("xT", [128, DC, T], F32, nb=(DC, TT))
        hT = kb.alloc("hT", [128, DC, T], BF16, nb=TT)
        ps = [kb.alloc(f"ps{i}", [128, 512], F32, psum=True) for i in range(8)]
        ps_i = [0]

        ps_excl = set()

        def psn():
            while (ps_i[0] % 8) in ps_excl:
                ps_i[0] += 1
            p = ps[ps_i[0] % 8]
            ps_i[0] += 1
            return p

        slots = [kb.alloc(f"slot{i}", [128, SLOT_ELEMS], BF16) for i in range(NSLOT)]
        slot_sems = [kb.newsem(f"slotsem{i}", dma=True) for i in range(NSLOT)]
        wd_sems = [kb.newsem(f"wdsem{i}", dma=True) for i in range(DC)]
        slot_i = [0]
        pinned = set()
        nfetch = [0]
        fetch_gate = []

        def fetch(parts):
            while (slot_i[0] % NSLOT) in pinned:
                slot_i[0] += 1
            i = slot_i[0] % NSLOT
            slot_i[0] += 1
            sl = slots[i]
            sl.idx = i
            nfetch[0] += 1
            if nfetch[0] == 5 and fetch_gate:
                kb._wait("pool", list(fetch_gate))
            kb._wait("pool", kb._deps((), [sl.b[0]]))
            sem = slot_sems[i]
            for (vf, src) in parts:
                inst = nc.gpsimd.dma_start(out=vf(sl.t[:]), in_=src)
                sem.count += 16
                inst.then_inc(sem.h, 16)
            kb._commit((sem, sem.count), (), [sl.b[0]])
            if nfetch[0] <= 4:
                fetch_gate.append((sem, sem.count))
            return sl

        csem = kb.newsem("const", dma=True)
        csem_p = kb.newsem("const_pool", dma=True)
        const_bufs = []
        const_bufs_p = []

        def cload(name, shape, src, dtype=F32, q="sp"):
            t = kb.alloc(name, shape, dtype)
            kb.dma(q, t.t[:], src, csem if q == "sp" else csem_p, w=[t.b[0]])
            (const_bufs if q == "sp" else const_bufs_p).append(t.b[0])
            return t

        ident = cload("ident", [128, 128], identin)
        link_sb = cload("link", [128, 1], linkin)
        stg_specs = [
            [("modb0", mod_b[0].rearrange("(c f) -> c f", f=128), 72),
             ("cvec", cvec.rearrange("(c f) -> c f", f=128), 8),
             ("ng", norm_g.rearrange("l s (c f) -> (l s c) f", f=128), 48)],
            [("modb1", mod_b[1].rearrange("(c f) -> c f", f=128), 72),
             ("fg", final_g.rearrange("(c f) -> c f", f=128), 8),
             ("h0", h0in.rearrange("r (c f) -> (r c) f", f=128), 16),
             ("convb", lru_conv_b[0].rearrange("(c f) -> c f", f=128), 8),
             ("lam", lru_lambda[0].rearrange("a (c f) -> (a c) f", f=128), 16)],
            [("convw", lru_conv_w[0].rearrange("k (c f) -> (k c) f", f=128), 32),
             ("gateb", lru_gate_b[0].rearrange("a b (c f) -> (a b c) f", f=128), 32)],
        ]
        stg = [kb.alloc(f"stg{i}", [128, 128], F32) for i in range(3)]
        coff = {}
        for i, specs in enumerate(stg_specs):
            ro = 0
            for (nm, src, rows) in specs:
                inst = nc.sync.dma_start(out=stg[i].t[ro:ro + rows, :], in_=src)
                csem.count += 16
                inst.then_inc(csem.h, 16)
                coff[nm] = i * 128 + ro
                ro += rows
            const_bufs.append(stg[i].b[0])
        ctok = (csem, csem.count)
        for b in const_bufs:
            b.w = ctok
        for b in const_bufs_p:
            b.w = (csem_p, csem_p.count)

        cpack = kb.alloc("cpack", [128, 3 * 128], F32)
        for i, specs in enumerate(stg_specs):
            R_ = sum(r_ for (_, _, r_) in specs)
            p = psn()
            kb.emit("pe", lambda: PE.transpose(p.t[:, 0:R_], stg[i].t[0:R_, :], ident.t[0:R_, 0:R_]),
                    r=[stg[i].b[0], ident.b[0]], w=[p.b[0]])
            kb.emit("dve", lambda: V.tensor_copy(out=cpack.t[:, i * 128:i * 128 + R_], in_=p.t[:, 0:R_]),
                    r=[p.b[0]], w=[cpack.b[0]])

        class CV:
            def __init__(self, ap):
                self.t = ap
                self.b = cpack.b

        def cview(nm, n, pat=None, **kw):
            a = cpack.t[:, coff[nm]:coff[nm] + n]
            if pat:
                a = a.rearrange(pat, **kw)
            return CV(a)
        cv_sb = cview("cvec", 8)
        modb_sb = [cview("modb0", 72), cview("modb1", 72)]
        ng_sb = cview("ng", 48, "p (l s c) -> p l s c", l=2, s=3)
        fg_sb = cview("fg", 8)
        h0_sb = cview("h0", 16, "p (r c) -> p r c", r=2)
        convb_sb = cview("convb", 8)
        lam_sb = cview("lam", 16, "p (a c) -> p a c", a=2)
        convw_sb = cview("convw", 32, "p (k c) -> p k c", k=4)
        gateb_sb = cview("gateb", 32, "p (a c) -> p a c", a=4)

        ones_f = kb.alloc("ones_f", [128, 128], F32)
        kb.emit("dve", lambda: V.memset(ones_f.t[:], 1.0), w=[ones_f.b[0]])
        eps_sb = kb.alloc("eps_sb", [128, 2], F32)
        kb.emit("dve", lambda: V.memset(eps_sb.t[:, 0:1], EPS), w=[eps_sb.b[0]])
        kb.emit("dve", lambda: V.memset(eps_sb.t[:, 1:2], 1.0), w=[eps_sb.b[0]])
        ones_b = kb.alloc("ones_b", [128, 128], BF16)
        kb.emit("dve", lambda: V.memset(ones_b.t[:], 1.0), w=[ones_b.b[0]])

        gw = kb.alloc("gw", [128, 4, DC, 128], BF16)
        kb.emit("dve", lambda: V.memset(gw.t[:], 0.0), w=[gw.b[0]])
        gsem = kb.newsem("gwsem", dma=True)
        for hb in range(2):
            src = lru_gate_w[0].rearrange("a b (dc h) k j -> h k (a b) dc j", h=2)[hb]
            kb.dma("pool", gw.t[hb * 64:(hb + 1) * 64, :, :, hb * 64:(hb + 1) * 64], src, gsem, w=[gw.b[0]])
        lw = kb.alloc("lw", [128, 4, DC], F32)
        Lc = kb.alloc("Lc", [128, 2, DC], F32)
        hgb = kb.alloc("hgb", [128, 4, DC], F32)
        t1 = kb.alloc("lt1", [128, 2, DC], F32)
        t2 = kb.alloc("lt2", [128, 2, DC], F32)
        t3 = kb.alloc("lt3", [128, 2, DC], F32)
        kb.emit("dve", lambda: V.tensor_scalar_mul(out=lw.t[:], in0=convw_sb.t[:], scalar1=link_sb.t[:, 0:1]),
                r=[convw_sb.b[0], link_sb.b[0]], w=[lw.b[0]])
        kb.emit("dve", lambda: V.tensor_scalar_mul(out=hgb.t[:], in0=gateb_sb.t[:], scalar1=0.5),
                r=[gateb_sb.b[0]], w=[hgb.b[0]])
        kb.emit("act", lambda: A.activation(out=t1.t[:], in_=lam_sb.t[:], func=AF.Abs),
                r=[lam_sb.b[0]], w=[t1.b[0]])
        kb.emit("act", lambda: A.activation(out=t1.t[:], in_=t1.t[:], func=AF.Exp, scale=-1.0),
                r=[t1.b[0]], w=[t1.b[0]])
        kb.emit("dve", lambda: V.tensor_scalar_add(out=t2.t[:], in0=t1.t[:], scalar1=2.0), r=[t1.b[0]], w=[t2.b[0]])
        kb.emit("dve", lambda: V.reciprocal(out=t2.t[:], in_=t2.t[:]), r=[t2.b[0]], w=[t2.b[0]])
        kb.emit("dve", lambda: V.tensor_tensor(out=t2.t[:], in0=t2.t[:], in1=t1.t[:], op=ALU.mult),
                r=[t1.b[0], t2.b[0]], w=[t2.b[0]])
        kb.emit("dve", lambda: V.tensor_tensor(out=t3.t[:], in0=t2.t[:], in1=t2.t[:], op=ALU.mult),
                r=[t2.b[0]], w=[t3.b[0]])
        kb.emit("dve", lambda: V.memset(t1.t[:], 1.0 / 17.0), r=[], w=[t1.b[0]])
        for k in (15, 13, 11, 9, 7, 5, 3, 1):
            kb.emit("dve", lambda: V.tensor_tensor(out=t1.t[:], in0=t1.t[:], in1=t3.t[:], op=ALU.mult),
                    r=[t3.b[0]], w=[t1.b[0]])
            kb.emit("dve", lambda k=k: V.tensor_scalar_add(out=t1.t[:], in0=t1.t[:], scalar1=1.0 / k),
                    r=[], w=[t1.b[0]])
        kb.emit("dve", lambda: V.tensor_tensor(out=t1.t[:], in0=t1.t[:], in1=t2.t[:], op=ALU.mult),
                r=[t2.b[0]], w=[t1.b[0]])
        kb.emit("dve", lambda: V.tensor_scalar_min(out=t3.t[:], in0=lam_sb.t[:], scalar1=0.0),
                r=[lam_sb.b[0]], w=[t3.b[0]])
        kb.emit("dve", lambda: V.scalar_tensor_tensor(out=Lc.t[:], in0=t1.t[:], scalar=-2.0, in1=t3.t[:],
                                                      op0=ALU.mult, op1=ALU.add),
                r=[t1.b[0], t3.b[0]], w=[Lc.b[0]])
        kb.emit("dve", lambda: V.tensor_scalar_mul(out=Lc.t[:], in0=Lc.t[:], scalar1=4.0), r=[], w=[Lc.b[0]])
        qh = kb.alloc("qh", [128, 1], F32)
        kb.emit("dve", lambda: V.memset(qh.t[:], 0.25), w=[qh.b[0]])


        with kb.scope():
            xtok = [kb.alloc(f"xtok{i}", [128, D], F32) for i in range(2)]
            xsem = [kb.newsem(f"xsem{i}", dma=True) for i in range(2)]
            for tc in range(8):
                xb_ = xtok[tc % 2]
                xtokn = kb.dma("sp", xb_.t[:], xin[tc * 128:(tc + 1) * 128, :], xsem[tc % 2], w=[xb_.b[0]])
                if tc >= 6:
                    fetch_gate.append(xtokn)
                for half in range(2):
                    p = psn()
                    kb.emit("pe", [(lambda p=p, q=q, dc=half * 4 + q: PE.transpose(
                        p.t[:, q * 128:(q + 1) * 128], xb_.t[:, dc * 128:(dc + 1) * 128], ident.t[:]))
                        for q in range(4)], r=[xb_.b[0], ident.b[0]], w=[p.b[0]])
                    tt = tc // 4
                    dst = xT.t[:, half * 4:half * 4 + 4, tc * 128:(tc + 1) * 128]
                    src = p.t[:].rearrange("p (q c) -> p q c", q=4)
                    eng = "act" if half == 0 else "dve"
                    if eng == "act":
                        kb.emit("act", lambda: A.copy(out=dst, in_=src), r=[p.b[0]],
                                w=[xT.b[half * 4 + q][tt] for q in range(4)])
                    else:
                        kb.emit("dve", lambda: V.tensor_copy(out=dst, in_=src), r=[p.b[0]],
                                w=[xT.b[half * 4 + q][tt] for q in range(4)])

        modv = kb.alloc("modv", [128, 2, 72], F32, nb=6)
        scs = kb.alloc("scs", [128, DC], BF16)
        tmpc = kb.alloc("tmpc", [128, DC], F32)
        kb.emit("act", lambda: A.activation(out=tmpc.t[:], in_=cv_sb.t[:], func=AF.Silu),
                r=[cv_sb.b[0]], w=[tmpc.b[0]])
        kb.emit("dve", lambda: V.tensor_copy(out=scs.t[:], in_=tmpc.t[:]), r=[tmpc.b[0]], w=[scs.b[0]])

        mod_pending = [(l, ch) for l in range(2) for ch in range(18)]

        def mod_chunk(l, ch, bank=None, col0=0, do_add=True):
            mod_pending.remove((l, ch))
            src = mod_w[l].rearrange("(dc p) f -> p dc f", p=128)[:, :, ch * 512:(ch + 1) * 512]
            sl = fetch([(lambda s: s.rearrange("p (dc f) -> p dc f", dc=DC), src)])
            sv = sl.t[:].rearrange("p (dc f) -> p dc f", dc=DC)
            p = psn() if bank is None else bank
            fns = []
            for c4 in range(4):
                for dc in range(DC):
                    fns.append(lambda c4=c4, dc=dc: PE.matmul(
                        p.t[:, col0 + c4:col0 + c4 + 1], sv[:, dc, c4 * 128:(c4 + 1) * 128], scs.t[:, dc:dc + 1],
                        start=(dc == 0), stop=(dc == DC - 1)))
            kb.emit("pe", fns, r=[sl.b[0], scs.b[0]], w=[p.b[0]])
            if do_add:
                kb.emit("dve", lambda: V.tensor_tensor(out=modv.t[:, l, ch * 4:(ch + 1) * 4], in0=p.t[:, col0:col0 + 4],
                                                       in1=modb_sb[l].t[:, ch * 4:(ch + 1) * 4], op=ALU.add),
                        r=[p.b[0], cpack.b[0]], w=[modv.b[l * 3 + ch // 6]])

        def run_bg(n=1):
            for _ in range(n):
                if mod_pending:
                    mod_chunk(*mod_pending[0])

        def ensure_mod(l, s_):
            for ch in range(s_ * 6, s_ * 6 + 6):
                if (l, ch) in mod_pending:
                    mod_chunk(l, ch)

        coef = kb.alloc("coef", [128, 3, DC], F32)
        coefg = kb.alloc("coefg", [128, DC], F32)

        def ensure_mod_part(l, s_, lo, hi):
            for ch in range(s_ * 6 + lo, s_ * 6 + hi):
                if (l, ch) in mod_pending:
                    mod_chunk(l, ch)

        def make_coef_g(l, s, gmul):
            ensure_mod_part(l, s, 4, 6)
            base = s * 24
            kb.emit("dve", lambda: V.tensor_scalar_mul(out=coefg.t[:], in0=modv.t[:, l, base + 16:base + 24],
                                                        scalar1=gmul), r=[modv.b[l * 3 + s]], w=[coefg.b[0]])

        def make_coef(l, s, gmul):
            ensure_mod_part(l, s, 0, 4)
            base = s * 24
            kb.emit("dve", lambda: V.scalar_tensor_tensor(
                out=coef.t[:, 0, :], in0=modv.t[:, l, base + 8:base + 16], scalar=1.0, in1=ng_sb.t[:, l, s, :],
                op0=ALU.add, op1=ALU.mult), r=[modv.b[l * 3 + s], ng_sb.b[0]], w=[coef.b[0]])
            kb.emit("dve", lambda: V.tensor_copy(out=coef.t[:, 1, :], in_=modv.t[:, l, base:base + 8]),
                    r=[modv.b[l * 3 + s]], w=[coef.b[0]])

        sq_top = [kb.alloc(f"sq{i}", [128, 512], BF16) for i in range(4)]

        stats_state = {"st": None}

        def stats_begin():
            pss = [psn(), psn()]
            idx = [ps.index(p_) for p_ in pss]
            for i_ in idx:
                ps_excl.add(i_)
            stats_state["st"] = {"pss": pss, "idx": idx, "n": 0, "done": set()}

        def stats_sq(dc, tt):
            st = stats_state["st"]
            tsl = slice(tt * 512, (tt + 1) * 512)
            s_ = sq_top[st["n"] % 4]
            st["n"] += 1
            st.setdefault("tiles", {})[(dc, tt)] = s_
            kb.emit("act", lambda: A.activation(out=s_.t[:], in_=xT.t[:, dc, tsl], func=AF.Square),
                    r=[xT.b[dc][tt]], w=[s_.b[0]])

        def stats_mm(dc, tt):
            st = stats_state["st"]
            p = st["pss"][tt]
            s_ = st["tiles"].pop((dc, tt))
            kb.emit("pe", lambda: PE.matmul(p.t[:], ones_b.t[:], s_.t[:], start=(dc == 0), stop=(dc == DC - 1)),
                    r=[s_.b[0], ones_b.b[0]], w=[p.b[0]])

        def stats_tile1(dc, tt):
            stats_sq(dc, tt)
            stats_mm(dc, tt)

        def stats_tile(dc):
            for tt in range(TT):
                stats_tile1(dc, tt)

        def sub_in_half(tt, acol, bcol, out_t, out_bufs, acol_buf, extra_w=(), col0=None):
            st = stats_state["st"]
            tsl = slice(tt * 512, (tt + 1) * 512)
            osl = tsl if col0 is None else slice(col0, col0 + 512)
            p = st["pss"][tt]
            rstd = kb.alloc("rstdh", [128, 512], F32)
            xn = [kb.alloc(f"xnh{i}", [128, 512], F32) for i in range(2)]
            kb.emit("act", lambda: A.activation(out=rstd.t[:], in_=p.t[:], func=AF.Ln, bias=eps_sb.t[:, 0:1],
                                                scale=1.0 / D), r=[p.b[0], eps_sb.b[0]], w=[rstd.b[0]])
            kb.emit("act", lambda: A.activation(out=rstd.t[:], in_=rstd.t[:], func=AF.Exp, scale=-0.5),
                    r=[rstd.b[0]], w=[rstd.b[0]])
            ps_excl.discard(st["idx"][tt])
            for dc in range(DC):
                x_ = xn[dc % 2]
                kb.emit("dve", lambda: V.scalar_tensor_tensor(out=x_.t[:], in0=xT.t[:, dc, tsl], scalar=acol(dc),
                                                              in1=rstd.t[:], op0=ALU.mult, op1=ALU.mult),
                        r=[xT.b[dc][tt], rstd.b[0], acol_buf], w=[x_.b[0]])
                if bcol is not None:
                    kb.emit("act", lambda: A.activation(out=out_t[:, dc, osl], in_=x_.t[:], func=AF.Identity,
                                                        bias=bcol(dc)),
                            r=[x_.b[0], acol_buf], w=[out_bufs[tt]] + list(extra_w))
                else:
                    kb.emit("act", lambda: A.copy(out=out_t[:, dc, osl], in_=x_.t[:]),
                            r=[x_.b[0]], w=[out_bufs[tt]] + list(extra_w))
            st["done"].add(tt)
            if len(st["done"]) == TT:
                stats_state["st"] = None

        def sub_in(acol, bcol, out_t, out_bufs, acol_buf, final=False, pre_apply=None):
            if stats_state["st"] is None:
                stats_begin()
                for dc in range(DC):
                    stats_tile(dc)
            st = stats_state["st"]
            with kb.scope():
                if 0 not in st["done"]:
                    if pre_apply is not None:
                        pre_apply()
                    sub_in_half(0, acol, bcol, out_t, out_bufs, acol_buf)
                sub_in_half(1, acol, bcol, out_t, out_bufs, acol_buf)

        def mid_apply(nxt):
            if nxt == "final":
                yv = hT.t[:].bitcast(F32)
                sub_in_half(0, lambda dc: fg_sb.t[:, dc:dc + 1], None, yv, hT.b, fg_sb.b[0], extra_w=[hT.b[1]], col0=0)
                early_out["yv"] = yv
                return
            make_coef(nxt[0], nxt[1], None)
            sub_in_half(0, lambda dc: coef.t[:, 0, dc:dc + 1], lambda dc: coef.t[:, 1, dc:dc + 1], hT.t, hT.b, coef.b[0])

        early_out = {}

        def emit_out_tc(tc, src_t, src_bufs, yb_, sem_):
            c0 = (tc % 4) * 128 if src_t is early_out.get("yv") else tc * 128
            for half in range(2):
                p = psn()
                kb.emit("pe", [(lambda q=q: PE.transpose(p.t[:, q * 128:(q + 1) * 128],
                                                         src_t[:, half * 4 + q, c0:c0 + 128], ident.t[:]))
                               for q in range(4)], r=list(src_bufs) + [ident.b[0]], w=[p.b[0]])
                if half == 0:
                    kb.emit("act", lambda: A.copy(out=yb_.t[:, 0:512], in_=p.t[:]), r=[p.b[0]], w=[yb_.b[0]])
                else:
                    kb.emit("dve", lambda: V.tensor_copy(out=yb_.t[:, 512:1024], in_=p.t[:]), r=[p.b[0]], w=[yb_.b[0]])
            kb.dma("sp", y_out[tc * 128:(tc + 1) * 128, :], yb_.t[:], sem_, r=[yb_.b[0]])

        def closing(group_fn, nxt):
            stats_begin()
            LAG = 2
            seq = [(tt, do) for tt in range(TT) for do in range(DC)]
            for gi, (tt, do) in enumerate(seq):
                pp = group_fn(tt, do)
                resid_update(pp, do, tt)
                stats_sq(do, tt)
                if gi >= LAG:
                    ptt, pdo = seq[gi - LAG]
                    stats_mm(pdo, ptt)
                if gi == DC - 1 + LAG and nxt is not None:
                    mid_apply(nxt)
                    if nxt == "final":
                        early_out["ytok"] = kb.alloc("ytokf", [128, D], F32)
                        early_out["sem"] = kb.newsem("ysemf", dma=True)
                        out_sems.append(early_out["sem"])
                if nxt == "final" and DC + LAG + 1 <= gi <= DC + LAG + 4:
                    emit_out_tc(gi - (DC + LAG + 1), early_out["yv"], [hT.b[0], hT.b[1]], early_out["ytok"], early_out["sem"])
            for gi in range(len(seq) - LAG, len(seq)):
                ptt, pdo = seq[gi]
                stats_mm(pdo, ptt)

        def sub_in_std(pre_apply=None):
            sub_in(lambda dc: coef.t[:, 0, dc:dc + 1], lambda dc: coef.t[:, 1, dc:dc + 1], hT.t, hT.b, coef.b[0],
                   pre_apply=pre_apply)

        def resid_update(p, do, tt):
            tsl = slice(tt * 512, (tt + 1) * 512)
            kb.emit("dve", lambda: V.scalar_tensor_tensor(
                out=xT.t[:, do, tsl], in0=p.t[:], scalar=coefg.t[:, do:do + 1], in1=xT.t[:, do, tsl],
                op0=ALU.mult, op1=ALU.add), r=[p.b[0], coefg.b[0]], w=[xT.b[do][tt]])

        def ffn(l, i, s, nbg=0, nxt=None):
            sub_in_std(pre_apply=lambda: make_coef(l, s, 0.5))
            with kb.scope():
                act = kb.alloc("act", [128, FC, T], BF16, nb=TT)
                sg = [kb.alloc(f"sg{k}", [128, 512], F32) for k in range(2)]
                sg_i = 0
                wd_sb = kb.alloc("wd_sb", [128, DC, FC * 128], BF16, nb=DC)
                wdv = wd_sb.t[:].rearrange("p d (fc c) -> p d fc c", fc=FC)
                wd = w_down[l, i].rearrange("(fc p) d -> p fc d", p=128)
                _dbg("ffn peak")
                chunk_tok = {}

                def load_wd(k, after=None):
                    if after is not None:
                        kb._wait("pool", [after])
                    kb.dma("pool", wdv[:, k, :, :], wd[:, :, k * 128:(k + 1) * 128], wd_sems[k], w=[wd_sb.b[k]])

                wv = w_gu[l, i].rearrange("(dc p) f -> p dc f", p=128)
                for ch in range(FC // 2):
                    j0 = ch * 2
                    sl = fetch([
                        (lambda s_: s_.rearrange("p (dc g c) -> p dc g c", dc=DC, g=2)[:, :, 0, :],
                         wv[:, :, j0 * 128:(j0 + 2) * 128]),
                        (lambda s_: s_.rearrange("p (dc g c) -> p dc g c", dc=DC, g=2)[:, :, 1, :],
                         wv[:, :, DFF + j0 * 128:DFF + (j0 + 2) * 128]),
                    ])
                    if ch >= 6:
                        load_wd(ch - 6)
                    sv = sl.t[:].rearrange("p (dc g c) -> p dc g c", dc=DC, g=2)
                    order = ([(jj, tt) for tt in range(TT) for jj in range(2)] if ch == 0
                             else [(jj, tt) for jj in range(2) for tt in range(TT)])
                    banks = {}
                    for (jj, tt) in order:
                        for g_ in range(2):
                            banks[(jj, g_, tt)] = psn()
                    for (jj, tt) in order:
                        j = j0 + jj
                        for g_ in range(2):
                            pp = banks[(jj, g_, tt)]
                            chunk_tok[ch] = kb.emit("pe", [(lambda dc=dc: PE.matmul(
                                pp.t[:], sv[:, dc, g_, jj * 128:(jj + 1) * 128],
                                hT.t[:, dc, tt * 512:(tt + 1) * 512], start=(dc == 0), stop=(dc == DC - 1)))
                                for dc in range(DC)], r=[sl.b[0], hT.b[tt]], w=[pp.b[0]])
                        pg_, pu_ = banks[(jj, 0, tt)], banks[(jj, 1, tt)]
                        s_ = sg[sg_i % 2]
                        sg_i += 1
                        kb.emit("act", lambda: A.activation(out=s_.t[:], in_=pg_.t[:], func=AF.Silu),
                                r=[pg_.b[0]], w=[s_.b[0]])
                        kb.emit("dve", lambda: V.tensor_tensor(out=act.t[:, j, tt * 512:(tt + 1) * 512], in0=s_.t[:],
                                                               in1=pu_.t[:], op=ALU.mult),
                                r=[s_.b[0], pu_.b[0]], w=[act.b[tt]])
                    if ch < nbg:
                        run_bg(1)
                for k in (5, 6, 7):
                    load_wd(k, after=chunk_tok[k + 2])
                make_coef_g(l, s, 0.5)

                def down_group(tt, do):
                    pp = psn()
                    kb.emit("pe", [(lambda fc=fc: PE.matmul(
                        pp.t[:], wdv[:, do, fc, :], act.t[:, fc, tt * 512:(tt + 1) * 512],
                        start=(fc == 0), stop=(fc == FC - 1))) for fc in range(FC)],
                        r=[wd_sb.b[do], act.b[tt]], w=[pp.b[0]])
                    return pp
                closing(down_group, nxt)

        def lru(l, lru_nxt=None):
            sub_in_std(pre_apply=lambda: make_coef(l, 1, 1.0))
            with kb.scope():
                yT = kb.alloc("yT", [128, DC, T], BF16, nb=TT)
                st_sb = kb.alloc("st_sb", [128, DC, 8], F32)
                NB = 2
                NB3 = 3
                xb = [kb.alloc(f"xb{k}", [128, T], F32) for k in range(NB3)]
                xc = [kb.alloc(f"xc{k}", [128, T], F32) for k in range(NB)]
                xcb = [kb.alloc(f"xcb{k}", [128, T], BF16) for k in range(NB)]
                gy = [kb.alloc(f"gy{k}", [128, T], F32) for k in range(NB3)]
                ra = [[kb.alloc(f"ra{k}{d}", [128, T], F32) for d in range(2)] for k in range(NB)]
                iu = [[kb.alloc(f"iu{k}{d}", [128, T], F32) for d in range(2)] for k in range(NB)]
                tm1 = [kb.alloc(f"tm{k}", [128, T], F32) for k in range(NB)]
                _dbg("lru peak")
                win = lru_w_in[0].rearrange("(dc p) f -> p dc f", p=128)

                def fetch_win(dc):
                    return fetch([
                        (lambda s_: s_[:, 0:2048].rearrange("p (dc g c) -> p dc g c", dc=DC, g=2)[:, :, 0, :],
                         win[:, :, dc * 128:(dc + 1) * 128]),
                        (lambda s_: s_[:, 0:2048].rearrange("p (dc g c) -> p dc g c", dc=DC, g=2)[:, :, 1, :],
                         win[:, :, D + dc * 128:D + (dc + 1) * 128]),
                    ])
                wsl = {0: fetch_win(0)}
                bg_sched = [2, 2, 2, 2, 2, 2, 1, 1]

                def stage_a(dc):
                    k = dc % NB
                    if dc + 1 < DC:
                        wsl[dc + 1] = fetch_win(dc + 1)
                    sl = wsl.pop(dc)
                    sv = sl.t[:, 0:2048].rearrange("p (dc g c) -> p dc g c", dc=DC, g=2)
                    pxb = [psn(), psn()]
                    pyb = [psn(), psn()]
                    for tt in range(TT):
                        for g_, pp in ((0, pxb), (1, pyb)):
                            kb.emit("pe", [(lambda d2=d2: PE.matmul(
                                pp[tt].t[:], sv[:, d2, g_, :], hT.t[:, d2, tt * 512:(tt + 1) * 512],
                                start=(d2 == 0), stop=(d2 == DC - 1))) for d2 in range(DC)],
                                r=[sl.b[0], hT.b[tt]], w=[pp[tt].b[0]])
                    XB, XC, XCB, GY = xb[dc % NB3], xc[k], xcb[k], gy[dc % NB3]
                    for tt in range(TT):
                        tsl = slice(tt * 512, (tt + 1) * 512)
                        kb.emit("act", lambda: A.copy(out=XB.t[:, tsl], in_=pxb[tt].t[:]), r=[pxb[tt].b[0]], w=[XB.b[0]])
                    for tt in range(TT):
                        tsl = slice(tt * 512, (tt + 1) * 512)
                        kb.emit("act", lambda: A.activation(out=GY.t[:, tsl], in_=pyb[tt].t[:], func=AF.Gelu_apprx_tanh),
                                r=[pyb[tt].b[0]], w=[GY.b[0]])
                    cw = lambda kk: convw_sb.t[:, kk, dc:dc + 1]
                    lwk = lambda kk: lw.t[:, kk, dc:dc + 1]
                    xb3 = XB.t[:].rearrange("p (s c) -> p s c", s=NSEG)
                    xc3 = XC.t[:].rearrange("p (s c) -> p s c", s=NSEG)
                    kb.emit("dve", lambda: V.tensor_scalar(out=XC.t[:], in0=XB.t[:], scalar1=cw(1), scalar2=convb_sb.t[:, dc:dc + 1],
                                                           op0=ALU.mult, op1=ALU.add),
                            r=[XB.b[0], convw_sb.b[0], convb_sb.b[0]], w=[XC.b[0]])

                    def mac(o, i_, sc_):
                        kb.emit("dve", lambda: V.scalar_tensor_tensor(out=o, in0=i_, scalar=sc_, in1=o,
                                                                      op0=ALU.mult, op1=ALU.add),
                                r=[XB.b[0], lw.b[0]], w=[XC.b[0]])
                    mac(xc3[:, :, 1:SEG], xb3[:, :, 0:SEG - 1], cw(0))
                    mac(xc3[:, :, 0:SEG - 1], xb3[:, :, 1:SEG], cw(2))
                    mac(xc3[:, :, 0:SEG - 2], xb3[:, :, 2:SEG], cw(3))
                    mac(xc3[:, 1:NSEG, 0], xb3[:, 0:NSEG - 1, SEG - 1], lwk(0))
                    mac(xc3[:, 0:NSEG - 1, SEG - 1], xb3[:, 1:NSEG, 0], lwk(2))
                    mac(xc3[:, 0:NSEG - 1, SEG - 1], xb3[:, 1:NSEG, 1], lwk(3))
                    mac(xc3[:, 0:NSEG - 1, SEG - 2], xb3[:, 1:NSEG, 0], lwk(3))

                def stage_a2(dc):
                    k = dc % NB
                    XB, XC, XCB, GY = xb[dc % NB3], xc[k], xcb[k], gy[dc % NB3]
                    kb.emit("act", lambda: A.copy(out=XCB.t[:], in_=XC.t[:]), r=[XC.b[0]], w=[XCB.b[0]])
                    pgs = {}
                    for d_ in range(2):
                        for gi in range(2):
                            pp = [psn(), psn()]
                            pgs[(d_, gi)] = pp
                            ab = d_ * 2 + gi
                            for tt in range(TT):
                                kb.emit("pe", lambda: PE.matmul(pp[tt].t[:], gw.t[:, ab, dc, :],
                                                                 XCB.t[:, tt * 512:(tt + 1) * 512], start=True, stop=True),
                                        r=[gw.b[0], XCB.b[0]], w=[pp[tt].b[0]])
                        for gi, dstt in ((0, ra[k][d_]), (1, iu[k][d_])):
                            ab = d_ * 2 + gi
                            pp = pgs[(d_, gi)]
                            for tt in range(TT):
                                tsl = slice(tt * 512, (tt + 1) * 512)
                                kb.emit("act", lambda: A.activation(out=dstt.t[:, tsl], in_=pp[tt].t[:], func=AF.Tanh,
                                                                    bias=hgb.t[:, ab, dc:dc + 1], scale=0.5),
                                        r=[pp[tt].b[0], hgb.b[0]], w=[dstt.b[0]])
                    run_bg(bg_sched[dc])
                    for d_ in range(2):
                        RA = ra[k][d_]
                        kb.emit("act", lambda: A.activation(out=RA.t[:], in_=RA.t[:], func=AF.Exp,
                                                            bias=Lc.t[:, d_, dc:dc + 1], scale=Lc.t[:, d_, dc:dc + 1]),
                                r=[Lc.b[0]], w=[RA.b[0]])
                    for d_ in range(2):
                        RA = ra[k][d_]
                        TM = XB if d_ == 0 else tm1[k]
                        kb.emit("act", lambda: A.activation(out=TM.t[:], in_=RA.t[:], func=AF.Square),
                                r=[RA.b[0], XC.b[0]], w=[TM.b[0]])
                    for d_ in range(2):
                        TM = XB if d_ == 0 else tm1[k]
                        kb.emit("act", lambda: A.activation(out=TM.t[:], in_=TM.t[:], func=AF.Sqrt, bias=qh.t[:, 0:1], scale=-0.25),
                                r=[qh.b[0]], w=[TM.b[0]])

                def stage_b(dc):
                    k = dc % NB
                    XB, XC, GY = xb[dc % NB3], xc[k], gy[dc % NB3]
                    for d_ in range(2):
                        RA, IU = ra[k][d_], iu[k][d_]
                        TM = XB if d_ == 0 else tm1[k]
                        kb.emit("dve", lambda: V.tensor_tensor(out=TM.t[:], in0=TM.t[:], in1=XC.t[:], op=ALU.mult),
                                r=[XC.b[0]], w=[TM.b[0]])
                        kb.emit("dve", lambda: V.scalar_tensor_tensor(out=IU.t[:], in0=IU.t[:], scalar=1.0, in1=TM.t[:],
                                                                      op0=ALU.add, op1=ALU.mult),
                                r=[TM.b[0]], w=[IU.b[0]])
                        ra3 = RA.t[:].rearrange("p (s c) -> p s c", s=NSEG)
                        iu3 = IU.t[:].rearrange("p (s c) -> p s c", s=NSEG)
                        if d_ == 0:
                            kb.emit("dve", lambda: V.tensor_scalar_mul(out=ra3[:, 1:NSEG, 0], in0=ra3[:, 1:NSEG, 0],
                                                                        scalar1=link_sb.t[:, 0:1]),
                                    r=[link_sb.b[0]], w=[RA.b[0]])
                            kb.emit("dve", lambda: V.tensor_tensor_scan(out=IU.t[:], data0=RA.t[:], data1=IU.t[:],
                                                                        initial=h0_sb.t[:, 0, dc:dc + 1],
                                                                        op0=ALU.mult, op1=ALU.add),
                                    r=[RA.b[0], h0_sb.b[0]], w=[IU.b[0]])
                            kb.emit("dve", lambda: V.tensor_copy(
                                out=st_sb.t[:, dc, :].rearrange("p (s r) -> p s r", r=2)[:, :, 0],
                                in_=iu3[:, :, SEG - 1]), r=[IU.b[0]], w=[st_sb.b[0]])
                        else:
                            kb.emit("dve", lambda: V.tensor_scalar_mul(out=ra3[:, 0:NSEG - 1, SEG - 1],
                                                                        in0=ra3[:, 0:NSEG - 1, SEG - 1],
                                                                        scalar1=link_sb.t[:, 0:1]),
                                    r=[link_sb.b[0]], w=[RA.b[0]])
                            kb.emit("dve", lambda: V.tensor_tensor_scan(out=IU.t[:, ::-1], data0=RA.t[:, ::-1],
                                                                        data1=IU.t[:, ::-1],
                                                                        initial=h0_sb.t[:, 1, dc:dc + 1],
                                                                        op0=ALU.mult, op1=ALU.add),
                                    r=[RA.b[0], h0_sb.b[0]], w=[IU.b[0]])
                            kb.emit("dve", lambda: V.tensor_copy(
                                out=st_sb.t[:, dc, :].rearrange("p (s r) -> p s r", r=2)[:, :, 1],
                                in_=iu3[:, :, 0]), r=[IU.b[0]], w=[st_sb.b[0]])
                    kb.emit("dve", lambda: V.tensor_tensor(out=iu[k][0].t[:], in0=iu[k][0].t[:], in1=iu[k][1].t[:], op=ALU.add),
                            r=[iu[k][1].b[0]], w=[iu[k][0].b[0]])
                    for tt in range(TT):
                        tsl = slice(tt * 512, (tt + 1) * 512)
                        kb.emit("dve", lambda: V.tensor_tensor(out=yT.t[:, dc, tsl], in0=iu[k][0].t[:, tsl], in1=GY.t[:, tsl],
                                                               op=ALU.mult), r=[iu[k][0].b[0], GY.b[0]], w=[yT.b[tt]])

                stage_a(0)
                for dc in range(DC):
                    if dc + 1 < DC:
                        stage_a(dc + 1)
                    stage_a2(dc)
                    stage_b(dc)
                ssem = kb.newsem("stsem", dma=True)
                st_tok = xb[2]
                p = psn()
                kb.emit("pe", [(lambda dc=dc: PE.transpose(p.t[0:8, dc * 128:(dc + 1) * 128], st_sb.t[:, dc, :], ident.t[:]))
                               for dc in range(4)], r=[st_sb.b[0], ident.b[0]], w=[p.b[0]])
                kb.emit("dve", lambda: V.tensor_copy(out=st_tok.t[0:8, 0:512], in_=p.t[0:8, :]), r=[p.b[0]], w=[st_tok.b[0]])
                p = psn()
                kb.emit("pe", [(lambda dc=dc: PE.transpose(p.t[0:8, (dc - 4) * 128:(dc - 3) * 128], st_sb.t[:, dc, :], ident.t[:]))
                               for dc in range(4, 8)], r=[st_sb.b[0], ident.b[0]], w=[p.b[0]])
                kb.emit("dve", lambda: V.tensor_copy(out=st_tok.t[0:8, 512:1024], in_=p.t[0:8, :]), r=[p.b[0]], w=[st_tok.b[0]])
                kb.dma("sp", st_out, st_tok.t[0:8, :], ssem, r=[st_tok.b[0]])
                out_sems.append(ssem)
                make_coef_g(l, 1, 1.0)
                wo = lru_w_out[0].rearrange("(dc p) f -> p dc f", p=128)
                wo_sl = [fetch([(lambda s_: s_.rearrange("p (dc f) -> p dc f", dc=DC), wo[:, :, ch * 512:(ch + 1) * 512])])
                         for ch in range(2)]
                for w_ in wo_sl:
                    pinned.add(w_.idx)

                def wout_group(tt, do):
                    sl = wo_sl[do // 4]
                    d4 = do % 4
                    sv = sl.t[:].rearrange("p (dc f) -> p dc f", dc=DC)
                    pp = psn()
                    kb.emit("pe", [(lambda d2=d2: PE.matmul(
                        pp.t[:], sv[:, d2, d4 * 128:(d4 + 1) * 128], yT.t[:, d2, tt * 512:(tt + 1) * 512],
                        start=(d2 == 0), stop=(d2 == DC - 1))) for d2 in range(DC)],
                        r=[sl.b[0], yT.b[tt]], w=[pp.b[0]])
                    return pp
                closing(wout_group, lru_nxt)
                pinned.clear()

        out_sems = []

        def _dbg(tag):
            import os
            if os.environ.get("KDEBUG"):
                print("SBUF remaining", tag, nc.sbuf_bytes_remaining)
        kb_tmp = None

        def attention(l, att_nxt=None):
            wq = att_w_qkv[0].rearrange("(dc p) f -> p dc f", p=128)
            wsl = []
            for n in range(3):
                wsl.append(fetch([(lambda s_: s_.rearrange("p (dc f) -> p dc f", dc=DC), wq[:, :, n * 512:(n + 1) * 512])]))
            for w_ in wsl:
                pinned.add(w_.idx)
            sub_in_std(pre_apply=lambda: make_coef(l, 1, 1.0))
            with kb.scope():
                asem = kb.newsem("attnconst", dma=True)
                asem_p = kb.newsem("attnconst_p", dma=True)

                def aload(name, shape, src, dtype=F32, q="sp"):
                    t = kb.alloc(name, shape, dtype)
                    kb.dma(q, t.t[:], src, asem if q == "sp" else asem_p, w=[t.b[0]])
                    return t
                qg_sb = aload("qg", [128, HD], att_q_g[0].partition_broadcast(128))
                kg_sb = aload("kg", [128, HD], att_k_g[0].partition_broadcast(128))
                cos_sb = aload("cos", [128, 8, 64], cosin.rearrange("(c p) f -> p c f", p=128))
                sin_sb = aload("sin", [128, 8, 64], sinin.rearrange("(c p) f -> p c f", p=128))
                mask_sb = aload("maskadd", [128, KC * 4], maskin)
                for t_ in (qg_sb, kg_sb, cos_sb, sin_sb, mask_sb):
                    t_.b[0].w = (asem, asem.count)
                qT = kb.alloc("qT", [128, NH, T], BF16)
                kT = kb.alloc("kT", [128, NKV, KC * 128], BF16)
                Vt = kb.alloc("Vt", [128, KC, NKV * HD], BF16)
                oT = hT
                nbias = kb.alloc("nbias", [128, 1], F32)
                vsem = kb.newsem("cvsem", dma=True)
                kb.dma("pool", Vt.t[:, 0:4, :], cvin.rearrange("(c p) f -> p c f", p=128), vsem, w=[Vt.b[0]])
                ck_sb = kb.alloc("ck_sb", [128, 4, NKV * HD], F32)
                cksem = kb.newsem("cksem", dma=True)
                kb.dma("sp", ck_sb.t[:], ckin.rearrange("(c p) f -> p c f", p=128), cksem, w=[ck_sb.b[0]])
                NQ = 3
                qkv = [kb.alloc(f"qkv{k}", [128, 1536], F32) for k in range(NQ)]
                ssq = [kb.alloc(f"ssq{k}", [128, 10], F32) for k in range(NQ)]
                qr = [kb.alloc(f"qr{k}", [128, 1280], F32) for k in range(NQ)]
                rt1 = kb.alloc("rt", [128, 10, 2, 32], F32)
                rt = [rt1] * NQ
                kvst = [kb.alloc(f"kvst{k}", [128, 512], F32) for k in range(2)]
                kvsem = [kb.newsem(f"kvsem{k}", dma=True) for k in range(2)]
                out_sems.extend(kvsem)

                def front(tc):
                    k = tc % NQ
                    Q, SS, QR = qkv[k], ssq[k], qr[k]
                    for n in range(3):
                        sv = wsl[n].t[:].rearrange("p (dc f) -> p dc f", dc=DC)
                        p = psn()
                        kb.emit("pe", [(lambda dc=dc: PE.matmul(p.t[:], hT.t[:, dc, tc * 128:(tc + 1) * 128], sv[:, dc, :],
                                                               start=(dc == 0), stop=(dc == DC - 1))) for dc in range(DC)],
                                r=[wsl[n].b[0], hT.b[tc // 4]], w=[p.b[0]])
                        kb.emit("act", lambda: A.copy(out=Q.t[:, n * 512:(n + 1) * 512], in_=p.t[:]), r=[p.b[0]], w=[Q.b[0]])
                    for h_ in range(10):
                        kb.emit("act", lambda: A.activation(out=QR.t[:, h_ * HD:(h_ + 1) * HD], in_=Q.t[:, h_ * HD:(h_ + 1) * HD],
                                                            func=AF.Square, accum_out=SS.t[:, h_:h_ + 1]),
                                r=[Q.b[0]], w=[QR.b[0], SS.b[0]])
                    kb.emit("act", lambda: A.activation(out=SS.t[:], in_=SS.t[:], func=AF.Sqrt, bias=eps_sb.t[:, 0:1], scale=1.0 / HD),
                            r=[eps_sb.b[0]], w=[SS.b[0]])

                def mid(tc):
                    k = tc % NQ
                    Q, SS, QR, RT = qkv[k], ssq[k], qr[k], rt[k]
                    kb.emit("dve", lambda: V.reciprocal(out=SS.t[:], in_=SS.t[:]), r=[], w=[SS.b[0]])
                    q3 = Q.t[:, 0:1280].rearrange("p (a h) -> p a h", h=HD)
                    kb.emit("dve", lambda: V.tensor_tensor(out=q3, in0=q3, in1=SS.t[:].unsqueeze(2).broadcast_to([128, 10, HD]),
                                                           op=ALU.mult), r=[SS.b[0]], w=[Q.b[0]])
                    kb.emit("dve", lambda: V.tensor_tensor(out=q3[:, 0:8, :], in0=q3[:, 0:8, :],
                                                           in1=qg_sb.t[:].unsqueeze(1).broadcast_to([128, 8, HD]), op=ALU.mult),
                            r=[qg_sb.b[0]], w=[Q.b[0]])
                    kb.emit("dve", lambda: V.tensor_tensor(out=q3[:, 8:10, :], in0=q3[:, 8:10, :],
                                                           in1=kg_sb.t[:].unsqueeze(1).broadcast_to([128, 2, HD]), op=ALU.mult),
                            r=[kg_sb.b[0]], w=[Q.b[0]])
                    q5 = Q.t[:, 0:1280].rearrange("p (a x h f) -> p a x h f", x=2, h=2, f=32)
                    o5 = QR.t[:].rearrange("p (a x h f) -> p a x h f", x=2, h=2, f=32)
                    x0, x1 = q5[:, :, :, 0, :], q5[:, :, :, 1, :]
                    o0, o1 = o5[:, :, :, 0, :], o5[:, :, :, 1, :]
                    cs = cos_sb.t[:, tc, :].rearrange("p (x f) -> p x f", x=2).unsqueeze(1).broadcast_to([128, 10, 2, 32])
                    sn = sin_sb.t[:, tc, :].rearrange("p (x f) -> p x f", x=2).unsqueeze(1).broadcast_to([128, 10, 2, 32])
                    cdeps = [Q.b[0], cos_sb.b[0], sin_sb.b[0]]
                    kb.emit("dve", lambda: V.tensor_tensor(out=RT.t[:], in0=x1, in1=sn, op=ALU.mult), r=cdeps, w=[RT.b[0]])
                    kb.emit("dve", lambda: V.tensor_tensor(out=o0, in0=x0, in1=cs, op=ALU.mult), r=cdeps, w=[QR.b[0]])
                    kb.emit("dve", lambda: V.tensor_tensor(out=o0, in0=o0, in1=RT.t[:], op=ALU.subtract), r=[RT.b[0]], w=[QR.b[0]])
                    kb.emit("dve", lambda: V.tensor_tensor(out=RT.t[:], in0=x0, in1=sn, op=ALU.mult), r=cdeps, w=[RT.b[0]])
                    kb.emit("dve", lambda: V.tensor_tensor(out=o1, in0=x1, in1=cs, op=ALU.mult), r=cdeps, w=[QR.b[0]])
                    kb.emit("dve", lambda: V.tensor_tensor(out=o1, in0=o1, in1=RT.t[:], op=ALU.add), r=[RT.b[0]], w=[QR.b[0]])

                def back(tc):
                    k = tc % NQ
                    Q, QR, KV = qkv[k], qr[k], kvst[tc % 2]
                    kb.emit("act", lambda: A.copy(out=KV.t[:], in_=Q.t[:, 1024:1536]), r=[Q.b[0]], w=[KV.b[0]])
                    kb.dma("sp", nk_out[tc * 128:(tc + 1) * 128, :], KV.t[:, 0:256], kvsem[tc % 2], r=[KV.b[0]])
                    kb.dma("sp", nv_out[tc * 128:(tc + 1) * 128, :], KV.t[:, 256:512], kvsem[tc % 2], r=[KV.b[0]])
                    kb.emit("act", lambda: A.copy(out=Vt.t[:, 4 + tc, :], in_=Q.t[:, 1280:1536]), r=[Q.b[0]], w=[Vt.b[0]])
                    for grp in range(3):
                        nblk = 4 if grp < 2 else 2
                        p = psn()
                        kb.emit("pe", [(lambda b_=b_: PE.transpose(p.t[:, b_ * 128:(b_ + 1) * 128],
                                                                   QR.t[:, (grp * 4 + b_) * 128:(grp * 4 + b_ + 1) * 128], ident.t[:]))
                                       for b_ in range(nblk)], r=[QR.b[0], ident.b[0]], w=[p.b[0]])
                        if grp < 2:
                            kb.emit("act", lambda: A.copy(out=qT.t[:, grp * 4:(grp + 1) * 4, tc * 128:(tc + 1) * 128],
                                                          in_=p.t[:].rearrange("p (a c) -> p a c", a=4)),
                                    r=[p.b[0]], w=[qT.b[0]])
                        else:
                            kb.emit("act", lambda: A.copy(out=kT.t[:, :, 512 + tc * 128:512 + (tc + 1) * 128],
                                                          in_=p.t[:, 0:256].rearrange("p (a c) -> p a c", a=2)),
                                    r=[p.b[0]], w=[kT.b[0]])

                ps_excl.add(7)
                front(0)
                front(1)
                nmod = 0
                for tc in range(8):
                    if tc + 2 < 8:
                        front(tc + 2)
                    if tc < 6 and (1, 12 + tc) in mod_pending:
                        mod_chunk(1, 12 + tc, bank=ps[7], col0=tc * 4, do_add=False)
                        nmod += 1
                    mid(tc)
                    back(tc)
                if nmod:
                    assert nmod == 6
                    kb.emit("dve", lambda: V.tensor_tensor(out=modv.t[:, 1, 48:72], in0=ps[7].t[:, 0:24],
                                                           in1=modb_sb[1].t[:, 48:72], op=ALU.add),
                            r=[ps[7].b[0], cpack.b[0]], w=[modv.b[5]])
                ps_excl.discard(7)
                pinned.clear()
                for g_ in range(NKV):
                    p = psn()
                    kb.emit("pe", [(lambda c=c: PE.transpose(p.t[:, c * 128:(c + 1) * 128],
                                                             ck_sb.t[:, c, g_ * HD:(g_ + 1) * HD], ident.t[:]))
                                   for c in range(4)], r=[ck_sb.b[0], ident.b[0]], w=[p.b[0]])
                    kb.emit("act", lambda: A.copy(out=kT.t[:, g_, 0:512], in_=p.t[:]), r=[p.b[0]], w=[kT.b[0]])
                s1 = kb.alloc("sb1", [128, 4 * NKV * HD], F32)
                s2 = kb.alloc("sb2", [128, 8], F32)
                m1 = kb.alloc("sbm1", [128, 4], F32)
                r1 = kb.alloc("sbr1", [1, 4], F32)
                ckf = ck_sb.t[:].rearrange("p c f -> p (c f)")
                kb.emit("dve", lambda: V.tensor_tensor(out=s1.t[:], in0=ckf, in1=ckf, op=ALU.mult), r=[ck_sb.b[0]], w=[s1.b[0]])
                kb.emit("dve", lambda: V.tensor_reduce(out=s2.t[:], in_=s1.t[:].rearrange("p (a h) -> p a h", h=HD),
                                                       axis=AX.X, op=ALU.add), r=[s1.b[0]], w=[s2.b[0]])
                kb.emit("dve", lambda: V.tensor_reduce(out=m1.t[:, 0:1], in_=s2.t[:], axis=AX.X, op=ALU.max),
                        r=[s2.b[0]], w=[m1.b[0]])
                p = psn()
                kb.emit("pe", lambda: PE.transpose(p.t[0:1, 0:128], m1.t[:, 0:1], ident.t[:]), r=[m1.b[0], ident.b[0]], w=[p.b[0]])
                kb.emit("dve", lambda: V.tensor_reduce(out=r1.t[:, 0:1], in_=p.t[0:1, 0:128], axis=AX.X, op=ALU.max),
                        r=[p.b[0]], w=[r1.b[0]])
                p2 = psn()
                kb.emit("pe", lambda: PE.matmul(p2.t[:, 0:1], ones_f.t[0:1, :], r1.t[0:1, 0:1], start=True, stop=True),
                        r=[r1.b[0], ones_f.b[0]], w=[p2.b[0]])
                kb.emit("dve", lambda: V.tensor_tensor(out=s1.t[:, 0:HD], in0=kg_sb.t[:], in1=kg_sb.t[:], op=ALU.mult),
                        r=[kg_sb.b[0]], w=[s1.b[0]])
                kb.emit("dve", lambda: V.tensor_reduce(out=m1.t[:, 1:2], in_=s1.t[:, 0:HD], axis=AX.X, op=ALU.max),
                        r=[s1.b[0]], w=[m1.b[0]])
                kb.emit("dve", lambda: V.tensor_tensor(out=s1.t[:, 0:HD], in0=qg_sb.t[:], in1=qg_sb.t[:], op=ALU.mult),
                        r=[qg_sb.b[0]], w=[s1.b[0]])
                kb.emit("dve", lambda: V.tensor_reduce(out=m1.t[:, 2:3], in_=s1.t[:, 0:HD], axis=AX.X, op=ALU.max),
                        r=[s1.b[0]], w=[m1.b[0]])
                kb.emit("dve", lambda: V.scalar_tensor_tensor(out=m1.t[:, 3:4], in0=m1.t[:, 1:2], scalar=float(HD), in1=p2.t[:, 0:1],
                                                              op0=ALU.mult, op1=ALU.max), r=[p2.b[0]], w=[m1.b[0]])
                kb.emit("dve", lambda: V.scalar_tensor_tensor(out=m1.t[:, 3:4], in0=m1.t[:, 2:3], scalar=float(HD), in1=m1.t[:, 3:4],
                                                              op0=ALU.mult, op1=ALU.mult), r=[], w=[m1.b[0]])
                kb.emit("act", lambda: A.activation(out=nbias.t[:], in_=m1.t[:, 3:4], func=AF.Sqrt), r=[m1.b[0]], w=[nbias.b[0]])
                kb.emit("dve", lambda: V.tensor_scalar_mul(out=nbias.t[:], in0=nbias.t[:], scalar1=-SCALE), r=[], w=[nbias.b[0]])

                mb = kb.alloc("maskb", [128, KC * 4], F32)
                kb.emit("dve", lambda: V.tensor_scalar_add(out=mb.t[:], in0=mask_sb.t[:], scalar1=nbias.t[:, 0:1]),
                        r=[mask_sb.b[0], nbias.b[0]], w=[mb.b[0]])
                PT = [kb.alloc(f"PT{k}", [128, 512], BF16) for k in range(4)]
                scb = [ps[0], ps[1], ps[2], ps[7]]
                rec = [kb.alloc(f"rec{k}", [128, 512], F32) for k in range(2)]
                _dbg("attn peak")
                items = [(g_, hp, sg_, kc) for g_ in range(NKV) for hp in range(2) for sg_ in range(NSEG) for kc in range(KC)]
                sc_ps = {}
                acc = {}

                def emit_qk(n):
                    g_, hp, sg_, kc = items[n]
                    h0_ = g_ * 4 + hp * 2
                    p = scb[n % 4]
                    sc_ps[n] = p
                    kb.emit("pe", lambda: PE.matmul(p.t[:].rearrange("p (j c) -> p j c", j=2), kT.t[:, g_, kc * 128:(kc + 1) * 128],
                                                     qT.t[:, h0_:h0_ + 2, sg_ * SEG:(sg_ + 1) * SEG], start=True, stop=True),
                            r=[kT.b[0], qT.b[0]], w=[p.b[0]])

                def emit_pv(n):
                    g_, hp, sg_, kc = items[n]
                    h0_ = g_ * 4 + hp * 2
                    p = sc_ps.pop(n)
                    pt = PT[n % 4]
                    col = kc * 4 + sg_
                    kb.emit("act", lambda: A.activation(out=pt.t[:], in_=p.t[:], func=AF.Exp,
                                                        bias=mb.t[:, col:col + 1], scale=SCALE),
                            r=[p.b[0], mb.b[0]], w=[pt.b[0]])
                    if kc == 0:
                        acc[(g_, hp, sg_)] = ((ps[3], ps[4]), (ps[5], ps[6]))[(n // KC) % 2]
                    po, psum_ = acc[(g_, hp, sg_)]
                    kb.emit("pe", [
                        lambda: PE.matmul(po.t[:], Vt.t[:, kc, g_ * HD:(g_ + 1) * HD], pt.t[:], start=(kc == 0), stop=(kc == KC - 1)),
                        lambda: PE.matmul(psum_.t[:], ones_b.t[:], pt.t[:], start=(kc == 0), stop=(kc == KC - 1))],
                        r=[Vt.b[0], pt.b[0], ones_b.b[0]], w=[po.b[0], psum_.b[0]])
                    if kc == KC - 1:
                        rc = rec[(n // KC) % 2]
                        kb.emit("dve", lambda: V.reciprocal(out=rc.t[:], in_=psum_.t[:]), r=[psum_.b[0]], w=[rc.b[0]])
                        kb.emit("dve", lambda: V.tensor_tensor(
                            out=oT.t[:, h0_:h0_ + 2, sg_ * SEG:(sg_ + 1) * SEG],
                            in0=po.t[:].rearrange("p (j c) -> p j c", j=2), in1=rc.t[:].rearrange("p (j c) -> p j c", j=2),
                            op=ALU.mult), r=[po.b[0], rc.b[0]], w=[oT.b[sg_ // 2]])
                        del acc[(g_, hp, sg_)]

                emit_qk(0)
                emit_qk(1)
                for n in range(len(items)):
                    if n + 2 < len(items):
                        emit_qk(n + 2)
                    emit_pv(n)
                make_coef_g(l, 1, 1.0)
                wo = att_w_o[0].rearrange("(h p) f -> p h f", p=128)
                wo_sl = [fetch([(lambda s_: s_.rearrange("p (h f) -> p h f", h=NH), wo[:, :, ch * 512:(ch + 1) * 512])])
                         for ch in range(2)]
                for w_ in wo_sl:
                    pinned.add(w_.idx)

                def wo_group(tt, do):
                    sl = wo_sl[do // 4]
                    d4 = do % 4
                    sv = sl.t[:].rearrange("p (h f) -> p h f", h=NH)
                    pp = psn()
                    kb.emit("pe", [(lambda h_=h_: PE.matmul(
                        pp.t[:], sv[:, h_, d4 * 128:(d4 + 1) * 128], oT.t[:, h_, tt * 512:(tt + 1) * 512],
                        start=(h_ == 0), stop=(h_ == NH - 1))) for h_ in range(NH)],
                        r=[sl.b[0], oT.b[tt]], w=[pp.b[0]])
                    return pp
                ps_i[0] = 7
                closing(wo_group, att_nxt)
                pinned.clear()

        stage = [0]

        def stop_here():
            stage[0] += 1
            return DEBUG_STOP is not None and stage[0] > DEBUG_STOP

        def body():
            nonlocal kb_tmp
            for l in range(2):
                if stop_here():
                    return
                ffn(l, 0, 0, nbg=(6 if l == 0 else 0), nxt=(l, 1))
                if stop_here():
                    return
                if l == 0:
                    lru(l, lru_nxt=(l, 2))
                else:
                    attention(l, att_nxt=(l, 2))
                if stop_here():
                    return
                ffn(l, 1, 2, nbg=(6 if l == 0 else 0), nxt=((l + 1, 0) if l == 0 else ("final" if DEBUG_STOP is None else None)))
                if stop_here():
                    return

        body()

        with kb.scope():
            yT_ = kb.alloc("yfin", [128, DC, T], F32, nb=TT)
            if DEBUG_STOP is None:
                sub_in(lambda dc: fg_sb.t[:, dc:dc + 1], None, yT_.t, yT_.b, fg_sb.b[0])
            else:
                for tt in range(TT):
                    for dc in range(DC):
                        kb.emit("dve", lambda: V.tensor_copy(out=yT_.t[:, dc, tt * 512:(tt + 1) * 512],
                                                             in_=xT.t[:, dc, tt * 512:(tt + 1) * 512]),
                                r=[xT.b[dc][tt]], w=[yT_.b[tt]])
            ytok = [kb.alloc(f"ytok{i}", [128, D], F32) for i in range(2)]
            ysem = [kb.newsem(f"ysem{i}", dma=True) for i in range(2)]
            out_sems.extend(ysem)
            for tc in range(4 if "yv" in early_out else 0, 8):
                emit_out_tc(tc, yT_.t, [yT_.b[tc // 4]], ytok[tc % 2], ysem[tc % 2])
            for s in out_sems:
                if s.count > 0:
                    nc.sync.wait_ge(s.h, s.count)
    return nc


_NC_CACHE = {}

PROMPT_ASSIGN = [[0, 1, 2], [3, 4, 5], [6, 7, 8], [9, 10, 11], [12, 13], [14, 15]]


def _rope_tables():
    t = np.arange(1024)
    r_idx = (t // 64).astype(np.float32)
    c_idx = (t % 64).astype(np.float32)
    n_freq = 32
    inv = (np.float32(10000.0) ** (-np.arange(n_freq, dtype=np.float32) / np.float32(n_freq))).astype(np.float32)
    ang = np.stack([r_idx[:, None] * inv, c_idx[:, None] * inv], axis=1).astype(np.float32)
    return np.cos(ang).reshape(1024, 64).astype(np.float32), np.sin(ang).reshape(1024, 64).astype(np.float32)


def kernel(x_prompt, x_sample, c, state_lru, cache_k, cache_v, c_ctx, mod_w, mod_b, norm_g,
           ffn_w_gu, ffn_w_down, lru_w_in, lru_conv_w, lru_conv_b, lru_gate_w, lru_gate_b,
           lru_lambda, lru_w_out, att_w_qkv, att_q_g, att_k_g, att_w_o, final_g):
    f = lambda a: np.ascontiguousarray(np.asarray(a, dtype=np.float32))
    x_prompt, x_sample = f(x_prompt), f(x_sample)
    if "nc" not in _NC_CACHE:
        _NC_CACHE["nc"] = build_program()
    nc = _NC_CACHE["nc"]
    shared = dict(mod_w=f(mod_w), mod_b=f(mod_b), norm_g=f(norm_g), ffn_w_gu=f(ffn_w_gu), ffn_w_down=f(ffn_w_down),
                  lru_w_in=f(lru_w_in), lru_conv_w=f(lru_conv_w), lru_conv_b=f(lru_conv_b), lru_gate_w=f(lru_gate_w),
                  lru_gate_b=f(lru_gate_b), lru_lambda=f(lru_lambda), lru_w_out=f(lru_w_out), att_w_qkv=f(att_w_qkv),
                  att_q_g=f(att_q_g), att_k_g=f(att_k_g), att_w_o=f(att_w_o), final_g=f(final_g),
                  ident=np.eye(128, dtype=np.float32))
    rcos, rsin = _rope_tables()
    in_maps = []
    for core in range(8):
        m = dict(shared)
        if core < 2:
            b = core
            m["xin"] = f(x_sample[b])
            m["cvec"] = f(c[b])
            m["h0"] = f(state_lru[b, 0])
            m["link"] = np.ones((128, 1), np.float32)
            m["ck"] = f(np.asarray(cache_k)[b, 0].reshape(PAST, NKV * HD))
            m["cv"] = f(np.asarray(cache_v)[b, 0].reshape(PAST, NKV * HD))
            ma = np.zeros((128, KC * 4), np.float32)
            m["rcos"], m["rsin"] = rcos, rsin
        else:
            seqs = PROMPT_ASSIGN[core - 2]
            xin = np.zeros((T, D), np.float32)
            for s_, sq_ in enumerate(seqs):
                xin[s_ * SEG:(s_ + 1) * SEG] = x_prompt[sq_]
            m["xin"] = xin
            m["cvec"] = f(c_ctx)
            m["h0"] = np.zeros((2, D), np.float32)
            m["link"] = np.zeros((128, 1), np.float32)
            m["ck"] = np.zeros((PAST, NKV * HD), np.float32)
            m["cv"] = np.zeros((PAST, NKV * HD), np.float32)
            ma = np.full((128, KC * 4), -30000.0, np.float32)
            for kc_ in range(4, KC):
                ma[:, kc_ * 4 + (kc_ - 4) // 2] = 0.0
            m["rcos"] = np.ones((T, 64), np.float32)
            m["rsin"] = np.zeros((T, 64), np.float32)
        m["maskadd"] = ma
        in_maps.append(m)
    res = run_bass_kernel_spmd(nc, in_maps, core_ids=list(range(8)))
    R = res.results
    y_prompt = np.zeros((16, SEG, D), np.float32)
    y_sample = np.zeros((2, T, D), np.float32)
    new_state = np.zeros((16, 1, 2, D), np.float32)
    new_k = np.zeros((16, 1, SEG, NKV, HD), np.float32)
    new_v = np.zeros((16, 1, SEG, NKV, HD), np.float32)
    for core in range(8):
        r = R[core]
        y = np.asarray(r["y"])
        if core < 2:
            y_sample[core] = y
        else:
            st = np.asarray(r["st"]).reshape(NSEG, 2, D)
            nk = np.asarray(r["nk"]).reshape(T, NKV, HD)
            nv = np.asarray(r["nv"]).reshape(T, NKV, HD)
            for s_, sq_ in enumerate(PROMPT_ASSIGN[core - 2]):
                y_prompt[sq_] = y[s_ * SEG:(s_ + 1) * SEG]
                new_state[sq_, 0] = st[s_]
                new_k[sq_, 0] = nk[s_ * SEG:(s_ + 1) * SEG]
                new_v[sq_, 0] = nv[s_ * SEG:(s_ + 1) * SEG]
    return (y_prompt, y_sample, new_state, new_k, new_v)
```

```python
import contextlib
import numpy as np
import concourse.bass as bass
import concourse.mybir as mybir
from concourse.bass_utils import run_bass_kernel_spmd

F32 = mybir.dt.float32
BF16 = mybir.dt.bfloat16
ALU = mybir.AluOpType
AF = mybir.ActivationFunctionType
AX = mybir.AxisListType

D = 1024
DC = 8
T = 1024
TT = 2
NSEG = 4
SEG = 256
DFF = 2816
FC = 22
HD = 128
NH = 8
NKV = 2
PAST = 512
KC = 12
EPS = 1e-6
BIG = 16384.0
SCALE = float(HD) ** -0.5
NSLOT = 5
SLOT_ELEMS = 4096

DEBUG_STOP = None


class Sem:
    def __init__(self, h, name):
        self.h = h
        self.name = name
        self.count = 0


class Buf:
    __slots__ = ("w", "r", "name")

    def __init__(self, name="", epoch=()):
        self.w = None
        self.r = list(epoch)
        self.name = name


class Tn:
    def __init__(self, t, bufs):
        self.t = t
        self.b = bufs

    def __getitem__(self, k):
        return self.t[k]


class KB:
    def __init__(self, nc, stack):
        self.nc = nc
        self.stack = stack
        self.topstack = stack
        self.eng = {"pe": nc.tensor, "act": nc.scalar, "dve": nc.vector, "pool": nc.gpsimd, "sp": nc.sync}
        self.waited = {e: {} for e in self.eng}
        self.freed = []
        self._alloc_lists = []
        self.psem = {}
        self.dma_sems = []
        for e in ("pe", "act", "dve", "pool"):
            self.psem[e] = self.newsem("prog_" + e)
        self.nsem = 0

    def newsem(self, name, dma=False):
        h = self.topstack.enter_context(self.nc.semaphore(name))
        s = Sem(h, name)
        if dma:
            self.dma_sems.append(s)
        return s

    def epoch(self, addr, size):
        need = {}
        for (a0, a1, toks) in self.freed:
            if a0 < addr + size and addr < a1:
                for s_, v in toks:
                    if need.get(s_, (s_, 0))[1] < v:
                        need[s_] = (s_, v)
        return list(need.values())

    @contextlib.contextmanager
    def scope(self):
        saved = self.stack
        allocs = []
        self._alloc_lists.append(allocs)
        with contextlib.ExitStack() as st:
            self.stack = st
            try:
                yield
            finally:
                self._alloc_lists.pop()
                for tn in allocs:
                    flat = []
                    for b in tn.b:
                        flat.extend(b if isinstance(b, list) else [b])
                    toks = {}
                    for b in flat:
                        for tok in ([b.w] if b.w is not None else []) + list(b.r):
                            s_, v = tok
                            if toks.get(s_, (s_, 0))[1] < v:
                                toks[s_] = (s_, v)
                    self.freed.append((tn.addr, tn.addr + tn.size, list(toks.values())))
                self.stack = saved

    def _wait(self, e, deps):
        need = {}
        for (s, v) in deps:
            if need.get(s, (None, 0))[1] < v:
                need[s] = (s, v)
        eng = self.eng[e]
        wd = self.waited[e]
        for s, v in need.values():
            if wd.get(s, 0) < v:
                eng.wait_ge(s.h, v)
                wd[s] = v

    def _deps(self, r, w):
        deps = []
        for b in r:
            if b.w is not None:
                deps.append(b.w)
        for b in w:
            if b.w is not None:
                deps.append(b.w)
            deps.extend(b.r)
        return deps

    def _commit(self, tok, r, w):
        for b in r:
            b.r.append(tok)
        for b in w:
            b.w = tok
            b.r = []

    def emit(self, e, fns, r=(), w=()):
        if callable(fns):
            fns = [fns]
        self._wait(e, self._deps(r, w))
        inst = None
        for f in fns:
            inst = f()
        s = self.psem[e]
        s.count += 1
        inst.then_inc(s.h, 1)
        tok = (s, s.count)
        self._commit(tok, r, w)
        return tok

    def dma(self, q, out, in_, sem, r=(), w=(), **kw):
        self._wait(q, self._deps(r, w))
        inst = self.eng[q].dma_start(out=out, in_=in_, **kw)
        sem.count += 16
        inst.then_inc(sem.h, 16)
        tok = (sem, sem.count)
        self._commit(tok, r, w)
        return tok

    def alloc(self, name, shape, dtype, nb=1, psum=False):
        self.nsem += 1
        name = f"t{self.nsem}_{name}"
        if psum:
            t = self.stack.enter_context(self.nc.psum_tensor(name, shape, dtype))
        else:
            t = self.stack.enter_context(self.nc.sbuf_tensor(name, shape, dtype))
        if psum:
            addr, size = 0, 0
            ep = []
        else:
            ml = self.nc.lookup_mloc(t)
            addr, size = int(ml.addr), int(ml.dims[1])
            ep = self.epoch(addr, size)
        if isinstance(nb, int):
            bufs = [Buf(f"{name}{i}", ep) for i in range(nb)]
        else:
            bufs = [[Buf(f"{name}{i}_{j}", ep) for j in range(nb[1])] for i in range(nb[0])]
        tn = Tn(t, bufs)
        tn.addr, tn.size = addr, size
        if self._alloc_lists:
            self._alloc_lists[-1].append(tn)
        return tn


def bc(ap, axis, n):
    l = [list(x) for x in ap.ap]
    l.insert(axis, [0, n])
    return bass.AP(ap.tensor, ap.offset, l)


def build_program():
    nc = bass.Bass("TRN2", target_bir_lowering=False)

    def din(name, shape, dt=F32):
        return nc.dram_tensor(name, list(shape), dt, kind="ExternalInput").ap()

    def dout(name, shape, dt=F32):
        return nc.dram_tensor(name, list(shape), dt, kind="ExternalOutput").ap()

    xin = din("xin", [T, D])
    cvec = din("cvec", [D])
    h0in = din("h0", [2, D])
    linkin = din("link", [128, 1])
    ckin = din("ck", [PAST, NKV * HD])
    cvin = din("cv", [PAST, NKV * HD])
    maskin = din("maskadd", [128, KC * 4])
    cosin = din("rcos", [T, 64])
    sinin = din("rsin", [T, 64])
    identin = din("ident", [128, 128])
    mod_w = din("mod_w", [2, D, 9 * D])
    mod_b = din("mod_b", [2, 9 * D])
    norm_g = din("norm_g", [2, 3, D])
    w_gu = din("ffn_w_gu", [2, 2, D, 2 * DFF])
    w_down = din("ffn_w_down", [2, 2, DFF, D])
    lru_w_in = din("lru_w_in", [1, D, 2 * D])
    lru_conv_w = din("lru_conv_w", [1, 4, D])
    lru_conv_b = din("lru_conv_b", [1, D])
    lru_gate_w = din("lru_gate_w", [1, 2, 2, 16, 64, 64])
    lru_gate_b = din("lru_gate_b", [1, 2, 2, D])
    lru_lambda = din("lru_lambda", [1, 2, D])
    lru_w_out = din("lru_w_out", [1, D, D])
    att_w_qkv = din("att_w_qkv", [1, D, 1536])
    att_q_g = din("att_q_g", [1, HD])
    att_k_g = din("att_k_g", [1, HD])
    att_w_o = din("att_w_o", [1, D, D])
    final_g = din("final_g", [D])
    y_out = dout("y", [T, D])
    st_out = dout("st", [8, D])
    nk_out = dout("nk", [T, NKV * HD])
    nv_out = dout("nv", [T, NKV * HD])

    stack = contextlib.ExitStack()
    with stack:
        stack.enter_context(nc.allow_non_contiguous_dma(reason="small strided constant loads"))
        kb = KB(nc, stack)
        V = nc.vector
        A = nc.scalar
        PE = nc.tensor

        xT = kb.alloc("xT", [128, DC, T], F32, nb=(DC, TT))
        hT = kb.alloc("hT", [128, DC, T], BF16, nb=TT)
        ps = [kb.alloc(f"ps{i}", [128, 512], F32, psum=True) for i in range(8)]
        ps_i = [0]

        ps_excl = set()

        def psn():
            while (ps_i[0] % 8) in ps_excl:
                ps_i[0] += 1
            p = ps[ps_i[0] % 8]
            ps_i[0] += 1
            return p

        slots = [kb.alloc(f"slot{i}", [128, SLOT_ELEMS], BF16) for i in range(NSLOT)]
        slot_sems = [kb.newsem(f"slotsem{i}", dma=True) for i in range(NSLOT)]
        wd_sems = [kb.newsem(f"wdsem{i}", dma=True) for i in range(DC)]
        slot_i = [0]
        pinned = set()
        nfetch = [0]
        fetch_gate = []

        def fetch(parts):
            while (slot_i[0] % NSLOT) in pinned:
                slot_i[0] += 1
            i = slot_i[0] % NSLOT
            slot_i[0] += 1
            sl = slots[i]
            sl.idx = i
            nfetch[0] += 1
            if nfetch[0] == 5 and fetch_gate:
                kb._wait("pool", list(fetch_gate))
            kb._wait("pool", kb._deps((), [sl.b[0]]))
            sem = slot_sems[i]
            for (vf, src) in parts:
                inst = nc.gpsimd.dma_start(out=vf(sl.t[:]), in_=src)
                sem.count += 16
                inst.then_inc(sem.h, 16)
            kb._commit((sem, sem.count), (), [sl.b[0]])
            if nfetch[0] <= 4:
                fetch_gate.append((sem, sem.count))
            return sl

        csem = kb.newsem("const", dma=True)
        csem_p = kb.newsem("const_pool", dma=True)
        const_bufs = []
        const_bufs_p = []

        def cload(name, shape, src, dtype=F32, q="sp"):
            t = kb.alloc(name, shape, dtype)
            kb.dma(q, t.t[:], src, csem if q == "sp" else csem_p, w=[t.b[0]])
            (const_bufs if q == "sp" else const_bufs_p).append(t.b[0])
            return t

        ident = cload("ident", [128, 128], identin)
        link_sb = cload("link", [128, 1], linkin)
        stg_specs = [
            [("modb0", mod_b[0].rearrange("(c f) -> c f", f=128), 72),
             ("cvec", cvec.rearrange("(c f) -> c f", f=128), 8),
             ("ng", norm_g.rearrange("l s (c f) -> (l s c) f", f=128), 48)],
            [("modb1", mod_b[1].rearrange("(c f) -> c f", f=128), 72),
             ("fg", final_g.rearrange("(c f) -> c f", f=128), 8),
             ("h0", h0in.rearrange("r (c f) -> (r c) f", f=128), 16),
             ("convb", lru_conv_b[0].rearrange("(c f) -> c f", f=128), 8),
             ("lam", lru_lambda[0].rearrange("a (c f) -> (a c) f", f=128), 16)],
            [("convw", lru_conv_w[0].rearrange("k (c f) -> (k c) f", f=128), 32),
             ("gateb", lru_gate_b[0].rearrange("a b (c f) -> (a b c) f", f=128), 32)],
        ]
        stg = [kb.alloc(f"stg{i}", [128, 128], F32) for i in range(3)]
        coff = {}
        for i, specs in enumerate(stg_specs):
            ro = 0
            for (nm, src, rows) in specs:
                inst = nc.sync.dma_start(out=stg[i].t[ro:ro + rows, :], in_=src)
                csem.count += 16
                inst.then_inc(csem.h, 16)
                coff[nm] = i * 128 + ro
                ro += rows
            const_bufs.append(stg[i].b[0])
        ctok = (csem, csem.count)
        for b in const_bufs:
            b.w = ctok
        for b in const_bufs_p:
            b.w = (csem_p, csem_p.count)

        cpack = kb.alloc("cpack", [128, 3 * 128], F32)
        for i, specs in enumerate(stg_specs):
            R_ = sum(r_ for (_, _, r_) in specs)
            p = psn()
            kb.emit("pe", lambda: PE.transpose(p.t[:, 0:R_], stg[i].t[0:R_, :], ident.t[0:R_, 0:R_]),
                    r=[stg[i].b[0], ident.b[0]], w=[p.b[0]])
            kb.emit("dve", lambda: V.tensor_copy(out=cpack.t[:, i * 128:i * 128 + R_], in_=p.t[:, 0:R_]),
                    r=[p.b[0]], w=[cpack.b[0]])

        class CV:
            def __init__(self, ap):
                self.t = ap
                self.b = cpack.b

        def cview(nm, n, pat=None, **kw):
            a = cpack.t[:, coff[nm]:coff[nm] + n]
            if pat:
                a = a.rearrange(pat, **kw)
            return CV(a)
        cv_sb = cview("cvec", 8)
        modb_sb = [cview("modb0", 72), cview("modb1", 72)]
        ng_sb = cview("ng", 48, "p (l s c) -> p l s c", l=2, s=3)
        fg_sb = cview("fg", 8)
        h0_sb = cview("h0", 16, "p (r c) -> p r c", r=2)
        convb_sb = cview("convb", 8)
        lam_sb = cview("lam", 16, "p (a c) -> p a c", a=2)
        convw_sb = cview("convw", 32, "p (k c) -> p k c", k=4)
        gateb_sb = cview("gateb", 32, "p (a c) -> p a c", a=4)

        ones_f = kb.alloc("ones_f", [128, 128], F32)
        kb.emit("dve", lambda: V.memset(ones_f.t[:], 1.0), w=[ones_f.b[0]])
        eps_sb = kb.alloc("eps_sb", [128, 2], F32)
        kb.emit("dve", lambda: V.memset(eps_sb.t[:, 0:1], EPS), w=[eps_sb.b[0]])
        kb.emit("dve", lambda: V.memset(eps_sb.t[:, 1:2], 1.0), w=[eps_sb.b[0]])
        ones_b = kb.alloc("ones_b", [128, 128], BF16)
        kb.emit("dve", lambda: V.memset(ones_b.t[:], 1.0), w=[ones_b.b[0]])

        gw = kb.alloc("gw", [128, 4, DC, 128], BF16)
        kb.emit("dve", lambda: V.memset(gw.t[:], 0.0), w=[gw.b[0]])
        gsem = kb.newsem("gwsem", dma=True)
        for hb in range(2):
            src = lru_gate_w[0].rearrange("a b (dc h) k j -> h k (a b) dc j", h=2)[hb]
            kb.dma("pool", gw.t[hb * 64:(hb + 1) * 64, :, :, hb * 64:(hb + 1) * 64], src, gsem, w=[gw.b[0]])
        lw = kb.alloc("lw", [128, 4, DC], F32)
        Lc = kb.alloc("Lc", [128, 2, DC], F32)
        hgb = kb.alloc("hgb", [128, 4, DC], F32)
        t1 = kb.alloc("lt1", [128, 2, DC], F32)
        t2 = kb.alloc("lt2", [128, 2, DC], F32)
        t3 = kb.alloc("lt3", [128, 2, DC], F32)
        kb.emit("dve", lambda: V.tensor_scalar_mul(out=lw.t[:], in0=convw_sb.t[:], scalar1=link_sb.t[:, 0:1]),
                r=[convw_sb.b[0], link_sb.b[0]], w=[lw.b[0]])
        kb.emit("dve", lambda: V.tensor_scalar_mul(out=hgb.t[:], in0=gateb_sb.t[:], scalar1=0.5),
                r=[gateb_sb.b[0]], w=[hgb.b[0]])
        kb.emit("act", lambda: A.activation(out=t1.t[:], in_=lam_sb.t[:], func=AF.Abs),
                r=[lam_sb.b[0]], w=[t1.b[0]])
        kb.emit("act", lambda: A.activation(out=t1.t[:], in_=t1.t[:], func=AF.Exp, scale=-1.0),
                r=[t1.b[0]], w=[t1.b[0]])
        kb.emit("dve", lambda: V.tensor_scalar_add(out=t2.t[:], in0=t1.t[:], scalar1=2.0), r=[t1.b[0]], w=[t2.b[0]])
        kb.emit("dve", lambda: V.reciprocal(out=t2.t[:], in_=t2.t[:]), r=[t2.b[0]], w=[t2.b[0]])
        kb.emit("dve", lambda: V.tensor_tensor(out=t2.t[:], in0=t2.t[:], in1=t1.t[:], op=ALU.mult),
                r=[t1.b[0], t2.b[0]], w=[t2.b[0]])
        kb.emit("dve", lambda: V.tensor_tensor(out=t3.t[:], in0=t2.t[:], in1=t2.t[:], op=ALU.mult),
                r=[t2.b[0]], w=[t3.b[0]])
        kb.emit("dve", lambda: V.memset(t1.t[:], 1.0 / 17.0), r=[], w=[t1.b[0]])
        for k in (15, 13, 11, 9, 7, 5, 3, 1):
            kb.emit("dve", lambda: V.tensor_tensor(out=t1.t[:], in0=t1.t[:], in1=t3.t[:], op=ALU.mult),
                    r=[t3.b[0]], w=[t1.b[0]])
            kb.emit("dve", lambda k=k: V.tensor_scalar_add(out=t1.t[:], in0=t1.t[:], scalar1=1.0 / k),
                    r=[], w=[t1.b[0]])
        kb.emit("dve", lambda: V.tensor_tensor(out=t1.t[:], in0=t1.t[:], in1=t2.t[:], op=ALU.mult),
                r=[t2.b[0]], w=[t1.b[0]])
        kb.emit("dve", lambda: V.tensor_scalar_min(out=t3.t[:], in0=lam_sb.t[:], scalar1=0.0),
                r=[lam_sb.b[0]], w=[t3.b[0]])
        kb.emit("dve", lambda: V.scalar_tensor_tensor(out=Lc.t[:], in0=t1.t[:], scalar=-2.0, in1=t3.t[:],
                                                      op0=ALU.mult, op1=ALU.add),
                r=[t1.b[0], t3.b[0]], w=[Lc.b[0]])
        kb.emit("dve", lambda: V.tensor_scalar_mul(out=Lc.t[:], in0=Lc.t[:], scalar1=4.0), r=[], w=[Lc.b[0]])
        qh = kb.alloc("qh", [128, 1], F32)
        kb.emit("dve", lambda: V.memset(qh.t[:], 0.25), w=[qh.b[0]])


        with kb.scope():
            xtok = [kb.alloc(f"xtok{i}", [128, D], F32) for i in range(2)]
            xsem = [kb.newsem(f"xsem{i}", dma=True) for i in range(2)]
            for tc in range(8):
                xb_ = xtok[tc % 2]
                xtokn = kb.dma("sp", xb_.t[:], xin[tc * 128:(tc + 1) * 128, :], xsem[tc % 2], w=[xb_.b[0]])
                if tc >= 6:
                    fetch_gate.append(xtokn)
                for half in range(2):
                    p = psn()
                    kb.emit("pe", [(lambda p=p, q=q, dc=half * 4 + q: PE.transpose(
                        p.t[:, q * 128:(q + 1) * 128], xb_.t[:, dc * 128:(dc + 1) * 128], ident.t[:]))
                        for q in range(4)], r=[xb_.b[0], ident.b[0]], w=[p.b[0]])
                    tt = tc // 4
                    dst = xT.t[:, half * 4:half * 4 + 4, tc * 128:(tc + 1) * 128]
                    src = p.t[:].rearrange("p (q c) -> p q c", q=4)
                    eng = "act" if half == 0 else "dve"
                    if eng == "act":
                        kb.emit("act", lambda: A.copy(out=dst, in_=src), r=[p.b[0]],
                                w=[xT.b[half * 4 + q][tt] for q in range(4)])
                    else:
                        kb.emit("dve", lambda: V.tensor_copy(out=dst, in_=src), r=[p.b[0]],
                                w=[xT.b[half * 4 + q][tt] for q in range(4)])

        modv = kb.alloc("modv", [128, 2, 72], F32, nb=6)
        scs = kb.alloc("scs", [128, DC], BF16)
        tmpc = kb.alloc("tmpc", [128, DC], F32)
        kb.emit("act", lambda: A.activation(out=tmpc.t[:], in_=cv_sb.t[:], func=AF.Silu),
                r=[cv_sb.b[0]], w=[tmpc.b[0]])
        kb.emit("dve", lambda: V.tensor_copy(out=scs.t[:], in_=tmpc.t[:]), r=[tmpc.b[0]], w=[scs.b[0]])

        mod_pending = [(l, ch) for l in range(2) for ch in range(18)]

        def mod_chunk(l, ch, bank=None, col0=0, do_add=True):
            mod_pending.remove((l, ch))
            src = mod_w[l].rearrange("(dc p) f -> p dc f", p=128)[:, :, ch * 512:(ch + 1) * 512]
            sl = fetch([(lambda s: s.rearrange("p (dc f) -> p dc f", dc=DC), src)])
            sv = sl.t[:].rearrange("p (dc f) -> p dc f", dc=DC)
            p = psn() if bank is None else bank
            fns = []
            for c4 in range(4):
                for dc in range(DC):
                    fns.append(lambda c4=c4, dc=dc: PE.matmul(
                        p.t[:, col0 + c4:col0 + c4 + 1], sv[:, dc, c4 * 128:(c4 + 1) * 128], scs.t[:, dc:dc + 1],
                        start=(dc == 0), stop=(dc == DC - 1)))
            kb.emit("pe", fns, r=[sl.b[0], scs.b[0]], w=[p.b[0]])
            if do_add:
                kb.emit("dve", lambda: V.tensor_tensor(out=modv.t[:, l, ch * 4:(ch + 1) * 4], in0=p.t[:, col0:col0 + 4],
                                                       in1=modb_sb[l].t[:, ch * 4:(ch + 1) * 4], op=ALU.add),
                        r=[p.b[0], cpack.b[0]], w=[modv.b[l * 3 + ch // 6]])

        def run_bg(n=1):
            for _ in range(n):
                if mod_pending:
                    mod_chunk(*mod_pending[0])

        def ensure_mod(l, s_):
            for ch in range(s_ * 6, s_ * 6 + 6):
                if (l, ch) in mod_pending:
                    mod_chunk(l, ch)

        coef = kb.alloc("coef", [128, 3, DC], F32)
        coefg = kb.alloc("coefg", [128, DC], F32)

        def ensure_mod_part(l, s_, lo, hi):
            for ch in range(s_ * 6 + lo, s_ * 6 + hi):
                if (l, ch) in mod_pending:
                    mod_chunk(l, ch)

        def make_coef_g(l, s, gmul):
            ensure_mod_part(l, s, 4, 6)
            base = s * 24
            kb.emit("dve", lambda: V.tensor_scalar_mul(out=coefg.t[:], in0=modv.t[:, l, base + 16:base + 24],
                                                        scalar1=gmul), r=[modv.b[l * 3 + s]], w=[coefg.b[0]])

        def make_coef(l, s, gmul):
            ensure_mod_part(l, s, 0, 4)
            base = s * 24
            kb.emit("dve", lambda: V.scalar_tensor_tensor(
                out=coef.t[:, 0, :], in0=modv.t[:, l, base + 8:base + 16], scalar=1.0, in1=ng_sb.t[:, l, s, :],
                op0=ALU.add, op1=ALU.mult), r=[modv.b[l * 3 + s], ng_sb.b[0]], w=[coef.b[0]])
            kb.emit("dve", lambda: V.tensor_copy(out=coef.t[:, 1, :], in_=modv.t[:, l, base:base + 8]),
                    r=[modv.b[l * 3 + s]], w=[coef.b[0]])

        sq_top = [kb.alloc(f"sq{i}", [128, 512], BF16) for i in range(4)]

        stats_state = {"st": None}

        def stats_begin():
            pss = [psn(), psn()]
            idx = [ps.index(p_) for p_ in pss]
            for i_ in idx:
                ps_excl.add(i_)
            stats_state["st"] = {"pss": pss, "idx": idx, "n": 0, "done": set()}

        def stats_sq(dc, tt):
            st = stats_state["st"]
            tsl = slice(tt * 512, (tt + 1) * 512)
            s_ = sq_top[st["n"] % 4]
            st["n"] += 1
            st.setdefault("tiles", {})[(dc, tt)] = s_
            kb.emit("act", lambda: A.activation(out=s_.t[:], in_=xT.t[:, dc, tsl], func=AF.Square),
                    r=[xT.b[dc][tt]], w=[s_.b[0]])

        def stats_mm(dc, tt):
            st = stats_state["st"]
            p = st["pss"][tt]
            s_ = st["tiles"].pop((dc, tt))
            kb.emit("pe", lambda: PE.matmul(p.t[:], ones_b.t[:], s_.t[:], start=(dc == 0), stop=(dc == DC - 1)),
                    r=[s_.b[0], ones_b.b[0]], w=[p.b[0]])

        def stats_tile1(dc, tt):
            stats_sq(dc, tt)
            stats_mm(dc, tt)

        def stats_tile(dc):
            for tt in range(TT):
                stats_tile1(dc, tt)

        def sub_in_half(tt, acol, bcol, out_t, out_bufs, acol_buf, extra_w=(), col0=None):
            st = stats_state["st"]
            tsl = slice(tt * 512, (tt + 1) * 512)
            osl = tsl if col0 is None else slice(col0, col0 + 512)
            p = st["pss"][tt]
            rstd = kb.alloc("rstdh", [128, 512], F32)
            xn = [kb.alloc(f"xnh{i}", [128, 512], F32) for i in range(2)]
            kb.emit("act", lambda: A.activation(out=rstd.t[:], in_=p.t[:], func=AF.Ln, bias=eps_sb.t[:, 0:1],
                                                scale=1.0 / D), r=[p.b[0], eps_sb.b[0]], w=[rstd.b[0]])
            kb.emit("act", lambda: A.activation(out=rstd.t[:], in_=rstd.t[:], func=AF.Exp, scale=-0.5),
                    r=[rstd.b[0]], w=[rstd.b[0]])
            ps_excl.discard(st["idx"][tt])
            for dc in range(DC):
                x_ = xn[dc % 2]
                kb.emit("dve", lambda: V.scalar_tensor_tensor(out=x_.t[:], in0=xT.t[:, dc, tsl], scalar=acol(dc),
                                                              in1=rstd.t[:], op0=ALU.mult, op1=ALU.mult),
                        r=[xT.b[dc][tt], rstd.b[0], acol_buf], w=[x_.b[0]])
                if bcol is not None:
                    kb.emit("act", lambda: A.activation(out=out_t[:, dc, osl], in_=x_.t[:], func=AF.Identity,
                                                        bias=bcol(dc)),
                            r=[x_.b[0], acol_buf], w=[out_bufs[tt]] + list(extra_w))
                else:
                    kb.emit("act", lambda: A.copy(out=out_t[:, dc, osl], in_=x_.t[:]),
                            r=[x_.b[0]], w=[out_bufs[tt]] + list(extra_w))
            st["done"].add(tt)
            if len(st["done"]) == TT:
                stats_state["st"] = None

        def sub_in(acol, bcol, out_t, out_bufs, acol_buf, final=False, pre_apply=None):
            if stats_state["st"] is None:
                stats_begin()
                for dc in range(DC):
                    stats_tile(dc)
            st = stats_state["st"]
            with kb.scope():
                if 0 not in st["done"]:
                    if pre_apply is not None:
                        pre_apply()
                    sub_in_half(0, acol, bcol, out_t, out_bufs, acol_buf)
                sub_in_half(1, acol, bcol, out_t, out_bufs, acol_buf)

        def mid_apply(nxt):
            if nxt == "final":
                yv = hT.t[:].bitcast(F32)
                sub_in_half(0, lambda dc: fg_sb.t[:, dc:dc + 1], None, yv, hT.b, fg_sb.b[0], extra_w=[hT.b[1]], col0=0)
                early_out["yv"] = yv
                return
            make_coef(nxt[0], nxt[1], None)
            sub_in_half(0, lambda dc: coef.t[:, 0, dc:dc + 1], lambda dc: coef.t[:, 1, dc:dc + 1], hT.t, hT.b, coef.b[0])

        early_out = {}

        def emit_out_tc(tc, src_t, src_bufs, yb_, sem_):
            c0 = (tc % 4) * 128 if src_t is early_out.get("yv") else tc * 128
            for half in range(2):
                p = psn()
                kb.emit("pe", [(lambda q=q: PE.transpose(p.t[:, q * 128:(q + 1) * 128],
                                                         src_t[:, half * 4 + q, c0:c0 + 128], ident.t[:]))
                               for q in range(4)], r=list(src_bufs) + [ident.b[0]], w=[p.b[0]])
                if half == 0:
                    kb.emit("act", lambda: A.copy(out=yb_.t[:, 0:512], in_=p.t[:]), r=[p.b[0]], w=[yb_.b[0]])
                else:
                    kb.emit("dve", lambda: V.tensor_copy(out=yb_.t[:, 512:1024], in_=p.t[:]), r=[p.b[0]], w=[yb_.b[0]])
            kb.dma("sp", y_out[tc * 128:(tc + 1) * 128, :], yb_.t[:], sem_, r=[yb_.b[0]])

        def closing(group_fn, nxt):
            stats_begin()
            LAG = 3
            seq = [(tt, do) for tt in range(TT) for do in range(DC)]
            for gi, (tt, do) in enumerate(seq):
                pp = group_fn(tt, do)
                resid_update(pp, do, tt)
                stats_sq(do, tt)
                if gi >= LAG:
                    ptt, pdo = seq[gi - LAG]
                    stats_mm(pdo, ptt)
                if gi == DC - 1 + LAG and nxt is not None:
                    mid_apply(nxt)
                    if nxt == "final":
                        early_out["ytok"] = kb.alloc("ytokf", [128, D], F32)
                        early_out["sem"] = kb.newsem("ysemf", dma=True)
                        out_sems.append(early_out["sem"])
                if nxt == "final" and DC + LAG + 1 <= gi <= DC + LAG + 4:
                    emit_out_tc(gi - (DC + LAG + 1), early_out["yv"], [hT.b[0], hT.b[1]], early_out["ytok"], early_out["sem"])
            for gi in range(len(seq) - LAG, len(seq)):
                ptt, pdo = seq[gi]
                stats_mm(pdo, ptt)

        def sub_in_std(pre_apply=None):
            sub_in(lambda dc: coef.t[:, 0, dc:dc + 1], lambda dc: coef.t[:, 1, dc:dc + 1], hT.t, hT.b, coef.b[0],
                   pre_apply=pre_apply)

        def resid_update(p, do, tt):
            tsl = slice(tt * 512, (tt + 1) * 512)
            kb.emit("dve", lambda: V.scalar_tensor_tensor(
                out=xT.t[:, do, tsl], in0=p.t[:], scalar=coefg.t[:, do:do + 1], in1=xT.t[:, do, tsl],
                op0=ALU.mult, op1=ALU.add), r=[p.b[0], coefg.b[0]], w=[xT.b[do][tt]])

        def ffn(l, i, s, nbg=0, nxt=None):
            sub_in_std(pre_apply=lambda: make_coef(l, s, 0.5))
            with kb.scope():
                act = kb.alloc("act", [128, FC, T], BF16, nb=TT)
                sg = [kb.alloc(f"sg{k}", [128, 512], F32) for k in range(2)]
                sg_i = 0
                wd_sb = kb.alloc("wd_sb", [128, DC, FC * 128], BF16, nb=DC)
                wdv = wd_sb.t[:].rearrange("p d (fc c) -> p d fc c", fc=FC)
                wd = w_down[l, i].rearrange("(fc p) d -> p fc d", p=128)
                _dbg("ffn peak")
                chunk_tok = {}

                def load_wd(k, after=None):
                    if after is not None:
                        kb._wait("pool", [after])
                    kb.dma("pool", wdv[:, k, :, :], wd[:, :, k * 128:(k + 1) * 128], wd_sems[k], w=[wd_sb.b[k]])

                wv = w_gu[l, i].rearrange("(dc p) f -> p dc f", p=128)
                for ch in range(FC // 2):
                    j0 = ch * 2
                    sl = fetch([
                        (lambda s_: s_.rearrange("p (dc g c) -> p dc g c", dc=DC, g=2)[:, :, 0, :],
                         wv[:, :, j0 * 128:(j0 + 2) * 128]),
                        (lambda s_: s_.rearrange("p (dc g c) -> p dc g c", dc=DC, g=2)[:, :, 1, :],
                         wv[:, :, DFF + j0 * 128:DFF + (j0 + 2) * 128]),
                    ])
                    if ch >= 6:
                        load_wd(ch - 6)
                    sv = sl.t[:].rearrange("p (dc g c) -> p dc g c", dc=DC, g=2)
                    def emit_part(ch_, sl_, sv_, j0_, jt_list):
                        nonlocal sg_i
                        for (jj, tt) in jt_list:
                            j = j0_ + jj
                            pg_, pu_ = psn(), psn()
                            for g_, pp in ((0, pg_), (1, pu_)):
                                chunk_tok[ch_] = kb.emit("pe", [(lambda dc=dc: PE.matmul(
                                    pp.t[:], sv_[:, dc, g_, jj * 128:(jj + 1) * 128],
                                    hT.t[:, dc, tt * 512:(tt + 1) * 512], start=(dc == 0), stop=(dc == DC - 1)))
                                    for dc in range(DC)], r=[sl_.b[0], hT.b[tt]], w=[pp.b[0]])
                            s_ = sg[sg_i % 2]
                            sg_i += 1
                            kb.emit("act", lambda: A.activation(out=s_.t[:], in_=pg_.t[:], func=AF.Silu),
                                    r=[pg_.b[0]], w=[s_.b[0]])
                            kb.emit("dve", lambda: V.tensor_tensor(out=act.t[:, j, tt * 512:(tt + 1) * 512], in0=s_.t[:],
                                                                   in1=pu_.t[:], op=ALU.mult),
                                    r=[s_.b[0], pu_.b[0]], w=[act.b[tt]])

                    if ch == 0:
                        emit_part(ch, sl, sv, j0, [(0, 0), (1, 0)])
                        held = (ch, sl, sv, j0)
                    elif ch == 1:
                        emit_part(ch, sl, sv, j0, [(0, 0), (1, 0)])
                        emit_part(*held, [(0, 1), (1, 1)])
                        emit_part(ch, sl, sv, j0, [(0, 1), (1, 1)])
                    else:
                        emit_part(ch, sl, sv, j0, [(0, 0), (0, 1), (1, 0), (1, 1)])
                    if ch < nbg:
                        run_bg(1)
                for k in (5, 6, 7):
                    load_wd(k, after=chunk_tok[k + 2])
                make_coef_g(l, s, 0.5)

                def down_group(tt, do):
                    pp = psn()
                    kb.emit("pe", [(lambda fc=fc: PE.matmul(
                        pp.t[:], wdv[:, do, fc, :], act.t[:, fc, tt * 512:(tt + 1) * 512],
                        start=(fc == 0), stop=(fc == FC - 1))) for fc in range(FC)],
                        r=[wd_sb.b[do], act.b[tt]], w=[pp.b[0]])
                    return pp
                closing(down_group, nxt)

        def lru(l, lru_nxt=None):
            sub_in_std(pre_apply=lambda: make_coef(l, 1, 1.0))
            with kb.scope():
                yT = kb.alloc("yT", [128, DC, T], BF16, nb=TT)
                st_sb = kb.alloc("st_sb", [128, DC, 8], F32)
                NB = 2
                NB3 = 3
                xb = [kb.alloc(f"xb{k}", [128, T], F32) for k in range(NB3)]
                xc = [kb.alloc(f"xc{k}", [128, T], F32) for k in range(NB)]
                xcb = [kb.alloc(f"xcb{k}", [128, T], BF16) for k in range(NB)]
                gy = [kb.alloc(f"gy{k}", [128, T], F32) for k in range(NB3)]
                ra = [[kb.alloc(f"ra{k}{d}", [128, T], F32) for d in range(2)] for k in range(NB)]
                iu = [[kb.alloc(f"iu{k}{d}", [128, T], F32) for d in range(2)] for k in range(NB)]
                tm1 = [kb.alloc(f"tm{k}", [128, T], F32) for k in range(NB)]
                _dbg("lru peak")
                win = lru_w_in[0].rearrange("(dc p) f -> p dc f", p=128)

                def fetch_win(dc):
                    return fetch([
                        (lambda s_: s_[:, 0:2048].rearrange("p (dc g c) -> p dc g c", dc=DC, g=2)[:, :, 0, :],
                         win[:, :, dc * 128:(dc + 1) * 128]),
                        (lambda s_: s_[:, 0:2048].rearrange("p (dc g c) -> p dc g c", dc=DC, g=2)[:, :, 1, :],
                         win[:, :, D + dc * 128:D + (dc + 1) * 128]),
                    ])
                wsl = {0: fetch_win(0)}
                bg_sched = [2, 2, 2, 2, 2, 2, 1, 1]

                def stage_a(dc):
                    k = dc % NB
                    if dc + 1 < DC:
                        wsl[dc + 1] = fetch_win(dc + 1)
                    sl = wsl.pop(dc)
                    sv = sl.t[:, 0:2048].rearrange("p (dc g c) -> p dc g c", dc=DC, g=2)
                    pxb = [psn(), psn()]
                    pyb = [psn(), psn()]
                    for tt in range(TT):
                        for g_, pp in ((0, pxb), (1, pyb)):
                            kb.emit("pe", [(lambda d2=d2: PE.matmul(
                                pp[tt].t[:], sv[:, d2, g_, :], hT.t[:, d2, tt * 512:(tt + 1) * 512],
                                start=(d2 == 0), stop=(d2 == DC - 1))) for d2 in range(DC)],
                                r=[sl.b[0], hT.b[tt]], w=[pp[tt].b[0]])
                    XB, XC, XCB, GY = xb[dc % NB3], xc[k], xcb[k], gy[dc % NB3]
                    for tt in range(TT):
                        tsl = slice(tt * 512, (tt + 1) * 512)
                        kb.emit("act", lambda: A.copy(out=XB.t[:, tsl], in_=pxb[tt].t[:]), r=[pxb[tt].b[0]], w=[XB.b[0]])
                    for tt in range(TT):
                        tsl = slice(tt * 512, (tt + 1) * 512)
                        kb.emit("act", lambda: A.activation(out=GY.t[:, tsl], in_=pyb[tt].t[:], func=AF.Gelu_apprx_tanh),
                                r=[pyb[tt].b[0]], w=[GY.b[0]])
                    cw = lambda kk: convw_sb.t[:, kk, dc:dc + 1]
                    lwk = lambda kk: lw.t[:, kk, dc:dc + 1]
                    xb3 = XB.t[:].rearrange("p (s c) -> p s c", s=NSEG)
                    xc3 = XC.t[:].rearrange("p (s c) -> p s c", s=NSEG)
                    kb.emit("dve", lambda: V.tensor_scalar(out=XC.t[:], in0=XB.t[:], scalar1=cw(1), scalar2=convb_sb.t[:, dc:dc + 1],
                                                           op0=ALU.mult, op1=ALU.add),
                            r=[XB.b[0], convw_sb.b[0], convb_sb.b[0]], w=[XC.b[0]])

                    def mac(o, i_, sc_):
                        kb.emit("dve", lambda: V.scalar_tensor_tensor(out=o, in0=i_, scalar=sc_, in1=o,
                                                                      op0=ALU.mult, op1=ALU.add),
                                r=[XB.b[0], lw.b[0]], w=[XC.b[0]])
                    mac(xc3[:, :, 1:SEG], xb3[:, :, 0:SEG - 1], cw(0))
                    mac(xc3[:, :, 0:SEG - 1], xb3[:, :, 1:SEG], cw(2))
                    mac(xc3[:, :, 0:SEG - 2], xb3[:, :, 2:SEG], cw(3))
                    mac(xc3[:, 1:NSEG, 0], xb3[:, 0:NSEG - 1, SEG - 1], lwk(0))
                    mac(xc3[:, 0:NSEG - 1, SEG - 1], xb3[:, 1:NSEG, 0], lwk(2))
                    mac(xc3[:, 0:NSEG - 1, SEG - 1], xb3[:, 1:NSEG, 1], lwk(3))
                    mac(xc3[:, 0:NSEG - 1, SEG - 2], xb3[:, 1:NSEG, 0], lwk(3))

                def stage_a2(dc):
                    k = dc % NB
                    XB, XC, XCB, GY = xb[dc % NB3], xc[k], xcb[k], gy[dc % NB3]
                    kb.emit("act", lambda: A.copy(out=XCB.t[:], in_=XC.t[:]), r=[XC.b[0]], w=[XCB.b[0]])
                    pgs = {}
                    for d_ in range(2):
                        for gi in range(2):
                            pp = [psn(), psn()]
                            pgs[(d_, gi)] = pp
                            ab = d_ * 2 + gi
                            for tt in range(TT):
                                kb.emit("pe", lambda: PE.matmul(pp[tt].t[:], gw.t[:, ab, dc, :],
                                                                 XCB.t[:, tt * 512:(tt + 1) * 512], start=True, stop=True),
                                        r=[gw.b[0], XCB.b[0]], w=[pp[tt].b[0]])
                        for gi, dstt in ((0, ra[k][d_]), (1, iu[k][d_])):
                            ab = d_ * 2 + gi
                            pp = pgs[(d_, gi)]
                            for tt in range(TT):
                                tsl = slice(tt * 512, (tt + 1) * 512)
                                kb.emit("act", lambda: A.activation(out=dstt.t[:, tsl], in_=pp[tt].t[:], func=AF.Tanh,
                                                                    bias=hgb.t[:, ab, dc:dc + 1], scale=0.5),
                                        r=[pp[tt].b[0], hgb.b[0]], w=[dstt.b[0]])
                    run_bg(bg_sched[dc])
                    for d_ in range(2):
                        RA = ra[k][d_]
                        kb.emit("act", lambda: A.activation(out=RA.t[:], in_=RA.t[:], func=AF.Exp,
                                                            bias=Lc.t[:, d_, dc:dc + 1], scale=Lc.t[:, d_, dc:dc + 1]),
                                r=[Lc.b[0]], w=[RA.b[0]])
                    for d_ in range(2):
                        RA = ra[k][d_]
                        TM = XB if d_ == 0 else tm1[k]
                        kb.emit("act", lambda: A.activation(out=TM.t[:], in_=RA.t[:], func=AF.Square),
                                r=[RA.b[0], XC.b[0]], w=[TM.b[0]])
                    for d_ in range(2):
                        TM = XB if d_ == 0 else tm1[k]
                        kb.emit("act", lambda: A.activation(out=TM.t[:], in_=TM.t[:], func=AF.Sqrt, bias=qh.t[:, 0:1], scale=-0.25),
                                r=[qh.b[0]], w=[TM.b[0]])

                def stage_b(dc):
                    k = dc % NB
                    XB, XC, GY = xb[dc % NB3], xc[k], gy[dc % NB3]
                    for d_ in range(2):
                        RA, IU = ra[k][d_], iu[k][d_]
                        TM = XB if d_ == 0 else tm1[k]
                        kb.emit("dve", lambda: V.tensor_tensor(out=TM.t[:], in0=TM.t[:], in1=XC.t[:], op=ALU.mult),
                                r=[XC.b[0]], w=[TM.b[0]])
                        kb.emit("dve", lambda: V.scalar_tensor_tensor(out=IU.t[:], in0=IU.t[:], scalar=1.0, in1=TM.t[:],
                                                                      op0=ALU.add, op1=ALU.mult),
                                r=[TM.b[0]], w=[IU.b[0]])
                        ra3 = RA.t[:].rearrange("p (s c) -> p s c", s=NSEG)
                        iu3 = IU.t[:].rearrange("p (s c) -> p s c", s=NSEG)
                        if d_ == 0:
                            kb.emit("dve", lambda: V.tensor_scalar_mul(out=ra3[:, 1:NSEG, 0], in0=ra3[:, 1:NSEG, 0],
                                                                        scalar1=link_sb.t[:, 0:1]),
                                    r=[link_sb.b[0]], w=[RA.b[0]])
                            kb.emit("dve", lambda: V.tensor_tensor_scan(out=IU.t[:], data0=RA.t[:], data1=IU.t[:],
                                                                        initial=h0_sb.t[:, 0, dc:dc + 1],
                                                                        op0=ALU.mult, op1=ALU.add),
                                    r=[RA.b[0], h0_sb.b[0]], w=[IU.b[0]])
                            kb.emit("dve", lambda: V.tensor_copy(
                                out=st_sb.t[:, dc, :].rearrange("p (s r) -> p s r", r=2)[:, :, 0],
                                in_=iu3[:, :, SEG - 1]), r=[IU.b[0]], w=[st_sb.b[0]])
                        else:
                            kb.emit("dve", lambda: V.tensor_scalar_mul(out=ra3[:, 0:NSEG - 1, SEG - 1],
                                                                        in0=ra3[:, 0:NSEG - 1, SEG - 1],
                                                                        scalar1=link_sb.t[:, 0:1]),
                                    r=[link_sb.b[0]], w=[RA.b[0]])
                            kb.emit("dve", lambda: V.tensor_tensor_scan(out=IU.t[:, ::-1], data0=RA.t[:, ::-1],
                                                                        data1=IU.t[:, ::-1],
                                                                        initial=h0_sb.t[:, 1, dc:dc + 1],
                                                                        op0=ALU.mult, op1=ALU.add),
                                    r=[RA.b[0], h0_sb.b[0]], w=[IU.b[0]])
                            kb.emit("dve", lambda: V.tensor_copy(
                                out=st_sb.t[:, dc, :].rearrange("p (s r) -> p s r", r=2)[:, :, 1],
                                in_=iu3[:, :, 0]), r=[IU.b[0]], w=[st_sb.b[0]])
                    kb.emit("dve", lambda: V.tensor_tensor(out=iu[k][0].t[:], in0=iu[k][0].t[:], in1=iu[k][1].t[:], op=ALU.add),
                            r=[iu[k][1].b[0]], w=[iu[k][0].b[0]])
                    for tt in range(TT):
                        tsl = slice(tt * 512, (tt + 1) * 512)
                        kb.emit("dve", lambda: V.tensor_tensor(out=yT.t[:, dc, tsl], in0=iu[k][0].t[:, tsl], in1=GY.t[:, tsl],
                                                               op=ALU.mult), r=[iu[k][0].b[0], GY.b[0]], w=[yT.b[tt]])

                stage_a(0)
                for dc in range(DC):
                    if dc + 1 < DC:
                        stage_a(dc + 1)
                    stage_a2(dc)
                    stage_b(dc)
                ssem = kb.newsem("stsem", dma=True)
                st_tok = xb[2]
                p = psn()
                kb.emit("pe", [(lambda dc=dc: PE.transpose(p.t[0:8, dc * 128:(dc + 1) * 128], st_sb.t[:, dc, :], ident.t[:]))
                               for dc in range(4)], r=[st_sb.b[0], ident.b[0]], w=[p.b[0]])
                kb.emit("dve", lambda: V.tensor_copy(out=st_tok.t[0:8, 0:512], in_=p.t[0:8, :]), r=[p.b[0]], w=[st_tok.b[0]])
                p = psn()
                kb.emit("pe", [(lambda dc=dc: PE.transpose(p.t[0:8, (dc - 4) * 128:(dc - 3) * 128], st_sb.t[:, dc, :], ident.t[:]))
                               for dc in range(4, 8)], r=[st_sb.b[0], ident.b[0]], w=[p.b[0]])
                kb.emit("dve", lambda: V.tensor_copy(out=st_tok.t[0:8, 512:1024], in_=p.t[0:8, :]), r=[p.b[0]], w=[st_tok.b[0]])
                kb.dma("sp", st_out, st_tok.t[0:8, :], ssem, r=[st_tok.b[0]])
                out_sems.append(ssem)
                make_coef_g(l, 1, 1.0)
                wo = lru_w_out[0].rearrange("(dc p) f -> p dc f", p=128)
                wo_sl = [fetch([(lambda s_: s_.rearrange("p (dc f) -> p dc f", dc=DC), wo[:, :, ch * 512:(ch + 1) * 512])])
                         for ch in range(2)]
                for w_ in wo_sl:
                    pinned.add(w_.idx)

                def wout_group(tt, do):
                    sl = wo_sl[do // 4]
                    d4 = do % 4
                    sv = sl.t[:].rearrange("p (dc f) -> p dc f", dc=DC)
                    pp = psn()
                    kb.emit("pe", [(lambda d2=d2: PE.matmul(
                        pp.t[:], sv[:, d2, d4 * 128:(d4 + 1) * 128], yT.t[:, d2, tt * 512:(tt + 1) * 512],
                        start=(d2 == 0), stop=(d2 == DC - 1))) for d2 in range(DC)],
                        r=[sl.b[0], yT.b[tt]], w=[pp.b[0]])
                    return pp
                closing(wout_group, lru_nxt)
                pinned.clear()

        out_sems = []

        def _dbg(tag):
            import os
            if os.environ.get("KDEBUG"):
                print("SBUF remaining", tag, nc.sbuf_bytes_remaining)
        kb_tmp = None

        def attention(l, att_nxt=None):
            wq = att_w_qkv[0].rearrange("(dc p) f -> p dc f", p=128)
            wsl = []
            for n in range(3):
                wsl.append(fetch([(lambda s_: s_.rearrange("p (dc f) -> p dc f", dc=DC), wq[:, :, n * 512:(n + 1) * 512])]))
            for w_ in wsl:
                pinned.add(w_.idx)
            sub_in_std(pre_apply=lambda: make_coef(l, 1, 1.0))
            with kb.scope():
                asem = kb.newsem("attnconst", dma=True)
                asem_p = kb.newsem("attnconst_p", dma=True)

                def aload(name, shape, src, dtype=F32, q="sp"):
                    t = kb.alloc(name, shape, dtype)
                    kb.dma(q, t.t[:], src, asem if q == "sp" else asem_p, w=[t.b[0]])
                    return t
                qg_sb = aload("qg", [128, HD], att_q_g[0].partition_broadcast(128))
                kg_sb = aload("kg", [128, HD], att_k_g[0].partition_broadcast(128))
                cos_sb = aload("cos", [128, 8, 64], cosin.rearrange("(c p) f -> p c f", p=128))
                sin_sb = aload("sin", [128, 8, 64], sinin.rearrange("(c p) f -> p c f", p=128))
                mask_sb = aload("maskadd", [128, KC * 4], maskin)
                for t_ in (qg_sb, kg_sb, cos_sb, sin_sb, mask_sb):
                    t_.b[0].w = (asem, asem.count)
                qT = kb.alloc("qT", [128, NH, T], BF16)
                kT = kb.alloc("kT", [128, NKV, KC * 128], BF16)
                Vt = kb.alloc("Vt", [128, KC, NKV * HD], BF16)
                oT = hT
                nbias = kb.alloc("nbias", [128, 1], F32)
                vsem = kb.newsem("cvsem", dma=True)
                kb.dma("pool", Vt.t[:, 0:4, :], cvin.rearrange("(c p) f -> p c f", p=128), vsem, w=[Vt.b[0]])
                ck_sb = kb.alloc("ck_sb", [128, 4, NKV * HD], F32)
                cksem = kb.newsem("cksem", dma=True)
                kb.dma("sp", ck_sb.t[:], ckin.rearrange("(c p) f -> p c f", p=128), cksem, w=[ck_sb.b[0]])
                NQ = 3
                qkv = [kb.alloc(f"qkv{k}", [128, 1536], F32) for k in range(NQ)]
                ssq = [kb.alloc(f"ssq{k}", [128, 10], F32) for k in range(NQ)]
                qr = [kb.alloc(f"qr{k}", [128, 1280], F32) for k in range(NQ)]
                rt1 = kb.alloc("rt", [128, 10, 2, 32], F32)
                rt = [rt1] * NQ
                kvst = [kb.alloc(f"kvst{k}", [128, 512], F32) for k in range(2)]
                kvsem = [kb.newsem(f"kvsem{k}", dma=True) for k in range(2)]
                out_sems.extend(kvsem)

                def front(tc):
                    k = tc % NQ
                    Q, SS, QR = qkv[k], ssq[k], qr[k]
                    for n in range(3):
                        sv = wsl[n].t[:].rearrange("p (dc f) -> p dc f", dc=DC)
                        p = psn()
                        kb.emit("pe", [(lambda dc=dc: PE.matmul(p.t[:], hT.t[:, dc, tc * 128:(tc + 1) * 128], sv[:, dc, :],
                                                               start=(dc == 0), stop=(dc == DC - 1))) for dc in range(DC)],
                                r=[wsl[n].b[0], hT.b[tc // 4]], w=[p.b[0]])
                        kb.emit("act", lambda: A.copy(out=Q.t[:, n * 512:(n + 1) * 512], in_=p.t[:]), r=[p.b[0]], w=[Q.b[0]])
                    for h_ in range(10):
                        kb.emit("act", lambda: A.activation(out=QR.t[:, h_ * HD:(h_ + 1) * HD], in_=Q.t[:, h_ * HD:(h_ + 1) * HD],
                                                            func=AF.Square, accum_out=SS.t[:, h_:h_ + 1]),
                                r=[Q.b[0]], w=[QR.b[0], SS.b[0]])
                    kb.emit("act", lambda: A.activation(out=SS.t[:], in_=SS.t[:], func=AF.Sqrt, bias=eps_sb.t[:, 0:1], scale=1.0 / HD),
                            r=[eps_sb.b[0]], w=[SS.b[0]])

                def mid(tc):
                    k = tc % NQ
                    Q, SS, QR, RT = qkv[k], ssq[k], qr[k], rt[k]
                    kb.emit("dve", lambda: V.reciprocal(out=SS.t[:], in_=SS.t[:]), r=[], w=[SS.b[0]])
                    q3 = Q.t[:, 0:1280].rearrange("p (a h) -> p a h", h=HD)
                    kb.emit("dve", lambda: V.tensor_tensor(out=q3, in0=q3, in1=SS.t[:].unsqueeze(2).broadcast_to([128, 10, HD]),
                                                           op=ALU.mult), r=[SS.b[0]], w=[Q.b[0]])
                    kb.emit("dve", lambda: V.tensor_tensor(out=q3[:, 0:8, :], in0=q3[:, 0:8, :],
                                                           in1=qg_sb.t[:].unsqueeze(1).broadcast_to([128, 8, HD]), op=ALU.mult),
                            r=[qg_sb.b[0]], w=[Q.b[0]])
                    kb.emit("dve", lambda: V.tensor_tensor(out=q3[:, 8:10, :], in0=q3[:, 8:10, :],
                                                           in1=kg_sb.t[:].unsqueeze(1).broadcast_to([128, 2, HD]), op=ALU.mult),
                            r=[kg_sb.b[0]], w=[Q.b[0]])
                    q5 = Q.t[:, 0:1280].rearrange("p (a x h f) -> p a x h f", x=2, h=2, f=32)
                    o5 = QR.t[:].rearrange("p (a x h f) -> p a x h f", x=2, h=2, f=32)
                    x0, x1 = q5[:, :, :, 0, :], q5[:, :, :, 1, :]
                    o0, o1 = o5[:, :, :, 0, :], o5[:, :, :, 1, :]
                    cs = cos_sb.t[:, tc, :].rearrange("p (x f) -> p x f", x=2).unsqueeze(1).broadcast_to([128, 10, 2, 32])
                    sn = sin_sb.t[:, tc, :].rearrange("p (x f) -> p x f", x=2).unsqueeze(1).broadcast_to([128, 10, 2, 32])
                    cdeps = [Q.b[0], cos_sb.b[0], sin_sb.b[0]]
                    kb.emit("dve", lambda: V.tensor_tensor(out=RT.t[:], in0=x1, in1=sn, op=ALU.mult), r=cdeps, w=[RT.b[0]])
                    kb.emit("dve", lambda: V.tensor_tensor(out=o0, in0=x0, in1=cs, op=ALU.mult), r=cdeps, w=[QR.b[0]])
                    kb.emit("dve", lambda: V.tensor_tensor(out=o0, in0=o0, in1=RT.t[:], op=ALU.subtract), r=[RT.b[0]], w=[QR.b[0]])
                    kb.emit("dve", lambda: V.tensor_tensor(out=RT.t[:], in0=x0, in1=sn, op=ALU.mult), r=cdeps, w=[RT.b[0]])
                    kb.emit("dve", lambda: V.tensor_tensor(out=o1, in0=x1, in1=cs, op=ALU.mult), r=cdeps, w=[QR.b[0]])
                    kb.emit("dve", lambda: V.tensor_tensor(out=o1, in0=o1, in1=RT.t[:], op=ALU.add), r=[RT.b[0]], w=[QR.b[0]])

                def back(tc):
                    k = tc % NQ
                    Q, QR, KV = qkv[k], qr[k], kvst[tc % 2]
                    kb.emit("act", lambda: A.copy(out=KV.t[:], in_=Q.t[:, 1024:1536]), r=[Q.b[0]], w=[KV.b[0]])
                    kb.dma("sp", nk_out[tc * 128:(tc + 1) * 128, :], KV.t[:, 0:256], kvsem[tc % 2], r=[KV.b[0]])
                    kb.dma("sp", nv_out[tc * 128:(tc + 1) * 128, :], KV.t[:, 256:512], kvsem[tc % 2], r=[KV.b[0]])
                    kb.emit("act", lambda: A.copy(out=Vt.t[:, 4 + tc, :], in_=Q.t[:, 1280:1536]), r=[Q.b[0]], w=[Vt.b[0]])
                    for grp in range(3):
                        nblk = 4 if grp < 2 else 2
                        p = psn()
                        kb.emit("pe", [(lambda b_=b_: PE.transpose(p.t[:, b_ * 128:(b_ + 1) * 128],
                                                                   QR.t[:, (grp * 4 + b_) * 128:(grp * 4 + b_ + 1) * 128], ident.t[:]))
                                       for b_ in range(nblk)], r=[QR.b[0], ident.b[0]], w=[p.b[0]])
                        if grp < 2:
                            kb.emit("act", lambda: A.copy(out=qT.t[:, grp * 4:(grp + 1) * 4, tc * 128:(tc + 1) * 128],
                                                          in_=p.t[:].rearrange("p (a c) -> p a c", a=4)),
                                    r=[p.b[0]], w=[qT.b[0]])
                        else:
                            kb.emit("act", lambda: A.copy(out=kT.t[:, :, 512 + tc * 128:512 + (tc + 1) * 128],
                                                          in_=p.t[:, 0:256].rearrange("p (a c) -> p a c", a=2)),
                                    r=[p.b[0]], w=[kT.b[0]])

                ps_excl.add(7)
                front(0)
                front(1)
                nmod = 0
                for tc in range(8):
                    if tc + 2 < 8:
                        front(tc + 2)
                    if tc < 6 and (1, 12 + tc) in mod_pending:
                        mod_chunk(1, 12 + tc, bank=ps[7], col0=tc * 4, do_add=False)
                        nmod += 1
                    mid(tc)
                    back(tc)
                if nmod:
                    assert nmod == 6
                    kb.emit("dve", lambda: V.tensor_tensor(out=modv.t[:, 1, 48:72], in0=ps[7].t[:, 0:24],
                                                           in1=modb_sb[1].t[:, 48:72], op=ALU.add),
                            r=[ps[7].b[0], cpack.b[0]], w=[modv.b[5]])
                ps_excl.discard(7)
                pinned.clear()
                for g_ in range(NKV):
                    p = psn()
                    kb.emit("pe", [(lambda c=c: PE.transpose(p.t[:, c * 128:(c + 1) * 128],
                                                             ck_sb.t[:, c, g_ * HD:(g_ + 1) * HD], ident.t[:]))
                                   for c in range(4)], r=[ck_sb.b[0], ident.b[0]], w=[p.b[0]])
                    kb.emit("act", lambda: A.copy(out=kT.t[:, g_, 0:512], in_=p.t[:]), r=[p.b[0]], w=[kT.b[0]])
                s1 = kb.alloc("sb1", [128, 4 * NKV * HD], F32)
                s2 = kb.alloc("sb2", [128, 8], F32)
                m1 = kb.alloc("sbm1", [128, 4], F32)
                r1 = kb.alloc("sbr1", [1, 4], F32)
                ckf = ck_sb.t[:].rearrange("p c f -> p (c f)")
                kb.emit("dve", lambda: V.tensor_tensor(out=s1.t[:], in0=ckf, in1=ckf, op=ALU.mult), r=[ck_sb.b[0]], w=[s1.b[0]])
                kb.emit("dve", lambda: V.tensor_reduce(out=s2.t[:], in_=s1.t[:].rearrange("p (a h) -> p a h", h=HD),
                                                       axis=AX.X, op=ALU.add), r=[s1.b[0]], w=[s2.b[0]])
                kb.emit("dve", lambda: V.tensor_reduce(out=m1.t[:, 0:1], in_=s2.t[:], axis=AX.X, op=ALU.max),
                        r=[s2.b[0]], w=[m1.b[0]])
                p = psn()
                kb.emit("pe", lambda: PE.transpose(p.t[0:1, 0:128], m1.t[:, 0:1], ident.t[:]), r=[m1.b[0], ident.b[0]], w=[p.b[0]])
                kb.emit("dve", lambda: V.tensor_reduce(out=r1.t[:, 0:1], in_=p.t[0:1, 0:128], axis=AX.X, op=ALU.max),
                        r=[p.b[0]], w=[r1.b[0]])
                p2 = psn()
                kb.emit("pe", lambda: PE.matmul(p2.t[:, 0:1], ones_f.t[0:1, :], r1.t[0:1, 0:1], start=True, stop=True),
                        r=[r1.b[0], ones_f.b[0]], w=[p2.b[0]])
                kb.emit("dve", lambda: V.tensor_tensor(out=s1.t[:, 0:HD], in0=kg_sb.t[:], in1=kg_sb.t[:], op=ALU.mult),
                        r=[kg_sb.b[0]], w=[s1.b[0]])
                kb.emit("dve", lambda: V.tensor_reduce(out=m1.t[:, 1:2], in_=s1.t[:, 0:HD], axis=AX.X, op=ALU.max),
                        r=[s1.b[0]], w=[m1.b[0]])
                kb.emit("dve", lambda: V.tensor_tensor(out=s1.t[:, 0:HD], in0=qg_sb.t[:], in1=qg_sb.t[:], op=ALU.mult),
                        r=[qg_sb.b[0]], w=[s1.b[0]])
                kb.emit("dve", lambda: V.tensor_reduce(out=m1.t[:, 2:3], in_=s1.t[:, 0:HD], axis=AX.X, op=ALU.max),
                        r=[s1.b[0]], w=[m1.b[0]])
                kb.emit("dve", lambda: V.scalar_tensor_tensor(out=m1.t[:, 3:4], in0=m1.t[:, 1:2], scalar=float(HD), in1=p2.t[:, 0:1],
                                                              op0=ALU.mult, op1=ALU.max), r=[p2.b[0]], w=[m1.b[0]])
                kb.emit("dve", lambda: V.scalar_tensor_tensor(out=m1.t[:, 3:4], in0=m1.t[:, 2:3], scalar=float(HD), in1=m1.t[:, 3:4],
                                                              op0=ALU.mult, op1=ALU.mult), r=[], w=[m1.b[0]])
                kb.emit("act", lambda: A.activation(out=nbias.t[:], in_=m1.t[:, 3:4], func=AF.Sqrt), r=[m1.b[0]], w=[nbias.b[0]])
                kb.emit("dve", lambda: V.tensor_scalar_mul(out=nbias.t[:], in0=nbias.t[:], scalar1=-SCALE), r=[], w=[nbias.b[0]])

                mb = kb.alloc("maskb", [128, KC * 4], F32)
                kb.emit("dve", lambda: V.tensor_scalar_add(out=mb.t[:], in0=mask_sb.t[:], scalar1=nbias.t[:, 0:1]),
                        r=[mask_sb.b[0], nbias.b[0]], w=[mb.b[0]])
                PT = [kb.alloc(f"PT{k}", [128, 512], BF16) for k in range(4)]
                scb = [ps[0], ps[1], ps[2], ps[7]]
                rec = [kb.alloc(f"rec{k}", [128, 512], F32) for k in range(2)]
                _dbg("attn peak")
                items = [(g_, hp, sg_, kc) for g_ in range(NKV) for hp in range(2) for sg_ in range(NSEG) for kc in range(KC)]
                sc_ps = {}
                acc = {}

                def emit_qk(n):
                    g_, hp, sg_, kc = items[n]
                    h0_ = g_ * 4 + hp * 2
                    p = scb[n % 4]
                    sc_ps[n] = p
                    kb.emit("pe", lambda: PE.matmul(p.t[:].rearrange("p (j c) -> p j c", j=2), kT.t[:, g_, kc * 128:(kc + 1) * 128],
                                                     qT.t[:, h0_:h0_ + 2, sg_ * SEG:(sg_ + 1) * SEG], start=True, stop=True),
                            r=[kT.b[0], qT.b[0]], w=[p.b[0]])

                def emit_pv(n):
                    g_, hp, sg_, kc = items[n]
                    h0_ = g_ * 4 + hp * 2
                    p = sc_ps.pop(n)
                    pt = PT[n % 4]
                    col = kc * 4 + sg_
                    kb.emit("act", lambda: A.activation(out=pt.t[:], in_=p.t[:], func=AF.Exp,
                                                        bias=mb.t[:, col:col + 1], scale=SCALE),
                            r=[p.b[0], mb.b[0]], w=[pt.b[0]])
                    if kc == 0:
                        acc[(g_, hp, sg_)] = ((ps[3], ps[4]), (ps[5], ps[6]))[(n // KC) % 2]
                    po, psum_ = acc[(g_, hp, sg_)]
                    kb.emit("pe", [
                        lambda: PE.matmul(po.t[:], Vt.t[:, kc, g_ * HD:(g_ + 1) * HD], pt.t[:], start=(kc == 0), stop=(kc == KC - 1)),
                        lambda: PE.matmul(psum_.t[:], ones_b.t[:], pt.t[:], start=(kc == 0), stop=(kc == KC - 1))],
                        r=[Vt.b[0], pt.b[0], ones_b.b[0]], w=[po.b[0], psum_.b[0]])
                    if kc == KC - 1:
                        rc = rec[(n // KC) % 2]
                        kb.emit("dve", lambda: V.reciprocal(out=rc.t[:], in_=psum_.t[:]), r=[psum_.b[0]], w=[rc.b[0]])
                        kb.emit("dve", lambda: V.tensor_tensor(
                            out=oT.t[:, h0_:h0_ + 2, sg_ * SEG:(sg_ + 1) * SEG],
                            in0=po.t[:].rearrange("p (j c) -> p j c", j=2), in1=rc.t[:].rearrange("p (j c) -> p j c", j=2),
                            op=ALU.mult), r=[po.b[0], rc.b[0]], w=[oT.b[sg_ // 2]])
                        del acc[(g_, hp, sg_)]

                emit_qk(0)
                emit_qk(1)
                for n in range(len(items)):
                    if n + 2 < len(items):
                        emit_qk(n + 2)
                    emit_pv(n)
                make_coef_g(l, 1, 1.0)
                wo = att_w_o[0].rearrange("(h p) f -> p h f", p=128)
                wo_sl = [fetch([(lambda s_: s_.rearrange("p (h f) -> p h f", h=NH), wo[:, :, ch * 512:(ch + 1) * 512])])
                         for ch in range(2)]
                for w_ in wo_sl:
                    pinned.add(w_.idx)

                def wo_group(tt, do):
                    sl = wo_sl[do // 4]
                    d4 = do % 4
                    sv = sl.t[:].rearrange("p (h f) -> p h f", h=NH)
                    pp = psn()
                    kb.emit("pe", [(lambda h_=h_: PE.matmul(
                        pp.t[:], sv[:, h_, d4 * 128:(d4 + 1) * 128], oT.t[:, h_, tt * 512:(tt + 1) * 512],
                        start=(h_ == 0), stop=(h_ == NH - 1))) for h_ in range(NH)],
                        r=[sl.b[0], oT.b[tt]], w=[pp.b[0]])
                    return pp
                closing(wo_group, att_nxt)
                pinned.clear()

        stage = [0]

        def stop_here():
            stage[0] += 1
            return DEBUG_STOP is not None and stage[0] > DEBUG_STOP

        def body():
            nonlocal kb_tmp
            for l in range(2):
                if stop_here():
                    return
                ffn(l, 0, 0, nbg=(6 if l == 0 else 0), nxt=(l, 1))
                if stop_here():
                    return
                if l == 0:
                    lru(l, lru_nxt=(l, 2))
                else:
                    attention(l, att_nxt=(l, 2))
                if stop_here():
                    return
                ffn(l, 1, 2, nbg=(6 if l == 0 else 0), nxt=((l + 1, 0) if l == 0 else ("final" if DEBUG_STOP is None else None)))
                if stop_here():
                    return

        body()

        with kb.scope():
            yT_ = kb.alloc("yfin", [128, DC, T], F32, nb=TT)
            if DEBUG_STOP is None:
                sub_in(lambda dc: fg_sb.t[:, dc:dc + 1], None, yT_.t, yT_.b, fg_sb.b[0])
            else:
                for tt in range(TT):
                    for dc in range(DC):
                        kb.emit("dve", lambda: V.tensor_copy(out=yT_.t[:, dc, tt * 512:(tt + 1) * 512],
                                                             in_=xT.t[:, dc, tt * 512:(tt + 1) * 512]),
                                r=[xT.b[dc][tt]], w=[yT_.b[tt]])
            ytok = [kb.alloc(f"ytok{i}", [128, D], F32) for i in range(2)]
            ysem = [kb.newsem(f"ysem{i}", dma=True) for i in range(2)]
            out_sems.extend(ysem)
            for tc in range(4 if "yv" in early_out else 0, 8):
                emit_out_tc(tc, yT_.t, [yT_.b[tc // 4]], ytok[tc % 2], ysem[tc % 2])
            for s in out_sems:
                if s.count > 0:
                    nc.sync.wait_ge(s.h, s.count)
    return nc


_NC_CACHE = {}

PROMPT_ASSIGN = [[0, 1, 2], [3, 4, 5], [6, 7, 8], [9, 10, 11], [12, 13], [14, 15]]


def _rope_tables():
    t = np.arange(1024)
    r_idx = (t // 64).astype(np.float32)
    c_idx = (t % 64).astype(np.float32)
    n_freq = 32
    inv = (np.float32(10000.0) ** (-np.arange(n_freq, dtype=np.float32) / np.float32(n_freq))).astype(np.float32)
    ang = np.stack([r_idx[:, None] * inv, c_idx[:, None] * inv], axis=1).astype(np.float32)
    return np.cos(ang).reshape(1024, 64).astype(np.float32), np.sin(ang).reshape(1024, 64).astype(np.float32)


def kernel(x_prompt, x_sample, c, state_lru, cache_k, cache_v, c_ctx, mod_w, mod_b, norm_g,
           ffn_w_gu, ffn_w_down, lru_w_in, lru_conv_w, lru_conv_b, lru_gate_w, lru_gate_b,
           lru_lambda, lru_w_out, att_w_qkv, att_q_g, att_k_g, att_w_o, final_g):
    f = lambda a: np.ascontiguousarray(np.asarray(a, dtype=np.float32))
    x_prompt, x_sample = f(x_prompt), f(x_sample)
    if "nc" not in _NC_CACHE:
        _NC_CACHE["nc"] = build_program()
    nc = _NC_CACHE["nc"]
    shared = dict(mod_w=f(mod_w), mod_b=f(mod_b), norm_g=f(norm_g), ffn_w_gu=f(ffn_w_gu), ffn_w_down=f(ffn_w_down),
                  lru_w_in=f(lru_w_in), lru_conv_w=f(lru_conv_w), lru_conv_b=f(lru_conv_b), lru_gate_w=f(lru_gate_w),
                  lru_gate_b=f(lru_gate_b), lru_lambda=f(lru_lambda), lru_w_out=f(lru_w_out), att_w_qkv=f(att_w_qkv),
                  att_q_g=f(att_q_g), att_k_g=f(att_k_g), att_w_o=f(att_w_o), final_g=f(final_g),
                  ident=np.eye(128, dtype=np.float32))
    rcos, rsin = _rope_tables()
    in_maps = []
    for core in range(8):
        m = dict(shared)
        if core < 2:
            b = core
            m["xin"] = f(x_sample[b])
            m["cvec"] = f(c[b])
            m["h0"] = f(state_lru[b, 0])
            m["link"] = np.ones((128, 1), np.float32)
            m["ck"] = f(np.asarray(cache_k)[b, 0].reshape(PAST, NKV * HD))
            m["cv"] = f(np.asarray(cache_v)[b, 0].reshape(PAST, NKV * HD))
            ma = np.zeros((128, KC * 4), np.float32)
            m["rcos"], m["rsin"] = rcos, rsin
        else:
            seqs = PROMPT_ASSIGN[core - 2]
            xin = np.zeros((T, D), np.float32)
            for s_, sq_ in enumerate(seqs):
                xin[s_ * SEG:(s_ + 1) * SEG] = x_prompt[sq_]
            m["xin"] = xin
            m["cvec"] = f(c_ctx)
            m["h0"] = np.zeros((2, D), np.float32)
            m["link"] = np.zeros((128, 1), np.float32)
            m["ck"] = np.zeros((PAST, NKV * HD), np.float32)
            m["cv"] = np.zeros((PAST, NKV * HD), np.float32)
            ma = np.full((128, KC * 4), -30000.0, np.float32)
            for kc_ in range(4, KC):
                ma[:, kc_ * 4 + (kc_ - 4) // 2] = 0.0
            m["rcos"] = np.ones((T, 64), np.float32)
            m["rsin"] = np.zeros((T, 64), np.float32)
        m["maskadd"] = ma
        in_maps.append(m)
    res = run_bass_kernel_spmd(nc, in_maps, core_ids=list(range(8)))
    R = res.results
    y_prompt = np.zeros((16, SEG, D), np.float32)
    y_sample = np.zeros((2, T, D), np.float32)
    new_state = np.zeros((16, 1, 2, D), np.float32)
    new_k = np.zeros((16, 1, SEG, NKV, HD), np.float32)
    new_v = np.zeros((16, 1, SEG, NKV, HD), np.float32)
    for core in range(8):
        r = R[core]
        y = np.asarray(r["y"])
        if core < 2:
            y_sample[core] = y
        else:
            st = np.asarray(r["st"]).reshape(NSEG, 2, D)
            nk = np.asarray(r["nk"]).reshape(T, NKV, HD)
            nv = np.asarray(r["nv"]).reshape(T, NKV, HD)
            for s_, sq_ in enumerate(PROMPT_ASSIGN[core - 2]):
                y_prompt[sq_] = y[s_ * SEG:(s_ + 1) * SEG]
                new_state[sq_, 0] = st[s_]
                new_k[sq_, 0] = nk[s_ * SEG:(s_ + 1) * SEG]
                new_v[sq_, 0] = nv[s_ * SEG:(s_ + 1) * SEG]
    return (y_prompt, y_sample, new_state, new_k, new_v)
```

```python
import contextlib
import numpy as np
import concourse.bass as bass
import concourse.mybir as mybir
from concourse.bass_utils import run_bass_kernel_spmd

F32 = mybir.dt.float32
BF16 = mybir.dt.bfloat16
ALU = mybir.AluOpType
AF = mybir.ActivationFunctionType
AX = mybir.AxisListType

D = 1024
DC = 8
T = 1024
TT = 2
NSEG = 4
SEG = 256
DFF = 2816
FC = 22
HD = 128
NH = 8
NKV = 2
PAST = 512
KC = 12
EPS = 1e-6
BIG = 16384.0
SCALE = float(HD) ** -0.5
NSLOT = 5
SLOT_ELEMS = 4096

DEBUG_STOP = None


class Sem:
    def __init__(self, h, name):
        self.h = h
        self.name = name
        self.count = 0


class Buf:
    __slots__ = ("w", "r", "name")

    def __init__(self, name="", epoch=()):
        self.w = None
        self.r = list(epoch)
        self.name = name


class Tn:
    def __init__(self, t, bufs):
        self.t = t
        self.b = bufs

    def __getitem__(self, k):
        return self.t[k]


class KB:
    def __init__(self, nc, stack):
        self.nc = nc
        self.stack = stack
        self.topstack = stack
        self.eng = {"pe": nc.tensor, "act": nc.scalar, "dve": nc.vector, "pool": nc.gpsimd, "sp": nc.sync}
        self.waited = {e: {} for e in self.eng}
        self.freed = []
        self._alloc_lists = []
        self.psem = {}
        self.dma_sems = []
        for e in ("pe", "act", "dve", "pool"):
            self.psem[e] = self.newsem("prog_" + e)
        self.nsem = 0

    def newsem(self, name, dma=False):
        h = self.topstack.enter_context(self.nc.semaphore(name))
        s = Sem(h, name)
        if dma:
            self.dma_sems.append(s)
        return s

    def epoch(self, addr, size):
        need = {}
        for (a0, a1, toks) in self.freed:
            if a0 < addr + size and addr < a1:
                for s_, v in toks:
                    if need.get(s_, (s_, 0))[1] < v:
                        need[s_] = (s_, v)
        return list(need.values())

    @contextlib.contextmanager
    def scope(self):
        saved = self.stack
        allocs = []
        self._alloc_lists.append(allocs)
        with contextlib.ExitStack() as st:
            self.stack = st
            try:
                yield
            finally:
                self._alloc_lists.pop()
                for tn in allocs:
                    flat = []
                    for b in tn.b:
                        flat.extend(b if isinstance(b, list) else [b])
                    toks = {}
                    for b in flat:
                        for tok in ([b.w] if b.w is not None else []) + list(b.r):
                            s_, v = tok
                            if toks.get(s_, (s_, 0))[1] < v:
                                toks[s_] = (s_, v)
                    self.freed.append((tn.addr, tn.addr + tn.size, list(toks.values())))
                self.stack = saved

    def _wait(self, e, deps):
        need = {}
        for (s, v) in deps:
            if need.get(s, (None, 0))[1] < v:
                need[s] = (s, v)
        eng = self.eng[e]
        wd = self.waited[e]
        for s, v in need.values():
            if wd.get(s, 0) < v:
                eng.wait_ge(s.h, v)
                wd[s] = v

    def _deps(self, r, w):
        deps = []
        for b in r:
            if b.w is not None:
                deps.append(b.w)
        for b in w:
            if b.w is not None:
                deps.append(b.w)
            deps.extend(b.r)
        return deps

    def _commit(self, tok, r, w):
        for b in r:
            b.r.append(tok)
        for b in w:
            b.w = tok
            b.r = []

    def emit(self, e, fns, r=(), w=()):
        if callable(fns):
            fns = [fns]
        self._wait(e, self._deps(r, w))
        inst = None
        for f in fns:
            inst = f()
        s = self.psem[e]
        s.count += 1
        inst.then_inc(s.h, 1)
        tok = (s, s.count)
        self._commit(tok, r, w)
        return tok

    def dma(self, q, out, in_, sem, r=(), w=(), **kw):
        self._wait(q, self._deps(r, w))
        inst = self.eng[q].dma_start(out=out, in_=in_, **kw)
        sem.count += 16
        inst.then_inc(sem.h, 16)
        tok = (sem, sem.count)
        self._commit(tok, r, w)
        return tok

    def alloc(self, name, shape, dtype, nb=1, psum=False):
        self.nsem += 1
        name = f"t{self.nsem}_{name}"
        if psum:
            t = self.stack.enter_context(self.nc.psum_tensor(name, shape, dtype))
        else:
            t = self.stack.enter_context(self.nc.sbuf_tensor(name, shape, dtype))
        if psum:
            addr, size = 0, 0
            ep = []
        else:
            ml = self.nc.lookup_mloc(t)
            addr, size = int(ml.addr), int(ml.dims[1])
            ep = self.epoch(addr, size)
        if isinstance(nb, int):
            bufs = [Buf(f"{name}{i}", ep) for i in range(nb)]
        else:
            bufs = [[Buf(f"{name}{i}_{j}", ep) for j in range(nb[1])] for i in range(nb[0])]
        tn = Tn(t, bufs)
        tn.addr, tn.size = addr, size
        if self._alloc_lists:
            self._alloc_lists[-1].append(tn)
        return tn


def bc(ap, axis, n):
    l = [list(x) for x in ap.ap]
    l.insert(axis, [0, n])
    return bass.AP(ap.tensor, ap.offset, l)


def build_program():
    nc = bass.Bass("TRN2", target_bir_lowering=False)

    def din(name, shape, dt=F32):
        return nc.dram_tensor(name, list(shape), dt, kind="ExternalInput").ap()

    def dout(name, shape, dt=F32):
        return nc.dram_tensor(name, list(shape), dt, kind="ExternalOutput").ap()

    xin = din("xin", [T, D])
    cvec = din("cvec", [D])
    h0in = din("h0", [2, D])
    linkin = din("link", [128, 1])
    ckin = din("ck", [PAST, NKV * HD])
    cvin = din("cv", [PAST, NKV * HD])
    maskin = din("maskadd", [128, KC * 4])
    cosin = din("rcos", [T, 64])
    sinin = din("rsin", [T, 64])
    identin = din("ident", [128, 128])
    mod_w = din("mod_w", [2, D, 9 * D])
    mod_b = din("mod_b", [2, 9 * D])
    norm_g = din("norm_g", [2, 3, D])
    w_gu = din("ffn_w_gu", [2, 2, D, 2 * DFF])
    w_down = din("ffn_w_down", [2, 2, DFF, D])
    lru_w_in = din("lru_w_in", [1, D, 2 * D])
    lru_conv_w = din("lru_conv_w", [1, 4, D])
    lru_conv_b = din("lru_conv_b", [1, D])
    lru_gate_w = din("lru_gate_w", [1, 2, 2, 16, 64, 64])
    lru_gate_b = din("lru_gate_b", [1, 2, 2, D])
    lru_lambda = din("lru_lambda", [1, 2, D])
    lru_w_out = din("lru_w_out", [1, D, D])
    att_w_qkv = din("att_w_qkv", [1, D, 1536])
    att_q_g = din("att_q_g", [1, HD])
    att_k_g = din("att_k_g", [1, HD])
    att_w_o = din("att_w_o", [1, D, D])
    final_g = din("final_g", [D])
    y_out = dout("y", [T, D])
    st_out = dout("st", [8, D])
    nk_out = dout("nk", [T, NKV * HD])
    nv_out = dout("nv", [T, NKV * HD])

    stack = contextlib.ExitStack()
    with stack:
        stack.enter_context(nc.allow_non_contiguous_dma(reason="small strided constant loads"))
        kb = KB(nc, stack)
        V = nc.vector
        A = nc.scalar
        PE = nc.tensor

        xT = kb.alloc("xT", [128, DC, T], F32, nb=(DC, TT))
        hT = kb.alloc("hT", [128, DC, T], BF16, nb=TT)
        ps = [kb.alloc(f"ps{i}", [128, 512], F32, psum=True) for i in range(8)]
        ps_i = [0]

        ps_excl = set()

        def psn():
            while (ps_i[0] % 8) in ps_excl:
                ps_i[0] += 1
            p = ps[ps_i[0] % 8]
            ps_i[0] += 1
            return p

        slots = [kb.alloc(f"slot{i}", [128, SLOT_ELEMS], BF16) for i in range(NSLOT)]
        slot_sems = [kb.newsem(f"slotsem{i}", dma=True) for i in range(NSLOT)]
        wd_sems = [kb.newsem(f"wdsem{i}", dma=True) for i in range(DC)]
        slot_i = [0]
        pinned = set()
        nfetch = [0]
        fetch_gate = []

        def fetch(parts):
            while (slot_i[0] % NSLOT) in pinned:
                slot_i[0] += 1
            i = slot_i[0] % NSLOT
            slot_i[0] += 1
            sl = slots[i]
            sl.idx = i
            nfetch[0] += 1
            if nfetch[0] == 5 and fetch_gate:
                kb._wait("pool", list(fetch_gate))
            kb._wait("pool", kb._deps((), [sl.b[0]]))
            sem = slot_sems[i]
            for (vf, src) in parts:
                inst = nc.gpsimd.dma_start(out=vf(sl.t[:]), in_=src)
                sem.count += 16
                inst.then_inc(sem.h, 16)
            kb._commit((sem, sem.count), (), [sl.b[0]])
            if nfetch[0] <= 4:
                fetch_gate.append((sem, sem.count))
            return sl

        csem = kb.newsem("const", dma=True)
        csem_p = kb.newsem("const_pool", dma=True)
        const_bufs = []
        const_bufs_p = []

        def cload(name, shape, src, dtype=F32, q="sp"):
            t = kb.alloc(name, shape, dtype)
            kb.dma(q, t.t[:], src, csem if q == "sp" else csem_p, w=[t.b[0]])
            (const_bufs if q == "sp" else const_bufs_p).append(t.b[0])
            return t

        ident = cload("ident", [128, 128], identin)
        link_sb = cload("link", [128, 1], linkin)
        stg_specs = [
            [("modb0", mod_b[0].rearrange("(c f) -> c f", f=128), 72),
             ("cvec", cvec.rearrange("(c f) -> c f", f=128), 8),
             ("ng", norm_g.rearrange("l s (c f) -> (l s c) f", f=128), 48)],
            [("modb1", mod_b[1].rearrange("(c f) -> c f", f=128), 72),
             ("fg", final_g.rearrange("(c f) -> c f", f=128), 8),
             ("h0", h0in.rearrange("r (c f) -> (r c) f", f=128), 16),
             ("convb", lru_conv_b[0].rearrange("(c f) -> c f", f=128), 8),
             ("lam", lru_lambda[0].rearrange("a (c f) -> (a c) f", f=128), 16)],
            [("convw", lru_conv_w[0].rearrange("k (c f) -> (k c) f", f=128), 32),
             ("gateb", lru_gate_b[0].rearrange("a b (c f) -> (a b c) f", f=128), 32)],
        ]
        stg = [kb.alloc(f"stg{i}", [128, 128], F32) for i in range(3)]
        coff = {}
        for i, specs in enumerate(stg_specs):
            ro = 0
            for (nm, src, rows) in specs:
                inst = nc.sync.dma_start(out=stg[i].t[ro:ro + rows, :], in_=src)
                csem.count += 16
                inst.then_inc(csem.h, 16)
                coff[nm] = i * 128 + ro
                ro += rows
            const_bufs.append(stg[i].b[0])
        ctok = (csem, csem.count)
        for b in const_bufs:
            b.w = ctok
        for b in const_bufs_p:
            b.w = (csem_p, csem_p.count)

        cpack = kb.alloc("cpack", [128, 3 * 128], F32)
        for i, specs in enumerate(stg_specs):
            R_ = sum(r_ for (_, _, r_) in specs)
            p = psn()
            kb.emit("pe", lambda: PE.transpose(p.t[:, 0:R_], stg[i].t[0:R_, :], ident.t[0:R_, 0:R_]),
                    r=[stg[i].b[0], ident.b[0]], w=[p.b[0]])
            kb.emit("dve", lambda: V.tensor_copy(out=cpack.t[:, i * 128:i * 128 + R_], in_=p.t[:, 0:R_]),
                    r=[p.b[0]], w=[cpack.b[0]])

        class CV:
            def __init__(self, ap):
                self.t = ap
                self.b = cpack.b

        def cview(nm, n, pat=None, **kw):
            a = cpack.t[:, coff[nm]:coff[nm] + n]
            if pat:
                a = a.rearrange(pat, **kw)
            return CV(a)
        cv_sb = cview("cvec", 8)
        modb_sb = [cview("modb0", 72), cview("modb1", 72)]
        ng_sb = cview("ng", 48, "p (l s c) -> p l s c", l=2, s=3)
        fg_sb = cview("fg", 8)
        h0_sb = cview("h0", 16, "p (r c) -> p r c", r=2)
        convb_sb = cview("convb", 8)
        lam_sb = cview("lam", 16, "p (a c) -> p a c", a=2)
        convw_sb = cview("convw", 32, "p (k c) -> p k c", k=4)
        gateb_sb = cview("gateb", 32, "p (a c) -> p a c", a=4)

        ones_f = kb.alloc("ones_f", [128, 128], F32)
        kb.emit("dve", lambda: V.memset(ones_f.t[:], 1.0), w=[ones_f.b[0]])
        eps_sb = kb.alloc("eps_sb", [128, 2], F32)
        kb.emit("dve", lambda: V.memset(eps_sb.t[:, 0:1], EPS), w=[eps_sb.b[0]])
        kb.emit("dve", lambda: V.memset(eps_sb.t[:, 1:2], 1.0), w=[eps_sb.b[0]])
        ones_b = kb.alloc("ones_b", [128, 128], BF16)
        kb.emit("dve", lambda: V.memset(ones_b.t[:], 1.0), w=[ones_b.b[0]])

        gw = kb.alloc("gw", [128, 4, DC, 128], BF16)
        kb.emit("dve", lambda: V.memset(gw.t[:], 0.0), w=[gw.b[0]])
        gsem = kb.newsem("gwsem", dma=True)
        for hb in range(2):
            src = lru_gate_w[0].rearrange("a b (dc h) k j -> h k (a b) dc j", h=2)[hb]
            kb.dma("pool", gw.t[hb * 64:(hb + 1) * 64, :, :, hb * 64:(hb + 1) * 64], src, gsem, w=[gw.b[0]])
        lw = kb.alloc("lw", [128, 4, DC], F32)
        Lc = kb.alloc("Lc", [128, 2, DC], F32)
        hgb = kb.alloc("hgb", [128, 4, DC], F32)
        t1 = kb.alloc("lt1", [128, 2, DC], F32)
        t2 = kb.alloc("lt2", [128, 2, DC], F32)
        t3 = kb.alloc("lt3", [128, 2, DC], F32)
        kb.emit("dve", lambda: V.tensor_scalar_mul(out=lw.t[:], in0=convw_sb.t[:], scalar1=link_sb.t[:, 0:1]),
                r=[convw_sb.b[0], link_sb.b[0]], w=[lw.b[0]])
        kb.emit("dve", lambda: V.tensor_scalar_mul(out=hgb.t[:], in0=gateb_sb.t[:], scalar1=0.5),
                r=[gateb_sb.b[0]], w=[hgb.b[0]])
        kb.emit("act", lambda: A.activation(out=t1.t[:], in_=lam_sb.t[:], func=AF.Abs),
                r=[lam_sb.b[0]], w=[t1.b[0]])
        kb.emit("act", lambda: A.activation(out=t1.t[:], in_=t1.t[:], func=AF.Exp, scale=-1.0),
                r=[t1.b[0]], w=[t1.b[0]])
        kb.emit("dve", lambda: V.tensor_scalar_add(out=t2.t[:], in0=t1.t[:], scalar1=2.0), r=[t1.b[0]], w=[t2.b[0]])
        kb.emit("dve", lambda: V.reciprocal(out=t2.t[:], in_=t2.t[:]), r=[t2.b[0]], w=[t2.b[0]])
        kb.emit("dve", lambda: V.tensor_tensor(out=t2.t[:], in0=t2.t[:], in1=t1.t[:], op=ALU.mult),
                r=[t1.b[0], t2.b[0]], w=[t2.b[0]])
        kb.emit("dve", lambda: V.tensor_tensor(out=t3.t[:], in0=t2.t[:], in1=t2.t[:], op=ALU.mult),
                r=[t2.b[0]], w=[t3.b[0]])
        kb.emit("dve", lambda: V.memset(t1.t[:], 1.0 / 17.0), r=[], w=[t1.b[0]])
        for k in (15, 13, 11, 9, 7, 5, 3, 1):
            kb.emit("dve", lambda: V.tensor_tensor(out=t1.t[:], in0=t1.t[:], in1=t3.t[:], op=ALU.mult),
                    r=[t3.b[0]], w=[t1.b[0]])
            kb.emit("dve", lambda k=k: V.tensor_scalar_add(out=t1.t[:], in0=t1.t[:], scalar1=1.0 / k),
                    r=[], w=[t1.b[0]])
        kb.emit("dve", lambda: V.tensor_tensor(out=t1.t[:], in0=t1.t[:], in1=t2.t[:], op=ALU.mult),
                r=[t2.b[0]], w=[t1.b[0]])
        kb.emit("dve", lambda: V.tensor_scalar_min(out=t3.t[:], in0=lam_sb.t[:], scalar1=0.0),
                r=[lam_sb.b[0]], w=[t3.b[0]])
        kb.emit("dve", lambda: V.scalar_tensor_tensor(out=Lc.t[:], in0=t1.t[:], scalar=-2.0, in1=t3.t[:],
                                                      op0=ALU.mult, op1=ALU.add),
                r=[t1.b[0], t3.b[0]], w=[Lc.b[0]])
        kb.emit("dve", lambda: V.tensor_scalar_mul(out=Lc.t[:], in0=Lc.t[:], scalar1=4.0), r=[], w=[Lc.b[0]])
        qh = kb.alloc("qh", [128, 1], F32)
        kb.emit("dve", lambda: V.memset(qh.t[:], 0.25), w=[qh.b[0]])


        with kb.scope():
            xtok = [kb.alloc(f"xtok{i}", [128, D], F32) for i in range(2)]
            xsem = [kb.newsem(f"xsem{i}", dma=True) for i in range(2)]
            for tc in range(8):
                xb_ = xtok[tc % 2]
                xtokn = kb.dma("sp", xb_.t[:], xin[tc * 128:(tc + 1) * 128, :], xsem[tc % 2], w=[xb_.b[0]])
                if tc >= 6:
                    fetch_gate.append(xtokn)
                for half in range(2):
                    p = psn()
                    kb.emit("pe", [(lambda p=p, q=q, dc=half * 4 + q: PE.transpose(
                        p.t[:, q * 128:(q + 1) * 128], xb_.t[:, dc * 128:(dc + 1) * 128], ident.t[:]))
                        for q in range(4)], r=[xb_.b[0], ident.b[0]], w=[p.b[0]])
                    tt = tc // 4
                    dst = xT.t[:, half * 4:half * 4 + 4, tc * 128:(tc + 1) * 128]
                    src = p.t[:].rearrange("p (q c) -> p q c", q=4)
                    eng = "act" if half == 0 else "dve"
                    if eng == "act":
                        kb.emit("act", lambda: A.copy(out=dst, in_=src), r=[p.b[0]],
                                w=[xT.b[half * 4 + q][tt] for q in range(4)])
                    else:
                        kb.emit("dve", lambda: V.tensor_copy(out=dst, in_=src), r=[p.b[0]],
                                w=[xT.b[half * 4 + q][tt] for q in range(4)])

        modv = kb.alloc("modv", [128, 2, 72], F32, nb=6)
        scs = kb.alloc("scs", [128, DC], BF16)
        tmpc = kb.alloc("tmpc", [128, DC], F32)
        kb.emit("act", lambda: A.activation(out=tmpc.t[:], in_=cv_sb.t[:], func=AF.Silu),
                r=[cv_sb.b[0]], w=[tmpc.b[0]])
        kb.emit("dve", lambda: V.tensor_copy(out=scs.t[:], in_=tmpc.t[:]), r=[tmpc.b[0]], w=[scs.b[0]])

        mod_pending = [(l, ch) for l in range(2) for ch in range(18)]

        def mod_chunk(l, ch, bank=None, col0=0, do_add=True):
            mod_pending.remove((l, ch))
            src = mod_w[l].rearrange("(dc p) f -> p dc f", p=128)[:, :, ch * 512:(ch + 1) * 512]
            sl = fetch([(lambda s: s.rearrange("p (dc f) -> p dc f", dc=DC), src)])
            sv = sl.t[:].rearrange("p (dc f) -> p dc f", dc=DC)
            p = psn() if bank is None else bank
            fns = []
            for c4 in range(4):
                for dc in range(DC):
                    fns.append(lambda c4=c4, dc=dc: PE.matmul(
                        p.t[:, col0 + c4:col0 + c4 + 1], sv[:, dc, c4 * 128:(c4 + 1) * 128], scs.t[:, dc:dc + 1],
                        start=(dc == 0), stop=(dc == DC - 1)))
            kb.emit("pe", fns, r=[sl.b[0], scs.b[0]], w=[p.b[0]])
            if do_add:
                kb.emit("dve", lambda: V.tensor_tensor(out=modv.t[:, l, ch * 4:(ch + 1) * 4], in0=p.t[:, col0:col0 + 4],
                                                       in1=modb_sb[l].t[:, ch * 4:(ch + 1) * 4], op=ALU.add),
                        r=[p.b[0], cpack.b[0]], w=[modv.b[l * 3 + ch // 6]])

        def run_bg(n=1):
            for _ in range(n):
                if mod_pending:
                    mod_chunk(*mod_pending[0])

        def ensure_mod(l, s_):
            for ch in range(s_ * 6, s_ * 6 + 6):
                if (l, ch) in mod_pending:
                    mod_chunk(l, ch)

        coef = kb.alloc("coef", [128, 3, DC], F32)
        coefg = kb.alloc("coefg", [128, DC], F32)

        def ensure_mod_part(l, s_, lo, hi):
            for ch in range(s_ * 6 + lo, s_ * 6 + hi):
                if (l, ch) in mod_pending:
                    mod_chunk(l, ch)

        def make_coef_g(l, s, gmul):
            ensure_mod_part(l, s, 4, 6)
            base = s * 24
            kb.emit("dve", lambda: V.tensor_scalar_mul(out=coefg.t[:], in0=modv.t[:, l, base + 16:base + 24],
                                                        scalar1=gmul), r=[modv.b[l * 3 + s]], w=[coefg.b[0]])

        def make_coef(l, s, gmul):
            ensure_mod_part(l, s, 0, 4)
            base = s * 24
            kb.emit("dve", lambda: V.scalar_tensor_tensor(
                out=coef.t[:, 0, :], in0=modv.t[:, l, base + 8:base + 16], scalar=1.0, in1=ng_sb.t[:, l, s, :],
                op0=ALU.add, op1=ALU.mult), r=[modv.b[l * 3 + s], ng_sb.b[0]], w=[coef.b[0]])
            kb.emit("dve", lambda: V.tensor_copy(out=coef.t[:, 1, :], in_=modv.t[:, l, base:base + 8]),
                    r=[modv.b[l * 3 + s]], w=[coef.b[0]])

        sq_top = [kb.alloc(f"sq{i}", [128, 512], BF16) for i in range(4)]

        stats_state = {"st": None}

        def stats_begin():
            pss = [psn(), psn()]
            idx = [ps.index(p_) for p_ in pss]
            for i_ in idx:
                ps_excl.add(i_)
            stats_state["st"] = {"pss": pss, "idx": idx, "n": 0, "done": set()}

        def stats_sq(dc, tt):
            st = stats_state["st"]
            tsl = slice(tt * 512, (tt + 1) * 512)
            s_ = sq_top[st["n"] % 4]
            st["n"] += 1
            st.setdefault("tiles", {})[(dc, tt)] = s_
            kb.emit("act", lambda: A.activation(out=s_.t[:], in_=xT.t[:, dc, tsl], func=AF.Square),
                    r=[xT.b[dc][tt]], w=[s_.b[0]])

        def stats_mm(dc, tt):
            st = stats_state["st"]
            p = st["pss"][tt]
            s_ = st["tiles"].pop((dc, tt))
            kb.emit("pe", lambda: PE.matmul(p.t[:], ones_b.t[:], s_.t[:], start=(dc == 0), stop=(dc == DC - 1)),
                    r=[s_.b[0], ones_b.b[0]], w=[p.b[0]])

        def stats_tile1(dc, tt):
            stats_sq(dc, tt)
            stats_mm(dc, tt)

        def stats_tile(dc):
            for tt in range(TT):
                stats_tile1(dc, tt)

        def sub_in_half(tt, acol, bcol, out_t, out_bufs, acol_buf, extra_w=(), col0=None):
            st = stats_state["st"]
            tsl = slice(tt * 512, (tt + 1) * 512)
            osl = tsl if col0 is None else slice(col0, col0 + 512)
            p = st["pss"][tt]
            rstd = kb.alloc("rstdh", [128, 512], F32)
            xn = [kb.alloc(f"xnh{i}", [128, 512], F32) for i in range(2)]
            kb.emit("act", lambda: A.activation(out=rstd.t[:], in_=p.t[:], func=AF.Ln, bias=eps_sb.t[:, 0:1],
                                                scale=1.0 / D), r=[p.b[0], eps_sb.b[0]], w=[rstd.b[0]])
            kb.emit("act", lambda: A.activation(out=rstd.t[:], in_=rstd.t[:], func=AF.Exp, scale=-0.5),
                    r=[rstd.b[0]], w=[rstd.b[0]])
            ps_excl.discard(st["idx"][tt])
            for dc in range(DC):
                x_ = xn[dc % 2]
                kb.emit("dve", lambda: V.scalar_tensor_tensor(out=x_.t[:], in0=xT.t[:, dc, tsl], scalar=acol(dc),
                                                              in1=rstd.t[:], op0=ALU.mult, op1=ALU.mult),
                        r=[xT.b[dc][tt], rstd.b[0], acol_buf], w=[x_.b[0]])
                if bcol is not None:
                    kb.emit("act", lambda: A.activation(out=out_t[:, dc, osl], in_=x_.t[:], func=AF.Identity,
                                                        bias=bcol(dc)),
                            r=[x_.b[0], acol_buf], w=[out_bufs[tt]] + list(extra_w))
                else:
                    kb.emit("act", lambda: A.copy(out=out_t[:, dc, osl], in_=x_.t[:]),
                            r=[x_.b[0]], w=[out_bufs[tt]] + list(extra_w))
            st["done"].add(tt)
            if len(st["done"]) == TT:
                stats_state["st"] = None

        def sub_in(acol, bcol, out_t, out_bufs, acol_buf, final=False, pre_apply=None):
            if stats_state["st"] is None:
                stats_begin()
                for dc in range(DC):
                    stats_tile(dc)
            st = stats_state["st"]
            with kb.scope():
                if 0 not in st["done"]:
                    if pre_apply is not None:
                        pre_apply()
                    sub_in_half(0, acol, bcol, out_t, out_bufs, acol_buf)
                sub_in_half(1, acol, bcol, out_t, out_bufs, acol_buf)

        def mid_apply(nxt):
            if nxt == "final":
                yv = hT.t[:].bitcast(F32)
                sub_in_half(0, lambda dc: fg_sb.t[:, dc:dc + 1], None, yv, hT.b, fg_sb.b[0], extra_w=[hT.b[1]], col0=0)
                early_out["yv"] = yv
                return
            make_coef(nxt[0], nxt[1], None)
            sub_in_half(0, lambda dc: coef.t[:, 0, dc:dc + 1], lambda dc: coef.t[:, 1, dc:dc + 1], hT.t, hT.b, coef.b[0])

        early_out = {}

        def emit_out_tc(tc, src_t, src_bufs, yb_, sem_):
            c0 = (tc % 4) * 128 if src_t is early_out.get("yv") else tc * 128
            for half in range(2):
                p = psn()
                kb.emit("pe", [(lambda q=q: PE.transpose(p.t[:, q * 128:(q + 1) * 128],
                                                         src_t[:, half * 4 + q, c0:c0 + 128], ident.t[:]))
                               for q in range(4)], r=list(src_bufs) + [ident.b[0]], w=[p.b[0]])
                if half == 0:
                    kb.emit("act", lambda: A.copy(out=yb_.t[:, 0:512], in_=p.t[:]), r=[p.b[0]], w=[yb_.b[0]])
                else:
                    kb.emit("dve", lambda: V.tensor_copy(out=yb_.t[:, 512:1024], in_=p.t[:]), r=[p.b[0]], w=[yb_.b[0]])
            kb.dma("sp", y_out[tc * 128:(tc + 1) * 128, :], yb_.t[:], sem_, r=[yb_.b[0]])

        def closing(group_fn, nxt):
            stats_begin()
            LAG = 3
            seq = [(tt, do) for tt in range(TT) for do in range(DC)]
            for gi, (tt, do) in enumerate(seq):
                pp = group_fn(tt, do)
                resid_update(pp, do, tt)
                stats_sq(do, tt)
                if gi >= LAG:
                    ptt, pdo = seq[gi - LAG]
                    stats_mm(pdo, ptt)
                if gi == DC - 1 + LAG and nxt is not None:
                    mid_apply(nxt)
                    if nxt == "final":
                        early_out["ytok"] = kb.alloc("ytokf", [128, D], F32)
                        early_out["sem"] = kb.newsem("ysemf", dma=True)
                        out_sems.append(early_out["sem"])
                if nxt == "final" and DC + LAG + 1 <= gi <= DC + LAG + 4:
                    emit_out_tc(gi - (DC + LAG + 1), early_out["yv"], [hT.b[0], hT.b[1]], early_out["ytok"], early_out["sem"])
            for gi in range(len(seq) - LAG, len(seq)):
                ptt, pdo = seq[gi]
                stats_mm(pdo, ptt)

        def sub_in_std(pre_apply=None):
            sub_in(lambda dc: coef.t[:, 0, dc:dc + 1], lambda dc: coef.t[:, 1, dc:dc + 1], hT.t, hT.b, coef.b[0],
                   pre_apply=pre_apply)

        def resid_update(p, do, tt):
            tsl = slice(tt * 512, (tt + 1) * 512)
            kb.emit("dve", lambda: V.scalar_tensor_tensor(
                out=xT.t[:, do, tsl], in0=p.t[:], scalar=coefg.t[:, do:do + 1], in1=xT.t[:, do, tsl],
                op0=ALU.mult, op1=ALU.add), r=[p.b[0], coefg.b[0]], w=[xT.b[do][tt]])

        def ffn(l, i, s, nbg=0, nxt=None):
            sub_in_std(pre_apply=lambda: make_coef(l, s, 0.5))
            with kb.scope():
                act = kb.alloc("act", [128, FC, T], BF16, nb=TT)
                sg = [kb.alloc(f"sg{k}", [128, 512], F32) for k in range(2)]
                sg_i = 0
                wd_sb = kb.alloc("wd_sb", [128, DC, FC * 128], BF16, nb=DC)
                wdv = wd_sb.t[:].rearrange("p d (fc c) -> p d fc c", fc=FC)
                wd = w_down[l, i].rearrange("(fc p) d -> p fc d", p=128)
                _dbg("ffn peak")
                chunk_tok = {}

                def load_wd(k, after=None):
                    if after is not None:
                        kb._wait("pool", [after])
                    kb.dma("pool", wdv[:, k, :, :], wd[:, :, k * 128:(k + 1) * 128], wd_sems[k], w=[wd_sb.b[k]])

                wv = w_gu[l, i].rearrange("(dc p) f -> p dc f", p=128)
                for ch in range(FC // 2):
                    j0 = ch * 2
                    sl = fetch([
                        (lambda s_: s_.rearrange("p (dc g c) -> p dc g c", dc=DC, g=2)[:, :, 0, :],
                         wv[:, :, j0 * 128:(j0 + 2) * 128]),
                        (lambda s_: s_.rearrange("p (dc g c) -> p dc g c", dc=DC, g=2)[:, :, 1, :],
                         wv[:, :, DFF + j0 * 128:DFF + (j0 + 2) * 128]),
                    ])
                    if ch >= 6:
                        load_wd(ch - 6)
                    sv = sl.t[:].rearrange("p (dc g c) -> p dc g c", dc=DC, g=2)
                    def emit_part(ch_, sl_, sv_, j0_, jt_list):
                        nonlocal sg_i
                        for (jj, tt) in jt_list:
                            j = j0_ + jj
                            pg_, pu_ = psn(), psn()
                            for g_, pp in ((0, pg_), (1, pu_)):
                                chunk_tok[ch_] = kb.emit("pe", [(lambda dc=dc: PE.matmul(
                                    pp.t[:], sv_[:, dc, g_, jj * 128:(jj + 1) * 128],
                                    hT.t[:, dc, tt * 512:(tt + 1) * 512], start=(dc == 0), stop=(dc == DC - 1)))
                                    for dc in range(DC)], r=[sl_.b[0], hT.b[tt]], w=[pp.b[0]])
                            s_ = sg[sg_i % 2]
                            sg_i += 1
                            kb.emit("act", lambda: A.activation(out=s_.t[:], in_=pg_.t[:], func=AF.Silu),
                                    r=[pg_.b[0]], w=[s_.b[0]])
                            kb.emit("dve", lambda: V.tensor_tensor(out=act.t[:, j, tt * 512:(tt + 1) * 512], in0=s_.t[:],
                                                                   in1=pu_.t[:], op=ALU.mult),
                                    r=[s_.b[0], pu_.b[0]], w=[act.b[tt]])

                    if ch == 0:
                        emit_part(ch, sl, sv, j0, [(0, 0), (1, 0)])
                        held = (ch, sl, sv, j0)
                    elif ch == 1:
                        emit_part(ch, sl, sv, j0, [(0, 0), (1, 0)])
                        emit_part(*held, [(0, 1), (1, 1)])
                        emit_part(ch, sl, sv, j0, [(0, 1), (1, 1)])
                    else:
                        emit_part(ch, sl, sv, j0, [(0, 0), (0, 1), (1, 0), (1, 1)])
                    if ch < nbg:
                        run_bg(1)
                for k in (5, 6, 7):
                    load_wd(k, after=chunk_tok[k + 2])
                make_coef_g(l, s, 0.5)

                def down_group(tt, do):
                    pp = psn()
                    kb.emit("pe", [(lambda fc=fc: PE.matmul(
                        pp.t[:], wdv[:, do, fc, :], act.t[:, fc, tt * 512:(tt + 1) * 512],
                        start=(fc == 0), stop=(fc == FC - 1))) for fc in range(FC)],
                        r=[wd_sb.b[do], act.b[tt]], w=[pp.b[0]])
                    return pp
                closing(down_group, nxt)

        def lru(l, lru_nxt=None):
            sub_in_std(pre_apply=lambda: make_coef(l, 1, 1.0))
            with kb.scope():
                yT = kb.alloc("yT", [128, DC, T], BF16, nb=TT)
                st_sb = kb.alloc("st_sb", [128, DC, 8], F32)
                NB = 2
                NB3 = 3
                xb = [kb.alloc(f"xb{k}", [128, T], F32) for k in range(NB3)]
                xc = [kb.alloc(f"xc{k}", [128, T], F32) for k in range(NB)]
                xcb = [kb.alloc(f"xcb{k}", [128, T], BF16) for k in range(NB)]
                gy = [kb.alloc(f"gy{k}", [128, T], F32) for k in range(NB3)]
                ra = [[kb.alloc(f"ra{k}{d}", [128, T], F32) for d in range(2)] for k in range(NB)]
                iu = [[kb.alloc(f"iu{k}{d}", [128, T], F32) for d in range(2)] for k in range(NB)]
                tm1 = [kb.alloc(f"tm{k}", [128, T], F32) for k in range(NB)]
                _dbg("lru peak")
                win = lru_w_in[0].rearrange("(dc p) f -> p dc f", p=128)

                def fetch_win(dc):
                    return fetch([
                        (lambda s_: s_[:, 0:2048].rearrange("p (dc g c) -> p dc g c", dc=DC, g=2)[:, :, 0, :],
                         win[:, :, dc * 128:(dc + 1) * 128]),
                        (lambda s_: s_[:, 0:2048].rearrange("p (dc g c) -> p dc g c", dc=DC, g=2)[:, :, 1, :],
                         win[:, :, D + dc * 128:D + (dc + 1) * 128]),
                    ])
                wsl = {0: fetch_win(0)}
                bg_sched = [2, 2, 2, 2, 2, 2, 1, 1]

                def stage_a(dc):
                    k = dc % NB
                    if dc + 1 < DC:
                        wsl[dc + 1] = fetch_win(dc + 1)
                    sl = wsl.pop(dc)
                    sv = sl.t[:, 0:2048].rearrange("p (dc g c) -> p dc g c", dc=DC, g=2)
                    pxb = [psn(), psn()]
                    pyb = [psn(), psn()]
                    for tt in range(TT):
                        for g_, pp in ((0, pxb), (1, pyb)):
                            kb.emit("pe", [(lambda d2=d2: PE.matmul(
                                pp[tt].t[:], sv[:, d2, g_, :], hT.t[:, d2, tt * 512:(tt + 1) * 512],
                                start=(d2 == 0), stop=(d2 == DC - 1))) for d2 in range(DC)],
                                r=[sl.b[0], hT.b[tt]], w=[pp[tt].b[0]])
                    XB, XC, XCB, GY = xb[dc % NB3], xc[k], xcb[k], gy[dc % NB3]
                    for tt in range(TT):
                        tsl = slice(tt * 512, (tt + 1) * 512)
                        kb.emit("act", lambda: A.copy(out=XB.t[:, tsl], in_=pxb[tt].t[:]), r=[pxb[tt].b[0]], w=[XB.b[0]])
                    for tt in range(TT):
                        tsl = slice(tt * 512, (tt + 1) * 512)
                        kb.emit("act", lambda: A.activation(out=GY.t[:, tsl], in_=pyb[tt].t[:], func=AF.Gelu_apprx_tanh),
                                r=[pyb[tt].b[0]], w=[GY.b[0]])
                    cw = lambda kk: convw_sb.t[:, kk, dc:dc + 1]
                    lwk = lambda kk: lw.t[:, kk, dc:dc + 1]
                    xb3 = XB.t[:].rearrange("p (s c) -> p s c", s=NSEG)
                    xc3 = XC.t[:].rearrange("p (s c) -> p s c", s=NSEG)
                    kb.emit("dve", lambda: V.tensor_scalar(out=XC.t[:], in0=XB.t[:], scalar1=cw(1), scalar2=convb_sb.t[:, dc:dc + 1],
                                                           op0=ALU.mult, op1=ALU.add),
                            r=[XB.b[0], convw_sb.b[0], convb_sb.b[0]], w=[XC.b[0]])

                    def mac(o, i_, sc_):
                        kb.emit("dve", lambda: V.scalar_tensor_tensor(out=o, in0=i_, scalar=sc_, in1=o,
                                                                      op0=ALU.mult, op1=ALU.add),
                                r=[XB.b[0], lw.b[0]], w=[XC.b[0]])
                    mac(xc3[:, :, 1:SEG], xb3[:, :, 0:SEG - 1], cw(0))
                    mac(xc3[:, :, 0:SEG - 1], xb3[:, :, 1:SEG], cw(2))
                    mac(xc3[:, :, 0:SEG - 2], xb3[:, :, 2:SEG], cw(3))
                    mac(xc3[:, 1:NSEG, 0], xb3[:, 0:NSEG - 1, SEG - 1], lwk(0))
                    mac(xc3[:, 0:NSEG - 1, SEG - 1], xb3[:, 1:NSEG, 0], lwk(2))
                    mac(xc3[:, 0:NSEG - 1, SEG - 1], xb3[:, 1:NSEG, 1], lwk(3))
                    mac(xc3[:, 0:NSEG - 1, SEG - 2], xb3[:, 1:NSEG, 0], lwk(3))

                def stage_a2(dc):
                    k = dc % NB
                    XB, XC, XCB, GY = xb[dc % NB3], xc[k], xcb[k], gy[dc % NB3]
                    kb.emit("act", lambda: A.copy(out=XCB.t[:], in_=XC.t[:]), r=[XC.b[0]], w=[XCB.b[0]])
                    pgs = {}
                    for d_ in range(2):
                        for gi in range(2):
                            pp = [psn(), psn()]
                            pgs[(d_, gi)] = pp
                            ab = d_ * 2 + gi
                            for tt in range(TT):
                                kb.emit("pe", lambda: PE.matmul(pp[tt].t[:], gw.t[:, ab, dc, :],
                                                                 XCB.t[:, tt * 512:(tt + 1) * 512], start=True, stop=True),
                                        r=[gw.b[0], XCB.b[0]], w=[pp[tt].b[0]])
                        for gi, dstt in ((0, ra[k][d_]), (1, iu[k][d_])):
                            ab = d_ * 2 + gi
                            pp = pgs[(d_, gi)]
                            for tt in range(TT):
                                tsl = slice(tt * 512, (tt + 1) * 512)
                                kb.emit("act", lambda: A.activation(out=dstt.t[:, tsl], in_=pp[tt].t[:], func=AF.Tanh,
                                                                    bias=hgb.t[:, ab, dc:dc + 1], scale=0.5),
                                        r=[pp[tt].b[0], hgb.b[0]], w=[dstt.b[0]])
                    run_bg(bg_sched[dc])
                    for d_ in range(2):
                        RA = ra[k][d_]
                        kb.emit("act", lambda: A.activation(out=RA.t[:], in_=RA.t[:], func=AF.Exp,
                                                            bias=Lc.t[:, d_, dc:dc + 1], scale=Lc.t[:, d_, dc:dc + 1]),
                                r=[Lc.b[0]], w=[RA.b[0]])
                    for d_ in range(2):
                        RA = ra[k][d_]
                        TM = XB if d_ == 0 else tm1[k]
                        kb.emit("act", lambda: A.activation(out=TM.t[:], in_=RA.t[:], func=AF.Square),
                                r=[RA.b[0], XC.b[0]], w=[TM.b[0]])
                    for d_ in range(2):
                        TM = XB if d_ == 0 else tm1[k]
                        kb.emit("act", lambda: A.activation(out=TM.t[:], in_=TM.t[:], func=AF.Sqrt, bias=qh.t[:, 0:1], scale=-0.25),
                                r=[qh.b[0]], w=[TM.b[0]])

                def stage_b(dc):
                    k = dc % NB
                    XB, XC, GY = xb[dc % NB3], xc[k], gy[dc % NB3]
                    for d_ in range(2):
                        RA, IU = ra[k][d_], iu[k][d_]
                        TM = XB if d_ == 0 else tm1[k]
                        kb.emit("dve", lambda: V.tensor_tensor(out=TM.t[:], in0=TM.t[:], in1=XC.t[:], op=ALU.mult),
                                r=[XC.b[0]], w=[TM.b[0]])
                        kb.emit("dve", lambda: V.scalar_tensor_tensor(out=IU.t[:], in0=IU.t[:], scalar=1.0, in1=TM.t[:],
                                                                      op0=ALU.add, op1=ALU.mult),
                                r=[TM.b[0]], w=[IU.b[0]])
                        ra3 = RA.t[:].rearrange("p (s c) -> p s c", s=NSEG)
                        iu3 = IU.t[:].rearrange("p (s c) -> p s c", s=NSEG)
                        if d_ == 0:
                            kb.emit("dve", lambda: V.tensor_scalar_mul(out=ra3[:, 1:NSEG, 0], in0=ra3[:, 1:NSEG, 0],
                                                                        scalar1=link_sb.t[:, 0:1]),
                                    r=[link_sb.b[0]], w=[RA.b[0]])
                            kb.emit("dve", lambda: V.tensor_tensor_scan(out=IU.t[:], data0=RA.t[:], data1=IU.t[:],
                                                                        initial=h0_sb.t[:, 0, dc:dc + 1],
                                                                        op0=ALU.mult, op1=ALU.add),
                                    r=[RA.b[0], h0_sb.b[0]], w=[IU.b[0]])
                            kb.emit("dve", lambda: V.tensor_copy(
                                out=st_sb.t[:, dc, :].rearrange("p (s r) -> p s r", r=2)[:, :, 0],
                                in_=iu3[:, :, SEG - 1]), r=[IU.b[0]], w=[st_sb.b[0]])
                        else:
                            kb.emit("dve", lambda: V.tensor_scalar_mul(out=ra3[:, 0:NSEG - 1, SEG - 1],
                                                                        in0=ra3[:, 0:NSEG - 1, SEG - 1],
                                                                        scalar1=link_sb.t[:, 0:1]),
                                    r=[link_sb.b[0]], w=[RA.b[0]])
                            kb.emit("dve", lambda: V.tensor_tensor_scan(out=IU.t[:, ::-1], data0=RA.t[:, ::-1],
                                                                        data1=IU.t[:, ::-1],
                                                                        initial=h0_sb.t[:, 1, dc:dc + 1],
                                                                        op0=ALU.mult, op1=ALU.add),
                                    r=[RA.b[0], h0_sb.b[0]], w=[IU.b[0]])
                            kb.emit("dve", lambda: V.tensor_copy(
                                out=st_sb.t[:, dc, :].rearrange("p (s r) -> p s r", r=2)[:, :, 1],
                                in_=iu3[:, :, 0]), r=[IU.b[0]], w=[st_sb.b[0]])
                    kb.emit("dve", lambda: V.tensor_tensor(out=iu[k][0].t[:], in0=iu[k][0].t[:], in1=iu[k][1].t[:], op=ALU.add),
                            r=[iu[k][1].b[0]], w=[iu[k][0].b[0]])
                    for tt in range(TT):
                        tsl = slice(tt * 512, (tt + 1) * 512)
                        kb.emit("dve", lambda: V.tensor_tensor(out=yT.t[:, dc, tsl], in0=iu[k][0].t[:, tsl], in1=GY.t[:, tsl],
                                                               op=ALU.mult), r=[iu[k][0].b[0], GY.b[0]], w=[yT.b[tt]])

                stage_a(0)
                for dc in range(DC):
                    if dc + 1 < DC:
                        stage_a(dc + 1)
                    stage_a2(dc)
                    stage_b(dc)
                ssem = kb.newsem("stsem", dma=True)
                st_tok = xb[2]
                p = psn()
                kb.emit("pe", [(lambda dc=dc: PE.transpose(p.t[0:8, dc * 128:(dc + 1) * 128], st_sb.t[:, dc, :], ident.t[:]))
                               for dc in range(4)], r=[st_sb.b[0], ident.b[0]], w=[p.b[0]])
                kb.emit("dve", lambda: V.tensor_copy(out=st_tok.t[0:8, 0:512], in_=p.t[0:8, :]), r=[p.b[0]], w=[st_tok.b[0]])
                p = psn()
                kb.emit("pe", [(lambda dc=dc: PE.transpose(p.t[0:8, (dc - 4) * 128:(dc - 3) * 128], st_sb.t[:, dc, :], ident.t[:]))
                               for dc in range(4, 8)], r=[st_sb.b[0], ident.b[0]], w=[p.b[0]])
                kb.emit("dve", lambda: V.tensor_copy(out=st_tok.t[0:8, 512:1024], in_=p.t[0:8, :]), r=[p.b[0]], w=[st_tok.b[0]])
                kb.dma("sp", st_out, st_tok.t[0:8, :], ssem, r=[st_tok.b[0]])
                out_sems.append(ssem)
                make_coef_g(l, 1, 1.0)
                wo = lru_w_out[0].rearrange("(dc p) f -> p dc f", p=128)
                wo_sl = [fetch([(lambda s_: s_.rearrange("p (dc f) -> p dc f", dc=DC), wo[:, :, ch * 512:(ch + 1) * 512])])
                         for ch in range(2)]
                for w_ in wo_sl:
                    pinned.add(w_.idx)

                def wout_group(tt, do):
                    sl = wo_sl[do // 4]
                    d4 = do % 4
                    sv = sl.t[:].rearrange("p (dc f) -> p dc f", dc=DC)
                    pp = psn()
                    kb.emit("pe", [(lambda d2=d2: PE.matmul(
                        pp.t[:], sv[:, d2, d4 * 128:(d4 + 1) * 128], yT.t[:, d2, tt * 512:(tt + 1) * 512],
                        start=(d2 == 0), stop=(d2 == DC - 1))) for d2 in range(DC)],
                        r=[sl.b[0], yT.b[tt]], w=[pp.b[0]])
                    return pp
                closing(wout_group, lru_nxt)
                pinned.clear()

        out_sems = []

        def _dbg(tag):
            import os
            if os.environ.get("KDEBUG"):
                print("SBUF remaining", tag, nc.sbuf_bytes_remaining)
        kb_tmp = None

        def attention(l, att_nxt=None):
            wq = att_w_qkv[0].rearrange("(dc p) f -> p dc f", p=128)
            wsl = []
            for n in range(3):
                wsl.append(fetch([(lambda s_: s_.rearrange("p (dc f) -> p dc f", dc=DC), wq[:, :, n * 512:(n + 1) * 512])]))
            for w_ in wsl:
                pinned.add(w_.idx)
            sub_in_std(pre_apply=lambda: make_coef(l, 1, 1.0))
            with kb.scope():
                asem = kb.newsem("attnconst", dma=True)
                asem_p = kb.newsem("attnconst_p", dma=True)

                def aload(name, shape, src, dtype=F32, q="sp"):
                    t = kb.alloc(name, shape, dtype)
                    kb.dma(q, t.t[:], src, asem if q == "sp" else asem_p, w=[t.b[0]])
                    return t
                qg_sb = aload("qg", [128, HD], att_q_g[0].partition_broadcast(128))
                kg_sb = aload("kg", [128, HD], att_k_g[0].partition_broadcast(128))
                cos_sb = aload("cos", [128, 8, 64], cosin.rearrange("(c p) f -> p c f", p=128))
                sin_sb = aload("sin", [128, 8, 64], sinin.rearrange("(c p) f -> p c f", p=128))
                mask_sb = aload("maskadd", [128, KC * 4], maskin)
                for t_ in (qg_sb, kg_sb, cos_sb, sin_sb, mask_sb):
                    t_.b[0].w = (asem, asem.count)
                qT = kb.alloc("qT", [128, NH, T], BF16)
                kT = kb.alloc("kT", [128, NKV, KC * 128], BF16)
                Vt = kb.alloc("Vt", [128, KC, NKV * HD], BF16)
                oT = hT
                nbias = kb.alloc("nbias", [128, 1], F32)
                vsem = kb.newsem("cvsem", dma=True)
                kb.dma("pool", Vt.t[:, 0:4, :], cvin.rearrange("(c p) f -> p c f", p=128), vsem, w=[Vt.b[0]])
                ck_sb = kb.alloc("ck_sb", [128, 4, NKV * HD], F32)
                cksem = kb.newsem("cksem", dma=True)
                kb.dma("sp", ck_sb.t[:], ckin.rearrange("(c p) f -> p c f", p=128), cksem, w=[ck_sb.b[0]])
                NQ = 3
                qkv = [kb.alloc(f"qkv{k}", [128, 1536], F32) for k in range(NQ)]
                ssq = [kb.alloc(f"ssq{k}", [128, 10], F32) for k in range(NQ)]
                qr = [kb.alloc(f"qr{k}", [128, 1280], F32) for k in range(NQ)]
                rt1 = kb.alloc("rt", [128, 10, 2, 32], F32)
                rt = [rt1] * NQ
                kvst = [kb.alloc(f"kvst{k}", [128, 512], F32) for k in range(2)]
                kvsem = [kb.newsem(f"kvsem{k}", dma=True) for k in range(2)]
                out_sems.extend(kvsem)

                def front(tc):
                    k = tc % NQ
                    Q, SS, QR = qkv[k], ssq[k], qr[k]
                    for n in range(3):
                        sv = wsl[n].t[:].rearrange("p (dc f) -> p dc f", dc=DC)
                        p = psn()
                        kb.emit("pe", [(lambda dc=dc: PE.matmul(p.t[:], hT.t[:, dc, tc * 128:(tc + 1) * 128], sv[:, dc, :],
                                                               start=(dc == 0), stop=(dc == DC - 1))) for dc in range(DC)],
                                r=[wsl[n].b[0], hT.b[tc // 4]], w=[p.b[0]])
                        kb.emit("act", lambda: A.copy(out=Q.t[:, n * 512:(n + 1) * 512], in_=p.t[:]), r=[p.b[0]], w=[Q.b[0]])
                    for h_ in range(10):
                        kb.emit("act", lambda: A.activation(out=QR.t[:, h_ * HD:(h_ + 1) * HD], in_=Q.t[:, h_ * HD:(h_ + 1) * HD],
                                                            func=AF.Square, accum_out=SS.t[:, h_:h_ + 1]),
                                r=[Q.b[0]], w=[QR.b[0], SS.b[0]])
                    kb.emit("act", lambda: A.activation(out=SS.t[:], in_=SS.t[:], func=AF.Sqrt, bias=eps_sb.t[:, 0:1], scale=1.0 / HD),
                            r=[eps_sb.b[0]], w=[SS.b[0]])

                def mid(tc):
                    k = tc % NQ
                    Q, SS, QR, RT = qkv[k], ssq[k], qr[k], rt[k]
                    kb.emit("dve", lambda: V.reciprocal(out=SS.t[:], in_=SS.t[:]), r=[], w=[SS.b[0]])
                    q3 = Q.t[:, 0:1280].rearrange("p (a h) -> p a h", h=HD)
                    kb.emit("dve", lambda: V.tensor_tensor(out=q3, in0=q3, in1=SS.t[:].unsqueeze(2).broadcast_to([128, 10, HD]),
                                                           op=ALU.mult), r=[SS.b[0]], w=[Q.b[0]])
                    kb.emit("dve", lambda: V.tensor_tensor(out=q3[:, 0:8, :], in0=q3[:, 0:8, :],
                                                           in1=qg_sb.t[:].unsqueeze(1).broadcast_to([128, 8, HD]), op=ALU.mult),
                            r=[qg_sb.b[0]], w=[Q.b[0]])
                    kb.emit("dve", lambda: V.tensor_tensor(out=q3[:, 8:10, :], in0=q3[:, 8:10, :],
                                                           in1=kg_sb.t[:].unsqueeze(1).broadcast_to([128, 2, HD]), op=ALU.mult),
                            r=[kg_sb.b[0]], w=[Q.b[0]])
                    q5 = Q.t[:, 0:1280].rearrange("p (a x h f) -> p a x h f", x=2, h=2, f=32)
                    o5 = QR.t[:].rearrange("p (a x h f) -> p a x h f", x=2, h=2, f=32)
                    x0, x1 = q5[:, :, :, 0, :], q5[:, :, :, 1, :]
                    o0, o1 = o5[:, :, :, 0, :], o5[:, :, :, 1, :]
                    cs = cos_sb.t[:, tc, :].rearrange("p (x f) -> p x f", x=2).unsqueeze(1).broadcast_to([128, 10, 2, 32])
                    sn = sin_sb.t[:, tc, :].rearrange("p (x f) -> p x f", x=2).unsqueeze(1).broadcast_to([128, 10, 2, 32])
                    cdeps = [Q.b[0], cos_sb.b[0], sin_sb.b[0]]
                    kb.emit("dve", lambda: V.tensor_tensor(out=RT.t[:], in0=x1, in1=sn, op=ALU.mult), r=cdeps, w=[RT.b[0]])
                    kb.emit("dve", lambda: V.tensor_tensor(out=o0, in0=x0, in1=cs, op=ALU.mult), r=cdeps, w=[QR.b[0]])
                    kb.emit("dve", lambda: V.tensor_tensor(out=o0, in0=o0, in1=RT.t[:], op=ALU.subtract), r=[RT.b[0]], w=[QR.b[0]])
                    kb.emit("dve", lambda: V.tensor_tensor(out=RT.t[:], in0=x0, in1=sn, op=ALU.mult), r=cdeps, w=[RT.b[0]])
                    kb.emit("dve", lambda: V.tensor_tensor(out=o1, in0=x1, in1=cs, op=ALU.mult), r=cdeps, w=[QR.b[0]])
                    kb.emit("dve", lambda: V.tensor_tensor(out=o1, in0=o1, in1=RT.t[:], op=ALU.add), r=[RT.b[0]], w=[QR.b[0]])

                def back(tc):
                    k = tc % NQ
                    Q, QR, KV = qkv[k], qr[k], kvst[tc % 2]
                    kb.emit("act", lambda: A.copy(out=KV.t[:], in_=Q.t[:, 1024:1536]), r=[Q.b[0]], w=[KV.b[0]])
                    kb.dma("sp", nk_out[tc * 128:(tc + 1) * 128, :], KV.t[:, 0:256], kvsem[tc % 2], r=[KV.b[0]])
                    kb.dma("sp", nv_out[tc * 128:(tc + 1) * 128, :], KV.t[:, 256:512], kvsem[tc % 2], r=[KV.b[0]])
                    kb.emit("act", lambda: A.copy(out=Vt.t[:, 4 + tc, :], in_=Q.t[:, 1280:1536]), r=[Q.b[0]], w=[Vt.b[0]])
                    for grp in range(3):
                        nblk = 4 if grp < 2 else 2
                        p = psn()
                        kb.emit("pe", [(lambda b_=b_: PE.transpose(p.t[:, b_ * 128:(b_ + 1) * 128],
                                                                   QR.t[:, (grp * 4 + b_) * 128:(grp * 4 + b_ + 1) * 128], ident.t[:]))
                                       for b_ in range(nblk)], r=[QR.b[0], ident.b[0]], w=[p.b[0]])
                        if grp < 2:
                            kb.emit("act", lambda: A.copy(out=qT.t[:, grp * 4:(grp + 1) * 4, tc * 128:(tc + 1) * 128],
                                                          in_=p.t[:].rearrange("p (a c) -> p a c", a=4)),
                                    r=[p.b[0]], w=[qT.b[0]])
                        else:
                            kb.emit("act", lambda: A.copy(out=kT.t[:, :, 512 + tc * 128:512 + (tc + 1) * 128],
                                                          in_=p.t[:, 0:256].rearrange("p (a c) -> p a c", a=2)),
                                    r=[p.b[0]], w=[kT.b[0]])

                ps_excl.add(7)
                front(0)
                front(1)
                nmod = 0
                for tc in range(8):
                    if tc + 2 < 8:
                        front(tc + 2)
                    if tc < 6 and (1, 12 + tc) in mod_pending:
                        mod_chunk(1, 12 + tc, bank=ps[7], col0=tc * 4, do_add=False)
                        nmod += 1
                    mid(tc)
                    back(tc)
                if nmod:
                    assert nmod == 6
                    kb.emit("dve", lambda: V.tensor_tensor(out=modv.t[:, 1, 48:72], in0=ps[7].t[:, 0:24],
                                                           in1=modb_sb[1].t[:, 48:72], op=ALU.add),
                            r=[ps[7].b[0], cpack.b[0]], w=[modv.b[5]])
                ps_excl.discard(7)
                pinned.clear()
                for g_ in range(NKV):
                    p = psn()
                    kb.emit("pe", [(lambda c=c: PE.transpose(p.t[:, c * 128:(c + 1) * 128],
                                                             ck_sb.t[:, c, g_ * HD:(g_ + 1) * HD], ident.t[:]))
                                   for c in range(4)], r=[ck_sb.b[0], ident.b[0]], w=[p.b[0]])
                    kb.emit("act", lambda: A.copy(out=kT.t[:, g_, 0:512], in_=p.t[:]), r=[p.b[0]], w=[kT.b[0]])
                s1 = kb.alloc("sb1", [128, 4 * NKV * HD], F32)
                s2 = kb.alloc("sb2", [128, 8], F32)
                m1 = kb.alloc("sbm1", [128, 4], F32)
                r1 = kb.alloc("sbr1", [1, 4], F32)
                ckf = ck_sb.t[:].rearrange("p c f -> p (c f)")
                kb.emit("dve", lambda: V.tensor_tensor(out=s1.t[:], in0=ckf, in1=ckf, op=ALU.mult), r=[ck_sb.b[0]], w=[s1.b[0]])
                kb.emit("dve", lambda: V.tensor_reduce(out=s2.t[:], in_=s1.t[:].rearrange("p (a h) -> p a h", h=HD),
                                                       axis=AX.X, op=ALU.add), r=[s1.b[0]], w=[s2.b[0]])
                kb.emit("dve", lambda: V.tensor_reduce(out=m1.t[:, 0:1], in_=s2.t[:], axis=AX.X, op=ALU.max),
                        r=[s2.b[0]], w=[m1.b[0]])
                p = psn()
                kb.emit("pe", lambda: PE.transpose(p.t[0:1, 0:128], m1.t[:, 0:1], ident.t[:]), r=[m1.b[0], ident.b[0]], w=[p.b[0]])
                kb.emit("dve", lambda: V.tensor_reduce(out=r1.t[:, 0:1], in_=p.t[0:1, 0:128], axis=AX.X, op=ALU.max),
                        r=[p.b[0]], w=[r1.b[0]])
                p2 = psn()
                kb.emit("pe", lambda: PE.matmul(p2.t[:, 0:1], ones_f.t[0:1, :], r1.t[0:1, 0:1], start=True, stop=True),
                        r=[r1.b[0], ones_f.b[0]], w=[p2.b[0]])
                kb.emit("dve", lambda: V.tensor_tensor(out=s1.t[:, 0:HD], in0=kg_sb.t[:], in1=kg_sb.t[:], op=ALU.mult),
                        r=[kg_sb.b[0]], w=[s1.b[0]])
                kb.emit("dve", lambda: V.tensor_reduce(out=m1.t[:, 1:2], in_=s1.t[:, 0:HD], axis=AX.X, op=ALU.max),
                        r=[s1.b[0]], w=[m1.b[0]])
                kb.emit("dve", lambda: V.tensor_tensor(out=s1.t[:, 0:HD], in0=qg_sb.t[:], in1=qg_sb.t[:], op=ALU.mult),
                        r=[qg_sb.b[0]], w=[s1.b[0]])
                kb.emit("dve", lambda: V.tensor_reduce(out=m1.t[:, 2:3], in_=s1.t[:, 0:HD], axis=AX.X, op=ALU.max),
                        r=[s1.b[0]], w=[m1.b[0]])
                kb.emit("dve", lambda: V.scalar_tensor_tensor(out=m1.t[:, 3:4], in0=m1.t[:, 1:2], scalar=float(HD), in1=p2.t[:, 0:1],
                                                              op0=ALU.mult, op1=ALU.max), r=[p2.b[0]], w=[m1.b[0]])
                kb.emit("dve", lambda: V.scalar_tensor_tensor(out=m1.t[:, 3:4], in0=m1.t[:, 2:3], scalar=float(HD), in1=m1.t[:, 3:4],
                                                              op0=ALU.mult, op1=ALU.mult), r=[], w=[m1.b[0]])
                kb.emit("act", lambda: A.activation(out=nbias.t[:], in_=m1.t[:, 3:4], func=AF.Sqrt), r=[m1.b[0]], w=[nbias.b[0]])
                kb.emit("dve", lambda: V.tensor_scalar_mul(out=nbias.t[:], in0=nbias.t[:], scalar1=-SCALE), r=[], w=[nbias.b[0]])

                mb = kb.alloc("maskb", [128, KC * 4], F32)
                kb.emit("dve", lambda: V.tensor_scalar_add(out=mb.t[:], in0=mask_sb.t[:], scalar1=nbias.t[:, 0:1]),
                        r=[mask_sb.b[0], nbias.b[0]], w=[mb.b[0]])
                PT = [kb.alloc(f"PT{k}", [128, 512], BF16) for k in range(4)]
                scb = [ps[0], ps[1], ps[2], ps[7]]
                rec = [kb.alloc(f"rec{k}", [128, 512], F32) for k in range(2)]
                _dbg("attn peak")
                items = [(g_, hp, sg_, kc) for g_ in range(NKV) for hp in range(2) for sg_ in range(NSEG) for kc in range(KC)]
                sc_ps = {}
                acc = {}

                def emit_qk(n):
                    g_, hp, sg_, kc = items[n]
                    h0_ = g_ * 4 + hp * 2
                    p = scb[n % 4]
                    sc_ps[n] = p
                    kb.emit("pe", lambda: PE.matmul(p.t[:].rearrange("p (j c) -> p j c", j=2), kT.t[:, g_, kc * 128:(kc + 1) * 128],
                                                     qT.t[:, h0_:h0_ + 2, sg_ * SEG:(sg_ + 1) * SEG], start=True, stop=True),
                            r=[kT.b[0], qT.b[0]], w=[p.b[0]])

                def emit_pv(n):
                    g_, hp, sg_, kc = items[n]
                    h0_ = g_ * 4 + hp * 2
                    p = sc_ps.pop(n)
                    pt = PT[n % 4]
                    col = kc * 4 + sg_
                    kb.emit("act", lambda: A.activation(out=pt.t[:], in_=p.t[:], func=AF.Exp,
                                                        bias=mb.t[:, col:col + 1], scale=SCALE),
                            r=[p.b[0], mb.b[0]], w=[pt.b[0]])
                    if kc == 0:
                        acc[(g_, hp, sg_)] = ((ps[3], ps[4]), (ps[5], ps[6]))[(n // KC) % 2]
                    po, psum_ = acc[(g_, hp, sg_)]
                    kb.emit("pe", [
                        lambda: PE.matmul(po.t[:], Vt.t[:, kc, g_ * HD:(g_ + 1) * HD], pt.t[:], start=(kc == 0), stop=(kc == KC - 1)),
                        lambda: PE.matmul(psum_.t[:], ones_b.t[:], pt.t[:], start=(kc == 0), stop=(kc == KC - 1))],
                        r=[Vt.b[0], pt.b[0], ones_b.b[0]], w=[po.b[0], psum_.b[0]])
                    if kc == KC - 1:
                        rc = rec[(n // KC) % 2]
                        kb.emit("dve", lambda: V.reciprocal(out=rc.t[:], in_=psum_.t[:]), r=[psum_.b[0]], w=[rc.b[0]])
                        kb.emit("dve", lambda: V.tensor_tensor(
                            out=oT.t[:, h0_:h0_ + 2, sg_ * SEG:(sg_ + 1) * SEG],
                            in0=po.t[:].rearrange("p (j c) -> p j c", j=2), in1=rc.t[:].rearrange("p (j c) -> p j c", j=2),
                            op=ALU.mult), r=[po.b[0], rc.b[0]], w=[oT.b[sg_ // 2]])
                        del acc[(g_, hp, sg_)]

                emit_qk(0)
                emit_qk(1)
                for n in range(len(items)):
                    if n + 2 < len(items):
                        emit_qk(n + 2)
                    emit_pv(n)
                make_coef_g(l, 1, 1.0)
                wo = att_w_o[0].rearrange("(h p) f -> p h f", p=128)
                wo_sl = [fetch([(lambda s_: s_.rearrange("p (h f) -> p h f", h=NH), wo[:, :, ch * 512:(ch + 1) * 512])])
                         for ch in range(2)]
                for w_ in wo_sl:
                    pinned.add(w_.idx)

                def wo_group(tt, do):
                    sl = wo_sl[do // 4]
                    d4 = do % 4
                    sv = sl.t[:].rearrange("p (h f) -> p h f", h=NH)
                    pp = psn()
                    kb.emit("pe", [(lambda h_=h_: PE.matmul(
                        pp.t[:], sv[:, h_, d4 * 128:(d4 + 1) * 128], oT.t[:, h_, tt * 512:(tt + 1) * 512],
                        start=(h_ == 0), stop=(h_ == NH - 1))) for h_ in range(NH)],
                        r=[sl.b[0], oT.b[tt]], w=[pp.b[0]])
                    return pp
                ps_i[0] = 7
                closing(wo_group, att_nxt)
                pinned.clear()

        stage = [0]

        def stop_here():
            stage[0] += 1
            return DEBUG_STOP is not None and stage[0] > DEBUG_STOP

        def body():
            nonlocal kb_tmp
            for l in range(2):
                if stop_here():
                    return
                ffn(l, 0, 0, nbg=(6 if l == 0 else 0), nxt=(l, 1))
                if stop_here():
                    return
                if l == 0:
                    lru(l, lru_nxt=(l, 2))
                else:
                    attention(l, att_nxt=(l, 2))
                if stop_here():
                    return
                ffn(l, 1, 2, nbg=(6 if l == 0 else 0), nxt=((l + 1, 0) if l == 0 else ("final" if DEBUG_STOP is None else None)))
                if stop_here():
                    return

        body()

        with kb.scope():
            yT_ = kb.alloc("yfin", [128, DC, T], F32, nb=TT)
            if DEBUG_STOP is None:
                sub_in(lambda dc: fg_sb.t[:, dc:dc + 1], None, yT_.t, yT_.b, fg_sb.b[0])
            else:
                for tt in range(TT):
                    for dc in range(DC):
                        kb.emit("dve", lambda: V.tensor_copy(out=yT_.t[:, dc, tt * 512:(tt + 1) * 512],
                                                             in_=xT.t[:, dc, tt * 512:(tt + 1) * 512]),
                                r=[xT.b[dc][tt]], w=[yT_.b[tt]])
            ytok = [kb.alloc(f"ytok{i}", [128, D], F32) for i in range(2)]
            ysem = [kb.newsem(f"ysem{i}", dma=True) for i in range(2)]
            out_sems.extend(ysem)
            for tc in range(4 if "yv" in early_out else 0, 8):
                emit_out_tc(tc, yT_.t, [yT_.b[tc // 4]], ytok[tc % 2], ysem[tc % 2])
            for s in out_sems:
                if s.count > 0:
                    nc.sync.wait_ge(s.h, s.count)
    return nc


_NC_CACHE = {}

PROMPT_ASSIGN = [[0, 1, 2], [3, 4, 5], [6, 7, 8], [9, 10, 11], [12, 13], [14, 15]]


def _rope_tables():
    t = np.arange(1024)
    r_idx = (t // 64).astype(np.float32)
    c_idx = (t % 64).astype(np.float32)
    n_freq = 32
    inv = (np.float32(10000.0) ** (-np.arange(n_freq, dtype=np.float32) / np.float32(n_freq))).astype(np.float32)
    ang = np.stack([r_idx[:, None] * inv, c_idx[:, None] * inv], axis=1).astype(np.float32)
    return np.cos(ang).reshape(1024, 64).astype(np.float32), np.sin(ang).reshape(1024, 64).astype(np.float32)


def kernel(x_prompt, x_sample, c, state_lru, cache_k, cache_v, c_ctx, mod_w, mod_b, norm_g,
           ffn_w_gu, ffn_w_down, lru_w_in, lru_conv_w, lru_conv_b, lru_gate_w, lru_gate_b,
           lru_lambda, lru_w_out, att_w_qkv, att_q_g, att_k_g, att_w_o, final_g):
    f = lambda a: np.ascontiguousarray(np.asarray(a, dtype=np.float32))
    x_prompt, x_sample = f(x_prompt), f(x_sample)
    if "nc" not in _NC_CACHE:
        _NC_CACHE["nc"] = build_program()
    nc = _NC_CACHE["nc"]
    shared = dict(mod_w=f(mod_w), mod_b=f(mod_b), norm_g=f(norm_g), ffn_w_gu=f(ffn_w_gu), ffn_w_down=f(ffn_w_down),
                  lru_w_in=f(lru_w_in), lru_conv_w=f(lru_conv_w), lru_conv_b=f(lru_conv_b), lru_gate_w=f(lru_gate_w),
                  lru_gate_b=f(lru_gate_b), lru_lambda=f(lru_lambda), lru_w_out=f(lru_w_out), att_w_qkv=f(att_w_qkv),
                  att_q_g=f(att_q_g), att_k_g=f(att_k_g), att_w_o=f(att_w_o), final_g=f(final_g),
                  ident=np.eye(128, dtype=np.float32))
    rcos, rsin = _rope_tables()
    in_maps = []
    for core in range(8):
        m = dict(shared)
        if core < 2:
            b = core
            m["xin"] = f(x_sample[b])
            m["cvec"] = f(c[b])
            m["h0"] = f(state_lru[b, 0])
            m["link"] = np.ones((128, 1), np.float32)
            m["ck"] = f(np.asarray(cache_k)[b, 0].reshape(PAST, NKV * HD))
            m["cv"] = f(np.asarray(cache_v)[b, 0].reshape(PAST, NKV * HD))
            ma = np.zeros((128, KC * 4), np.float32)
            m["rcos"], m["rsin"] = rcos, rsin
        else:
            seqs = PROMPT_ASSIGN[core - 2]
            xin = np.zeros((T, D), np.float32)
            for s_, sq_ in enumerate(seqs):
                xin[s_ * SEG:(s_ + 1) * SEG] = x_prompt[sq_]
            m["xin"] = xin
            m["cvec"] = f(c_ctx)
            m["h0"] = np.zeros((2, D), np.float32)
            m["link"] = np.zeros((128, 1), np.float32)
            m["ck"] = np.zeros((PAST, NKV * HD), np.float32)
            m["cv"] = np.zeros((PAST, NKV * HD), np.float32)
            ma = np.full((128, KC * 4), -30000.0, np.float32)
            for kc_ in range(4, KC):
                ma[:, kc_ * 4 + (kc_ - 4) // 2] = 0.0
            m["rcos"] = np.ones((T, 64), np.float32)
            m["rsin"] = np.zeros((T, 64), np.float32)
        m["maskadd"] = ma
        in_maps.append(m)
    res = run_bass_kernel_spmd(nc, in_maps, core_ids=list(range(8)))
    R = res.results
    y_prompt = np.zeros((16, SEG, D), np.float32)
    y_sample = np.zeros((2, T, D), np.float32)
    new_state = np.zeros((16, 1, 2, D), np.float32)
    new_k = np.zeros((16, 1, SEG, NKV, HD), np.float32)
    new_v = np.zeros((16, 1, SEG, NKV, HD), np.float32)
    for core in range(8):
        r = R[core]
        y = np.asarray(r["y"])
        if core < 2:
            y_sample[core] = y
        else:
            st = np.asarray(r["st"]).reshape(NSEG, 2, D)
            nk = np.asarray(r["nk"]).reshape(T, NKV, HD)
            nv = np.asarray(r["nv"]).reshape(T, NKV, HD)
            for s_, sq_ in enumerate(PROMPT_ASSIGN[core - 2]):
                y_prompt[sq_] = y[s_ * SEG:(s_ + 1) * SEG]
                new_state[sq_, 0] = st[s_]
                new_k[sq_, 0] = nk[s_ * SEG:(s_ + 1) * SEG]
                new_v[sq_, 0] = nv[s_ * SEG:(s_ + 1) * SEG]
    return (y_prompt, y_sample, new_state, new_k, new_v)
```
